# Optimizing a Trainium2 kernel written in Bass

```python
import math
import jax
import jax.numpy as jnp
from jax import lax
import numpy as np

D_MODEL = 4096
BATCH = 2
SEQ = 8192
DEPTH = 1

CTX_LEN = 256
GRID_W = 64
N_MOD = 9
RET_HEADS = 8
RET_HEAD_DIM = 256
RET_WIDTH = RET_HEADS * RET_HEAD_DIM
DIFF_HEADS = 8
DIFF_HEAD_DIM = 128
DIFF_V_DIM = 2 * DIFF_HEAD_DIM
DIFF_WIDTH = DIFF_HEADS * DIFF_V_DIM
MIX_WIDTH = RET_WIDTH + DIFF_WIDTH
D_FF = 256 * ((8 * D_MODEL // 3 + 255) // 256)
CHUNK = 128
Q_BLOCK = 128
ROPE_BASE = 10000.0
EPS = 1e-6
FFN_HALF = 0.5
PROJ_SPLITS = (
    ("ret_q", RET_WIDTH),
    ("ret_k", RET_WIDTH),
    ("ret_v", RET_WIDTH),
    ("ret_g", RET_WIDTH),
    ("diff_q", 2 * DIFF_HEADS * DIFF_HEAD_DIM),
    ("diff_k", 2 * DIFF_HEADS * DIFF_HEAD_DIM),
    ("diff_v", DIFF_WIDTH),
)
IN_COLS = sum(w for _, w in PROJ_SPLITS)

kernel_name = "hybrid_retention_diffattn_macaron_dit"


def rms_norm(x, w):
    xf = x.astype(jnp.float32)
    y = xf * lax.rsqrt(jnp.mean(xf * xf, axis=-1, keepdims=True) + EPS)
    return (y * w.astype(jnp.float32)).astype(x.dtype)


def modulate(h, shift, scale):
    return h * (1 + scale) + shift


def swiglu(h, wg, wu, wd):
    return (jax.nn.silu(h @ wg) * (h @ wu)) @ wd


def ffn_sublayer(x, mod, base, norm_g, wg, wu, wd):
    h = modulate(rms_norm(x, norm_g), mod[base], mod[base + 1])
    return x + mod[base + 2] * (FFN_HALF * swiglu(h, wg, wu, wd))


def axial_rope_tables(rows, cols, head_dim):
    n_freq = head_dim // 4
    inv_freq = ROPE_BASE ** (-jnp.arange(n_freq, dtype=jnp.float32) / n_freq)
    ang_r = rows[:, None].astype(jnp.float32) * inv_freq
    ang_c = cols[:, None].astype(jnp.float32) * inv_freq
    ang = jnp.concatenate([ang_r, ang_r, ang_c, ang_c], axis=-1)
    return jnp.cos(ang), jnp.sin(ang)


def apply_axial_rope(x, cos, sin):
    shape = (1, x.shape[1]) + (1,) * (x.ndim - 3) + (x.shape[-1],)
    cos = cos.reshape(shape).astype(x.dtype)
    sin = sin.reshape(shape).astype(x.dtype)
    x1, x2, x3, x4 = jnp.split(x, 4, axis=-1)
    rot = jnp.concatenate([-x2, x1, -x4, x3], axis=-1)
    return x * cos + rot * sin


def split_in_proj(w):
    parts, off = {}, 0
    for name, width in PROJ_SPLITS:
        parts[name] = w[:, off:off + width]
        off += width
    return parts


def retention_chunkwise(q, k, v, log_gamma, s0, strict):
    B, L, H, _ = q.shape
    dv = v.shape[-1]
    n_chunks = L // CHUNK

    def chunks(a):
        return a.astype(jnp.float32).reshape(B, n_chunks, CHUNK, H, a.shape[-1]).transpose(1, 0, 2, 3, 4)

    pos = jnp.arange(CHUNK, dtype=jnp.float32)
    rel = pos[:, None] - pos[None, :]
    mask = (rel > 0) if strict else (rel >= 0)
    decay_intra = jnp.where(mask[None], jnp.exp(jnp.where(mask, rel, 0.0)[None] * log_gamma[:, None, None]), 0.0)
    decay_q = jnp.exp((pos[:, None] + 1.0) * log_gamma[None, :])[None, :, :, None]
    decay_k = jnp.exp((CHUNK - 1.0 - pos[:, None]) * log_gamma[None, :])[None, :, :, None]
    decay_chunk = jnp.exp(CHUNK * log_gamma)[None, :, None, None]

    def step(state, blk):
        qb, kb, vb = blk
        scores = jnp.einsum("bihd,bjhd->bhij", qb, kb) * decay_intra
        inner = jnp.einsum("bhij,bjhe->bihe", scores, vb)
        cross = jnp.einsum("bihd,bhde->bihe", qb, state) * decay_q
        new_state = decay_chunk * state + jnp.einsum("bjhd,bjhe->bhde", kb * decay_k, vb)
        return new_state, inner + cross

    final_state, out = lax.scan(step, s0, (chunks(q), chunks(k), chunks(v)))
    return out.transpose(1, 0, 2, 3, 4).reshape(B, L, H, dv), final_state


def retention_final_state(k, v, log_gamma):
    L = k.shape[1]
    w = jnp.exp((L - 1.0 - jnp.arange(L, dtype=jnp.float32))[:, None] * log_gamma[None, :])
    return jnp.einsum("blhd,blhe->bhde", k.astype(jnp.float32) * w[None, :, :, None], v.astype(jnp.float32))


def retention_output(o, g, gn_w):
    B, S, H, dv = o.shape
    mu = jnp.mean(o, axis=-1, keepdims=True)
    var = jnp.mean(jnp.square(o - mu), axis=-1, keepdims=True)
    y = ((o - mu) * lax.rsqrt(var + EPS)).reshape(B, S, H * dv) * gn_w.astype(jnp.float32)
    return y.astype(g.dtype) * jax.nn.silu(g)


def diff_attention(q, k, v, lam):
    B, S, H, _, dh = q.shape
    n_blocks = S // Q_BLOCK
    qb = q.reshape(B, n_blocks, Q_BLOCK, H, 2, dh).transpose(1, 0, 2, 3, 4, 5)
    scale = dh ** -0.5

    def attend(q_blk):
        s = jnp.einsum("bqhmd,bkhmd->bhmqk", q_blk, k, preferred_element_type=jnp.float32) * scale
        p = jax.nn.softmax(s, axis=-1)
        a = p[:, :, 0] - lam * p[:, :, 1]
        return jnp.einsum("bhqk,bkhe->bqhe", a.astype(v.dtype), v)

    out = lax.map(attend, qb)
    return out.transpose(1, 0, 2, 3, 4).reshape(B, S, H, v.shape[-1])


def setup_inputs(seed: int = 0) -> dict:
    key = jax.random.key(seed)
    ks = jax.random.split(key, 18)
    f32 = jnp.float32

    def nrm(k, shape, scale):
        return jax.random.normal(k, shape, f32) * scale

    base_decay = np.log(-np.log(1.0 - 2.0 ** (-5.0 - np.arange(RET_HEADS, dtype=np.float32)))).astype(np.float32)
    return {
        "x": nrm(ks[0], (BATCH, SEQ, D_MODEL), 1.0),
        "c": nrm(ks[1], (BATCH, D_MODEL), 1.0),
        "ctx": nrm(ks[2], (BATCH, CTX_LEN, D_MODEL), 1.0),
        "c_ctx": nrm(ks[3], (D_MODEL,), 1.0),
        "ada_w": nrm(ks[4], (DEPTH, D_MODEL, N_MOD * D_MODEL), 0.5 * D_MODEL ** -0.5),
        "ada_b": nrm(ks[5], (DEPTH, N_MOD * D_MODEL), 0.02),
        "norm_w": 1.0 + nrm(ks[6], (DEPTH, 3, D_MODEL), 0.02),
        "ffn_gate": nrm(ks[7], (DEPTH, 2, D_MODEL, D_FF), D_MODEL ** -0.5),
        "ffn_up": nrm(ks[8], (DEPTH, 2, D_MODEL, D_FF), D_MODEL ** -0.5),
        "ffn_down": nrm(ks[9], (DEPTH, 2, D_FF, D_MODEL), D_FF ** -0.5),
        "w_in": nrm(ks[10], (DEPTH, D_MODEL, IN_COLS), D_MODEL ** -0.5),
        "w_out": nrm(ks[11], (DEPTH, MIX_WIDTH, D_MODEL), MIX_WIDTH ** -0.5),
        "ret_decay": jnp.asarray(base_decay)[None, None, :] + nrm(ks[12], (DEPTH, 2, RET_HEADS), 0.05),
        "ret_gn_w": 1.0 + nrm(ks[13], (DEPTH, RET_WIDTH), 0.02),
        "diff_lambda": nrm(ks[14], (DEPTH, 4, DIFF_HEAD_DIM), 0.1),
        "diff_subln_w": 1.0 + nrm(ks[15], (DEPTH, DIFF_V_DIM), 0.02),
        "final_norm_w": 1.0 + nrm(ks[16], (D_MODEL,), 0.02),
    }


def reference(x, c, ctx, c_ctx, ada_w, ada_b, norm_w, ffn_gate, ffn_up, ffn_down,
              w_in, w_out, ret_decay, ret_gn_w, diff_lambda, diff_subln_w, final_norm_w):
    B, S, D = x.shape
    L = ctx.shape[1]
    ROWS = S // GRID_W
    rows = jnp.repeat(jnp.arange(ROWS, dtype=jnp.int32), GRID_W)
    cols = jnp.tile(jnp.arange(GRID_W, dtype=jnp.int32), ROWS)
    cos_r, sin_r = axial_rope_tables(rows, cols, RET_HEAD_DIM)
    cos_d, sin_d = axial_rope_tables(rows, cols, DIFF_HEAD_DIM)
    ret_scale = RET_HEAD_DIM ** -0.5

    def ret_heads(a):
        return a.reshape(a.shape[0], a.shape[1], RET_HEADS, RET_HEAD_DIM)

    def diff_qk_heads(a):
        return a.reshape(a.shape[0], a.shape[1], DIFF_HEADS, 2, DIFF_HEAD_DIM)

    def diff_v_heads(a):
        return a.reshape(a.shape[0], a.shape[1], DIFF_HEADS, DIFF_V_DIM)

    xl, xc = x, ctx
    for l in range(DEPTH):
        last = l == DEPTH - 1
        lam_init = 0.8 - 0.6 * math.exp(-0.3 * l)
        mod_l = (jax.nn.silu(c) @ ada_w[l] + ada_b[l]).reshape(B, N_MOD, 1, D).transpose(1, 0, 2, 3)
        mod_c = (jax.nn.silu(c_ctx) @ ada_w[l] + ada_b[l]).reshape(N_MOD, 1, 1, D)
        ffn_pre = (ffn_gate[l, 0], ffn_up[l, 0], ffn_down[l, 0])
        ffn_post = (ffn_gate[l, 1], ffn_up[l, 1], ffn_down[l, 1])

        xl = ffn_sublayer(xl, mod_l, 0, norm_w[l, 0], *ffn_pre)
        xc = ffn_sublayer(xc, mod_c, 0, norm_w[l, 0], *ffn_pre)

        wp = split_in_proj(w_in[l])
        h_l = modulate(rms_norm(xl, norm_w[l, 1]), mod_l[3], mod_l[4])
        h_c = modulate(rms_norm(xc, norm_w[l, 1]), mod_c[3], mod_c[4])
        pl = {name: h_l @ wp[name] for name, _ in PROJ_SPLITS}
        ctx_names = ("ret_k", "ret_v", "diff_k", "diff_v") if last else tuple(n for n, _ in PROJ_SPLITS)
        pc = {name: h_c @ wp[name] for name in ctx_names}

        log_gamma = -jnp.exp(ret_decay[l].astype(jnp.float32))
        rq = apply_axial_rope(ret_heads(pl["ret_q"]), cos_r, sin_r)
        rk = apply_axial_rope(ret_heads(pl["ret_k"]), cos_r, sin_r) * ret_scale
        rv = ret_heads(pl["ret_v"])
        ck = ret_heads(pc["ret_k"]) * ret_scale
        cv = ret_heads(pc["ret_v"])
        if last:
            s_f = retention_final_state(ck, cv, log_gamma[0])
            s_b = retention_final_state(ck[:, ::-1], cv[:, ::-1], log_gamma[1])
        else:
            zero_state = jnp.zeros((B, RET_HEADS, RET_HEAD_DIM, RET_HEAD_DIM), jnp.float32)
            cq = ret_heads(pc["ret_q"])
            oc_f, s_f = retention_chunkwise(cq, ck, cv, log_gamma[0], zero_state, False)
            oc_b, s_b = retention_chunkwise(cq[:, ::-1], ck[:, ::-1], cv[:, ::-1], log_gamma[1], zero_state, True)
            ret_c = retention_output(oc_f + oc_b[:, ::-1], pc["ret_g"], ret_gn_w[l])
        o_f, _ = retention_chunkwise(rq, rk, rv, log_gamma[0], s_f, False)
        o_b, _ = retention_chunkwise(rq[:, ::-1], rk[:, ::-1], rv[:, ::-1], log_gamma[1], s_b, True)
        ret_l = retention_output(o_f + o_b[:, ::-1], pl["ret_g"], ret_gn_w[l])

        lam_p = diff_lambda[l].astype(jnp.float32)
        lam = jnp.exp(jnp.sum(lam_p[0] * lam_p[1])) - jnp.exp(jnp.sum(lam_p[2] * lam_p[3])) + lam_init
        dq = apply_axial_rope(diff_qk_heads(pl["diff_q"]), cos_d, sin_d)
        dk = apply_axial_rope(diff_qk_heads(pl["diff_k"]), cos_d, sin_d)
        dv = diff_v_heads(pl["diff_v"])
        cdk = diff_qk_heads(pc["diff_k"])
        cdv = diff_v_heads(pc["diff_v"])
        keys = jnp.concatenate([cdk, dk], axis=1)
        vals = jnp.concatenate([cdv, dv], axis=1)
        diff_l = diff_attention(dq, keys, vals, lam)
        diff_l = (rms_norm(diff_l, diff_subln_w[l]) * (1.0 - lam_init)).reshape(B, S, DIFF_WIDTH)

        mix_l = jnp.concatenate([ret_l, diff_l], axis=-1) @ w_out[l]
        xl = xl + mod_l[5] * mix_l
        if not last:
            diff_c = diff_attention(diff_qk_heads(pc["diff_q"]), cdk, cdv, lam)
            diff_c = (rms_norm(diff_c, diff_subln_w[l]) * (1.0 - lam_init)).reshape(B, L, DIFF_WIDTH)
            mix_c = jnp.concatenate([ret_c, diff_c], axis=-1) @ w_out[l]
            xc = xc + mod_c[5] * mix_c
            xc = ffn_sublayer(xc, mod_c, 6, norm_w[l, 2], *ffn_post)

        xl = ffn_sublayer(xl, mod_l, 6, norm_w[l, 2], *ffn_post)

    return rms_norm(xl, final_norm_w)
```

```python
import numpy as np
from contextlib import ExitStack
import concourse.bass as bass
import concourse.mybir as mybir
from concourse.bass_utils import run_bass_kernel_spmd

F32 = mybir.dt.float32
BF16 = mybir.dt.bfloat16
AF = mybir.ActivationFunctionType
ALU = mybir.AluOpType
AX = mybir.AxisListType
EPS = 1e-6


class Cfg:
    def __init__(s, D, S, L, DFF, RH, DHH, TT, GW=64):
        s.D, s.S, s.L, s.DFF, s.RH, s.DHH, s.TT, s.GW = D, S, L, DFF, RH, DHH, TT, GW
        s.NT = S // 4
        s.KC = D // 128
        s.FC = DFF // 128
        s.RW = RH * 256
        s.DW = DHH * 256
        s.MIX = s.RW + s.DW
        s.MC = s.MIX // 128
        s.INC = 4 * s.RW + 3 * s.DW
        s.NTA = s.NT + L
        s.NCH = s.NT // 128
        s.LC = L // 128
        s.NKC = (L + 4 * s.NT) // 128
        s.NCONST = 128 * 9 + 2 + 2 * s.LC + 2 * s.NCH
        s.WT = 256 if D >= 256 else 128


FULL = Cfg(D=4096, S=8192, L=256, DFF=11008, RH=8, DHH=8, TT=512)


class DSem:
    def __init__(s, nc, name):
        s.h = nc.alloc_semaphore(name)
        s.v = 0


class Tracker:
    def __init__(s, nc):
        s.nc = nc
        s.E = {"pe": nc.tensor, "act": nc.scalar, "dve": nc.vector, "pool": nc.gpsimd, "sp": nc.sync}
        s.esem = {k: nc.alloc_semaphore("e_" + k) for k in s.E}
        s.ecnt = {k: 0 for k in s.E}
        s.seen = {}
        s.dsems = []
        s.semname = {}

    def dsem(s, name):
        d = DSem(s.nc, name)
        s.dsems.append(d)
        return d

    def wait(s, e, *toks):
        for tok in toks:
            if tok is None:
                continue
            if isinstance(tok, list):
                s.wait(e, *tok)
                continue
            sem, val, key = tok
            k = (e, key)
            if s.seen.get(k, 0) >= val:
                continue
            s.E[e].wait_ge(sem, val)
            s.seen[k] = val

    def mark(s, e, inst):
        s.ecnt[e] += 1
        inst.then_inc(s.esem[e], 1)
        return (s.esem[e], s.ecnt[e], "e_" + e)

    def op(s, e, inst_fn, waits=(), mark=True):
        s.wait(e, *waits)
        inst = inst_fn(s.E[e])
        if mark:
            return s.mark(e, inst)
        return None

    def dma(s, e, out, in_, sem, waits=()):
        s.wait(e, *waits)
        s.E[e].dma_start(out=out, in_=in_).then_inc(sem.h, 16)
        sem.v += 16
        return (sem.h, sem.v, "d_%d" % id(sem))

    def chain(s, e, fns, waits=()):
        tok = None
        for k, fn in enumerate(fns):
            tok = s.op(e, fn, waits=(list(waits) if k == 0 else [tok]))
        return tok

    def tot(s, sem):
        return (sem.h, sem.v, "d_%d" % id(sem))

    def barrier(s, markers):
        toks = []
        toks.append(s.op("act", lambda en: en.copy(out=markers["act"][:, 0:1], in_=markers["act"][:, 1:2])))
        for e in ("dve", "pool"):
            toks.append(s.op(e, lambda en, e=e: en.memset(markers[e][:, 0:1], 0.0)))
        toks.append((s.esem["pe"], s.ecnt["pe"], "e_pe"))
        for d in s.dsems:
            if d.v > 0:
                toks.append((d.h, d.v, "d_%d" % id(d)))
        for e in s.E:
            s.wait(e, *toks)


class WStream:
    def __init__(s, tr, nslots, slot_elems):
        s.tr = tr
        s.n = nslots
        s.slots = [tr.nc.alloc_sbuf_tensor("wslot%d" % i, [128, slot_elems], BF16) for i in range(nslots)]
        s.sems = [tr.dsem("wsem%d" % i) for i in range(nslots)]
        s.tiles = []
        s.issued = 0
        s.free_tok = {}
        s.load_tok = {}
        s.cur = 0
        s.slot_elems = slot_elems

    def add(s, dram_ap, a, b, tag):
        assert a * b <= s.slot_elems, (a, b)
        s.tiles.append((dram_ap, a, b, tag))

    def view(s, i):
        _, a, b, _ = s.tiles[i]
        return s.slots[i % s.n][:, 0:a * b].rearrange("p (a b) -> p a b", a=a)

    def next(s, tag):
        i = s.cur
        s.cur += 1
        assert s.tiles[i][3] == tag, (i, s.tiles[i][3], tag)
        upto = min(len(s.tiles), i + s.n)
        while s.issued < upto and (s.issued < s.n or (s.issued - s.n) in s.free_tok):
            j = s.issued
            waits = [s.free_tok[j - s.n]] if j >= s.n else []
            s.load_tok[j] = s.tr.dma("pool", s.view(j), s.tiles[j][0], s.sems[j % s.n], waits)
            s.issued += 1
        assert i in s.load_tok, ("weight tile not issued (too many tiles held)", i)
        return i, s.view(i), s.load_tok[i]

    def release(s, i, tok):
        s.free_tok[i] = tok


def wtile_cols(w, c0, ncols):
    return w.rearrange("(kc p) n -> p kc n", p=128)[:, :, c0:c0 + ncols]


def wtile_rows(w, r0, nrows, c0, ncols):
    return w[r0:r0 + nrows, c0:c0 + ncols].rearrange("(fc p) n -> p fc n", p=128)


class _Stop(Exception):
    pass


def build(c, stop=None, dbg=False):
    nc = bass.Bass("TRN2", target_bir_lowering=False)
    try:
        _build(c, nc, stop, dbg)
    except _Stop:
        pass
    return nc


def _build(c, nc, stop, dbg):
    tr = Tracker(nc)

    def _ck(k):
        if stop == k:
            raise _Stop()

    KC, FC, TT, NT, L, D = c.KC, c.FC, c.TT, c.NT, c.L, c.D
    WT = c.WT

    def din(name, shape, dt=F32):
        return nc.dram_tensor(name, list(shape), dt, kind="ExternalInput").ap()

    def dsc(name, shape, dt):
        if dbg and name in ("x1T", "x2T", "x3T", "rqT", "rkT", "rgT", "rv", "dqT", "cdkT", "cdv", "ckd", "cvd", "mixT"):
            return nc.dram_tensor(name, list(shape), dt, kind="ExternalOutput").ap()
        return nc.dram_tensor(name, list(shape), dt).ap()

    I = {}
    I["xin"] = din("xin", [D, c.NTA])
    I["cT"] = din("cT", [128, KC * 2])
    I["ada_w"] = din("ada_w", [D, 9 * D])
    I["ada_bT"] = din("ada_bT", [128, 9 * KC])
    I["nwT"] = din("nwT", [128, 3 * KC])
    I["finT"] = din("finT", [128, KC])
    for i in range(2):
        I["wg%d" % i] = din("wg%d" % i, [D, c.DFF])
        I["wu%d" % i] = din("wu%d" % i, [D, c.DFF])
        I["wd%d" % i] = din("wd%d" % i, [c.DFF, D])
    I["w_in"] = din("w_in", [D, c.INC])
    I["w_out"] = din("w_out", [c.MIX, D])
    I["rdec"] = din("rdec", [1, 2 * c.RH])
    I["lamp"] = din("lamp", [1, 512])
    I["subw"] = din("subw", [1, 256])
    I["gnwT"] = din("gnwT", [128, c.RW // 128])
    I["segmeta"] = din("segmeta", [1, 20])
    I["cosr"] = din("cosr", [128, 2 * NT])
    I["sinr"] = din("sinr", [128, 2 * NT])
    I["cosd"] = din("cosd", [128, NT])
    I["sind"] = din("sind", [128, NT])
    I["consts"] = din("consts", [128, c.NCONST])
    yT = nc.dram_tensor("yT", [D, NT], F32, kind="ExternalOutput").ap()

    x1T = dsc("x1T", [D, c.NTA], F32)
    x2T = dsc("x2T", [D, NT], F32)
    x3T = dsc("x3T", [D, NT], F32)
    rqT = dsc("rqT", [c.RW, NT], BF16)
    rkT = dsc("rkT", [c.RW, NT], BF16)
    rgT = dsc("rgT", [c.RW, NT], BF16)
    rv = dsc("rv", [NT, c.RW], BF16)
    dqT = dsc("dqT", [c.DW, NT], BF16)
    dk_loc = [dsc("dk_loc%d" % h, [256, NT], BF16) for h in range(c.DHH)]
    dv_loc = [dsc("dv_loc%d" % h, [NT, 256], BF16) for h in range(c.DHH)]
    dk_all = [dsc("dk_all%d" % h, [4 * 256, NT], BF16) for h in range(c.DHH)]
    dv_all = [dsc("dv_all%d" % h, [4 * NT, 256], BF16) for h in range(c.DHH)]
    cdkT = dsc("cdkT", [c.DW, L], BF16)
    cdv = dsc("cdv", [L, c.DW], BF16)
    ckd = dsc("ckd", [L, c.RW], BF16)
    cvd = dsc("cvd", [L, c.RW], BF16)
    st_loc = [dsc("st_loc%d" % h, [512, 256], F32) for h in range(c.RH)]
    st_all = [dsc("st_all%d" % h, [4 * 512, 256], F32) for h in range(c.RH)]
    mixT = dsc("mixT", [c.MIX, NT], BF16)

    DBW = 512 if D >= 512 else D
    NDJ = DBW // 128
    FSB = 16
    ws = WStream(tr, 4, max(KC * WT, min(FSB, max(FC, c.MC)) * DBW))
    cmat = nc.alloc_sbuf_tensor("cmat", [128, 4, 128], BF16)
    ones_bf, ident_bf, permr_bf, permd_bf = cmat[:, 0, :], cmat[:, 1, :], cmat[:, 2, :], cmat[:, 3, :]
    modv = nc.alloc_sbuf_tensor("modv", [128, 18, KC], F32)
    mk = {e: nc.alloc_sbuf_tensor("mk_" + e, [128, 2], F32) for e in ("act", "dve", "pool")}
    ld = tr.dsem("ld_misc")
    st_sem = tr.dsem("st_sem")

    def MV(sub, kind, stream):
        return modv[:, sub * 6 + kind * 2 + stream, :]

    tiles_lat = [(t0, min(TT, NT - t0), 0) for t0 in range(0, NT, TT)]
    tiles_pre = tiles_lat + [(NT, L, 1)]

    def plan_down(w, nin):
        for db in range(D // DBW):
            for f0 in range(0, nin, FSB):
                nf = min(FSB, nin - f0)
                ws.add(wtile_rows(w, f0 * 128, nf * 128, db * DBW, DBW), nf, DBW, "dn")

    def plan_ffn(i, tiles):
        pg = 0
        for _ in tiles:
            for p, f0 in enumerate(range(0, c.DFF, WT)):
                ws.add(wtile_cols(I["wg%d" % i], f0, WT), KC, WT, "g")
                ws.add(wtile_cols(I["wu%d" % i], f0, WT), KC, WT, "u")
                if i == 0 and ada_plan["g"] < NADA and ada_after_pair(pg):
                    plan_ada_tile()
                pg += 1
            plan_down(I["wd%d" % i], FC)

    RW, DW = c.RW, c.DW
    segs = [("rq", 0, RW), ("rk", RW, RW), ("rv", 2 * RW, RW), ("rg", 3 * RW, RW),
            ("dq", 4 * RW, DW), ("dk", 4 * RW + DW, DW), ("dv", 4 * RW + 2 * DW, DW)]

    def plan_proj():
        for (t0, T, stream) in tiles_pre:
            for (nm, c0, w) in segs:
                if stream == 1 and nm in ("rq", "rg", "dq"):
                    continue
                for cc in range(0, w, WT):
                    ws.add(wtile_cols(I["w_in"], c0 + cc, WT), KC, WT, "in")

    AW = 512 if ((9 * D) % 512 == 0 and (5 * D) % 512 == 0 and KC % 4 == 0) else WT
    NKH = 2 if AW == 512 else 1
    KH = KC // NKH
    NADA = 9 * D // AW
    NADA0 = 5 * D // AW
    ada_plan = {"g": 0}

    def plan_ada_tile():
        g = ada_plan["g"]
        ada_plan["g"] += 1
        for kh in range(NKH):
            ws.add(wtile_rows(I["ada_w"], kh * KH * 128, KH * 128, g * AW, AW), KH, AW, "ada")

    NPAIR0 = len(tiles_pre) * (c.DFF // WT)

    def ada_after_pair(pg):
        return False

    for g in range(NADA0):
        plan_ada_tile()
    plan_ffn(0, tiles_pre)
    plan_proj()
    while ada_plan["g"] < NADA:
        plan_ada_tile()
    for _ in tiles_lat:
        plan_down(I["w_out"], c.MC)
    plan_ffn(1, tiles_lat)

    sc = nc.alloc_sbuf_tensor("sc", [128, KC, 2], BF16)
    adab = nc.alloc_sbuf_tensor("adab", [128, 9 * KC], F32)
    adps = nc.alloc_sbuf_tensor("adps", [128, 4, 2], F32)
    nw = nc.alloc_sbuf_tensor("nw", [128, 3 * KC], F32)
    adtmp = nc.alloc_sbuf_tensor("adtmp", [128, 2, AW // 128], F32)
    ada_state = {"g": 0, "ev": None, "prep": None}
    nj_a = AW // 128
    assert KC % nj_a == 0

    def ada_tile(banks, extra_waits=()):
        g = ada_state["g"]
        ada_state["g"] += 1
        for kh in range(NKH):
            wi, wv, wtok = ws.next("ada")
            tr.wait("pe", wtok, ada_state["prep"], ada_state["ev"], *[w for w in extra_waits if w is not None])
            for j in range(nj_a):
                for kl in range(KH):
                    kc = kh * KH + kl
                    mm = nc.tensor.matmul(banks[j][:, 0:2], lhsT=wv[:, kl, j * 128:(j + 1) * 128], rhs=sc[:, kc, :],
                                          start=(kc == 0), stop=(kc == KC - 1))
            t_pe = tr.mark("pe", mm)
            ws.release(wi, t_pe)
        tr.wait("dve", t_pe, ada_state["prep"])
        for j in range(nj_a):
            i_ = nc.vector.tensor_copy(out=adps[:, j, :], in_=banks[j][:, 0:2])
        t_pe = tr.mark("dve", i_)
        ps_ap = adps[:].rearrange("p j t -> p (j t)")
        ch0 = g * nj_a
        mi = ch0 // KC
        sub, kind = mi // 3, mi % 3
        k0 = ch0 % KC
        tok = None
        for col in range(2):
            pin = ps_ap[:, 0:2 * nj_a].rearrange("p (j t) -> p j t", t=2)[:, :, col]
            ab = adab[:, ch0:ch0 + nj_a]
            if kind == 0:
                tok = tr.op("dve", lambda e: e.tensor_tensor(out=MV(sub, 1, col)[:, k0:k0 + nj_a], in0=pin, in1=ab, op=ALU.add),
                            waits=[t_pe, ada_state["prep"]])
            elif kind == 1:
                t1 = tr.op("dve", lambda e: e.tensor_tensor(out=adtmp[:, col, :], in0=pin, in1=ab, op=ALU.add),
                           waits=[t_pe, ada_state["prep"]])
                tok = tr.op("dve", lambda e: e.tensor_tensor(out=MV(sub, 0, col)[:, k0:k0 + nj_a], in0=adtmp[:, col, :],
                                                              in1=nw[:, sub * KC + k0:sub * KC + k0 + nj_a], op=ALU.mult), waits=[t1])
            else:
                gsc = 1.0 if sub == 1 else 0.5
                tok = tr.op("dve", lambda e: e.scalar_tensor_tensor(out=MV(sub, 2, col)[:, k0:k0 + nj_a], in0=pin, scalar=gsc, in1=ab,
                                                                     op0=ALU.mult, op1=ALU.add), waits=[t_pe, ada_state["prep"]])
        ada_state["ev"] = tok
        return tok

    with ExitStack() as es:
        cst = es.enter_context(nc.sbuf_tensor("cst0", [128, 4 * 128], F32))
        cTt = es.enter_context(nc.sbuf_tensor("cTt", [128, KC * 2], F32))
        psa = [es.enter_context(nc.psum_tensor("psa%d" % k, [128, 512], F32)) for k in range(4)]

        t_c = tr.dma("sp", cTt[:], I["cT"], ld)
        tr.dma("sp", adab[:], I["ada_bT"], ld)
        tr.dma("sp", nw[:], I["nwT"], ld)
        tr.dma("sp", cst[:, 0:384], I["consts"][:, 0:384], ld)
        t_ld = tr.tot(ld)
        tr.op("dve", lambda e: e.memset(ones_bf, 1.0), mark=False)
        t_cm = tr.op("dve", lambda e: e.tensor_copy(out=cmat[:, 1:4, :], in_=cst[:, 0:384].rearrange("p (a b) -> p a b", a=3)),
                     waits=[t_ld])
        t_sc = tr.op("act", lambda e: e.activation(out=sc[:].rearrange("p a b -> p (a b)"), in_=cTt[:], func=AF.Silu),
                     waits=[t_ld])
        tr.wait("dve", t_ld)
        for sub in range(3):
            r1 = (3 * sub + 1) * KC
            nc.vector.tensor_scalar(out=adab[:, r1:r1 + KC], in0=adab[:, r1:r1 + KC], scalar1=1.0, scalar2=1.0, op0=ALU.add, op1=ALU.mult)
            r2 = (3 * sub + 2) * KC
            i_ = nc.vector.tensor_scalar(out=adab[:, r2:r2 + KC], in0=adab[:, r2:r2 + KC], scalar1=(1.0 if sub == 1 else 0.5), scalar2=0.0,
                                         op0=ALU.mult, op1=ALU.add)
        t_prep = tr.mark("dve", i_)
        ada_state["prep"] = [t_prep, t_sc, t_cm]
        for g in range(NADA0):
            ada_tile(psa)
        tr.barrier(mk)

    NGX = 4 if KC >= 4 else 1
    xsems = [tr.dsem("xsem%d" % g) for g in range(NGX)]

    def norm_tile(src, t0, T, A, B, xbuf, hT, ps_ss, rstd, pre_waits):
        pre_waits = [w for w in pre_waits if w is not None]
        xv = xbuf[:, :, 0:T]
        srcv = src.rearrange("(kc p) n -> p kc n", p=128)[:, :, t0:t0 + T]
        NG = NGX
        gk = KC // NG
        lt = []
        for g in range(NG):
            lt.append(tr.dma("sp", xv[:, g * gk:(g + 1) * gk, :], srcv[:, g * gk:(g + 1) * gk, :], xsems[g], waits=pre_waits))
        sq = []
        for g in range(NG):
            sq.append(tr.op("act", lambda e, g=g: e.activation(out=hT[:, g * gk:(g + 1) * gk, 0:T], in_=xv[:, g * gk:(g + 1) * gk, :],
                                                             func=AF.Square), waits=[lt[g]] + pre_waits))
        for kc in range(KC):
            if kc % gk == 0:
                tr.wait("pe", sq[kc // gk], *pre_waits)
            mm = nc.tensor.matmul(ps_ss[:, 0:T], lhsT=ones_bf, rhs=hT[:, kc, 0:T], start=(kc == 0), stop=(kc == KC - 1))
        t_ss = tr.mark("pe", mm)
        t_r0 = tr.op("dve", lambda e: e.tensor_scalar(out=rstd[:, 0:T], in0=ps_ss[:, 0:T], scalar1=1.0 / D, scalar2=EPS,
                                                      op0=ALU.mult, op1=ALU.add), waits=[t_ss] + pre_waits)
        t_r0 = tr.op("dve", lambda e: e.reciprocal(out=rstd[:, 0:T], in_=rstd[:, 0:T]), waits=[tr.op("act", lambda e: e.activation(out=rstd[:, 0:T], in_=rstd[:, 0:T], func=AF.Sqrt), waits=[t_r0])])
        tr.wait("dve", t_r0, *lt)
        toks = []
        for g in range(NG):
            for kc in range(g * gk, (g + 1) * gk):
                i1 = nc.vector.scalar_tensor_tensor(out=xv[:, kc, :], in0=xv[:, kc, :], scalar=A[:, kc:kc + 1], in1=rstd[:, 0:T],
                                                    op0=ALU.mult, op1=ALU.mult)
            t1 = tr.mark("dve", i1)
            tr.wait("act", t1, t_ss)
            for kc in range(g * gk, (g + 1) * gk):
                i2 = nc.scalar.activation(out=hT[:, kc, 0:T], in_=xv[:, kc, :], func=AF.Identity, bias=B[:, kc:kc + 1], scale=1.0)
            toks.append(tr.mark("act", i2))
        return toks

    def make_rb_state(rbufs, name):
        return {"i": 0, "free": [None, None], "psfree": None, "bufs": rbufs,
                "lsem": [tr.dsem(name + "_l%d" % k) for k in range(2)], "ssem": [tr.dsem(name + "_s%d" % k) for k in range(2)]}

    def down_proj(aT, nin, T, resid, dst, G, t0, psd, rb_state, a_ready, tag="dn"):
        rv_ = resid.rearrange("(kc p) n -> p kc n", p=128)
        dv_ = dst.rearrange("(kc p) n -> p kc n", p=128)
        a_ready = [w for w in a_ready if w is not None]
        last_pe = None
        for db in range(D // DBW):
            k = rb_state["i"] % 2
            rb = rb_state["bufs"][k]
            rfree = rb_state["free"][k]
            rb_state["i"] += 1
            t_res = tr.dma("sp", rb[:, :, 0:T], rv_[:, db * NDJ:(db + 1) * NDJ, t0:t0 + T], rb_state["lsem"][k], waits=[rfree])
            nsub = (nin + FSB - 1) // FSB
            ev = None
            for si in range(nsub):
                f0 = si * FSB
                nf = min(FSB, nin - f0)
                wi, wv, wtok = ws.next(tag)
                tr.wait("pe", wtok, *a_ready)
                if si == 0:
                    tr.wait("pe", rb_state["psfree"])
                for dj in range(NDJ):
                    for fl in range(nf):
                        f = f0 + fl
                        mm = nc.tensor.matmul(psd[dj][:, 0:T], lhsT=wv[:, fl, dj * 128:(dj + 1) * 128], rhs=aT[:, f, 0:T],
                                              start=(f == 0), stop=(f == nin - 1))
                    if si == nsub - 1:
                        pt = tr.mark("pe", mm)
                        kc = db * NDJ + dj
                        ev = tr.op("dve", lambda e, dj=dj, kc=kc: e.scalar_tensor_tensor(
                            out=rb[:, dj, 0:T], in0=psd[dj][:, 0:T], scalar=G[:, kc:kc + 1], in1=rb[:, dj, 0:T],
                            op0=ALU.mult, op1=ALU.add), waits=[pt, t_res])
                last_pe = tr.mark("pe", mm) if si < nsub - 1 else pt
                ws.release(wi, last_pe)
            rb_state["psfree"] = ev
            rb_state["free"][k] = tr.dma("sp", dv_[:, db * NDJ:(db + 1) * NDJ, t0:t0 + T], rb[:, :, 0:T], rb_state["ssem"][k], waits=[ev])
        return last_pe

    def ffn_phase(i, src, dst, tiles, sub):
        with ExitStack() as es:
            aT = es.enter_context(nc.sbuf_tensor("aT%d" % i, [128, FC, TT], BF16))
            hT = es.enter_context(nc.sbuf_tensor("hT%d" % i, [128, KC, TT], BF16))
            rbufs = [es.enter_context(nc.sbuf_tensor("rb%d_%d" % (i, k), [128, NDJ, TT], F32)) for k in range(2)]
            rstd = es.enter_context(nc.sbuf_tensor("rstd%d" % i, [128, TT], F32))
            sg1 = es.enter_context(nc.sbuf_tensor("sg%d" % i, [128, TT], F32))
            sg = [sg1, sg1]
            ps = [es.enter_context(nc.psum_tensor("psf%d_%d" % (i, k), [128, 512], F32)) for k in range(8)]
            assert KC * TT * 2 <= FC * TT, "x tile must fit inside aT"
            xbuf = aT[:].rearrange("p a b -> p (a b)")[:, 0:KC * TT * 2].bitcast(F32).rearrange("p (a b) -> p a b", a=KC)
            rb_state = make_rb_state(rbufs, "rbf%d" % i)
            pg_ = [0]
            last_ada = [None]
            a_free = None
            gu_free = [None, None]
            sg_free = [None, None]
            cnt = 0
            for (t0, T, stream) in tiles:
                A, B, G = MV(sub, 0, stream), MV(sub, 1, stream), MV(sub, 2, stream)
                pre = [a_free, rb_state["psfree"]]
                h_ready = norm_tile(src, t0, T, A, B, xbuf, hT, ps[4], rstd, pre)
                a_toks = []
                for p_, f0 in enumerate(range(0, c.DFF, WT)):
                    gi, gv, gtok = ws.next("g")
                    ui, uv, utok = ws.next("u")
                    for j in range(WT // 128):
                        f = f0 // 128 + j
                        if f >= FC:
                            break
                        b = cnt % 2
                        cnt += 1
                        psg, psu = ps[2 * b], ps[2 * b + 1]
                        tr.wait("pe", gtok, utok, gu_free[b], *h_ready)
                        for kc in range(KC):
                            nc.tensor.matmul(psg[:, 0:T], lhsT=gv[:, kc, j * 128:(j + 1) * 128], rhs=hT[:, kc, 0:T],
                                             start=(kc == 0), stop=(kc == KC - 1))
                        for kc in range(KC):
                            mm = nc.tensor.matmul(psu[:, 0:T], lhsT=uv[:, kc, j * 128:(j + 1) * 128], rhs=hT[:, kc, 0:T],
                                                  start=(kc == 0), stop=(kc == KC - 1))
                        pt = tr.mark("pe", mm)
                        t_s = tr.op("act", lambda e: e.activation(out=sg[b][:, 0:T], in_=psg[:, 0:T], func=AF.Silu),
                                    waits=[pt, sg_free[b]])
                        t_a = tr.op("dve", lambda e: e.tensor_tensor(out=aT[:, f, 0:T], in0=sg[b][:, 0:T], in1=psu[:, 0:T],
                                                                      op=ALU.mult), waits=[t_s, pt] + h_ready)
                        gu_free[b] = t_a
                        sg_free[0] = t_a
                        sg_free[1] = t_a
                        a_toks = [t_a]
                    ws.release(gi, pt)
                    ws.release(ui, pt)
                    if i == 0 and ada_state["g"] < NADA and ada_after_pair(pg_[0]):
                        last_ada[0] = ada_tile(ps[4:8], [rb_state["psfree"]])
                    pg_[0] += 1
                a_free = down_proj(aT, FC, T, src, dst, G, t0, ps[4:4 + NDJ], rb_state, a_toks + [gu_free[0], gu_free[1], last_ada[0]])
            tr.barrier(mk)

    _ck(0)
    ffn_phase(0, I["xin"], x1T, tiles_pre, 0)
    _ck(1)

    with ExitStack() as es:
        hT = es.enter_context(nc.sbuf_tensor("hTp", [128, KC, TT], BF16))
        xbuf = es.enter_context(nc.sbuf_tensor("xbp", [128, KC, TT], F32))
        rstd = es.enter_context(nc.sbuf_tensor("rstdp", [128, TT], F32))
        ropes = es.enter_context(nc.sbuf_tensor("ropes", [128, 6, TT], F32))
        xb = [es.enter_context(nc.sbuf_tensor("xbb%d" % k, [128, TT], BF16)) for k in range(2)]
        t1b = [es.enter_context(nc.sbuf_tensor("t1b%d" % k, [128, TT], F32)) for k in range(2)]
        t2b = [es.enter_context(nc.sbuf_tensor("t2b%d" % k, [128, TT], F32)) for k in range(2)]
        ob = [es.enter_context(nc.sbuf_tensor("ob%d" % k, [128, TT], BF16)) for k in range(3)]
        ps = [es.enter_context(nc.psum_tensor("psp%d" % k, [128, 512], F32)) for k in range(8)]
        rsem = tr.dsem("ropesem")
        t_rope = None
        rope_free = None
        ob_free = [None, None, None]
        obsem = [tr.dsem("obsem%d" % k) for k in range(3)]
        xb_free = [None, None]
        t12_free = [None, None]
        psx_free = [None, None]
        psr_free = [None, None]
        obi = 0
        ci = 0
        h_free = None
        last_pe_proj = None
        for (t0, T, stream) in tiles_pre:
            A, B = MV(1, 0, stream), MV(1, 1, stream)
            pre = [h_free]
            h_ready = norm_tile(x1T, t0, T, A, B, xbuf, hT, ps[7], rstd, pre)
            if stream == 0:
                tr.wait("sp", rope_free)
                cr_ = I["cosr"].rearrange("p (a n) -> p a n", a=2)
                sr_ = I["sinr"].rearrange("p (a n) -> p a n", a=2)
                tr.dma("sp", ropes[:, 0:2, 0:T], cr_[:, :, t0:t0 + T], rsem)
                tr.dma("sp", ropes[:, 2:4, 0:T], sr_[:, :, t0:t0 + T], rsem)
                tr.dma("sp", ropes[:, 4, 0:T], I["cosd"][:, t0:t0 + T], rsem)
                tr.dma("sp", ropes[:, 5, 0:T], I["sind"][:, t0:t0 + T], rsem)
                t_rope = tr.tot(rsem)
            if stop == 20:
                tr.barrier(mk)
                raise _Stop()
            for si_, (nm, c0, w) in enumerate(segs):
                if stop is not None and 21 <= stop <= 27 and si_ >= stop - 20:
                    tr.barrier(mk)
                    raise _Stop()
                if stream == 1 and nm in ("rq", "rg", "dq"):
                    continue
                token_major = nm in ("rv", "dv") or (stream == 1 and nm == "rk")
                for cc in range(0, w, WT):
                    wi, wv, wtok = ws.next("in")
                    if token_major:
                        assert WT == 256
                        dstt = {"rv": rv, "dv": dv_loc[cc // 256], "rk": ckd}[nm] if stream == 0 else {"rv": cvd, "dv": cdv, "rk": ckd}[nm]
                        dcol = 0 if (stream == 0 and nm == "dv") else cc
                        for ts in range(T // 128):
                            b = ci % 2
                            ci += 1
                            tr.wait("pe", wtok, psx_free[b], *h_ready)
                            for kc in range(KC):
                                mm = nc.tensor.matmul(ps[b][:, 0:WT], lhsT=hT[:, kc, ts * 128:(ts + 1) * 128], rhs=wv[:, kc, :],
                                                      start=(kc == 0), stop=(kc == KC - 1))
                            pt = tr.mark("pe", mm)
                            o = ob[obi % 3]
                            ofree = ob_free[obi % 3]
                            scale = (1.0 / 16.0) if nm == "rk" else 1.0
                            t_o = tr.op("act", lambda e, o=o, b=b, scale=scale: e.activation(
                                out=o[:, 0:WT], in_=ps[b][:, 0:WT], func=AF.Identity, scale=scale), waits=[pt, ofree])
                            psx_free[b] = t_o
                            trow = (t0 - NT if stream == 1 else t0) + ts * 128
                            ob_free[obi % 3] = tr.dma("sp", dstt[trow:trow + 128, dcol:dcol + WT], o[:, 0:WT], obsem[obi % 3], waits=[t_o])
                            obi += 1
                    else:
                        for j in range(WT // 128):
                            col = cc + j * 128
                            b = ci % 2
                            ci += 1
                            tr.wait("pe", wtok, psx_free[b], *h_ready)
                            for kc in range(KC):
                                mm = nc.tensor.matmul(ps[b][:, 0:T], lhsT=wv[:, kc, j * 128:(j + 1) * 128], rhs=hT[:, kc, 0:T],
                                                      start=(kc == 0), stop=(kc == KC - 1))
                            pt = tr.mark("pe", mm)
                            o = ob[obi % 3]
                            ofree = ob_free[obi % 3]
                            rope = (stream == 0) and nm in ("rq", "rk", "dq", "dk")
                            if not rope:
                                if nm == "rg":
                                    t_o = tr.op("act", lambda e, o=o, b=b: e.activation(out=o[:, 0:T], in_=ps[b][:, 0:T], func=AF.Silu),
                                                waits=[pt, ofree])
                                    dstt = rgT
                                else:
                                    t_o = tr.op("act", lambda e, o=o, b=b: e.activation(out=o[:, 0:T], in_=ps[b][:, 0:T], func=AF.Identity),
                                                waits=[pt, ofree])
                                    dstt = cdkT
                                psx_free[b] = t_o
                                tcol = t0 - NT if stream == 1 else t0
                            else:
                                if nm in ("rq", "rk"):
                                    chunk = (col // 128) % 2
                                    cos = ropes[:, chunk, 0:T]
                                    sin = ropes[:, 2 + chunk, 0:T]
                                    perm = permr_bf
                                    scale = 1.0 if nm == "rq" else 1.0 / 16.0
                                    dstt = rqT if nm == "rq" else rkT
                                else:
                                    cos = ropes[:, 4, 0:T]
                                    sin = ropes[:, 5, 0:T]
                                    perm = permd_bf
                                    scale = 128.0 ** -0.5 if nm == "dq" else 1.0
                                    dstt = dqT if nm == "dq" else dk_loc[col // 256]
                                t_xb = tr.op("act", lambda e, b=b: e.activation(out=xb[b][:, 0:T], in_=ps[b][:, 0:T], func=AF.Identity),
                                             waits=[pt, xb_free[b]])
                                tr.wait("pe", t_xb, psr_free[b])
                                mm = nc.tensor.matmul(ps[2 + b][:, 0:T], lhsT=perm, rhs=xb[b][:, 0:T], start=True, stop=True)
                                pr = tr.mark("pe", mm)
                                xb_free[b] = pr
                                t_1 = tr.op("dve", lambda e, b=b, cos=cos, scale=scale: e.scalar_tensor_tensor(
                                    out=t1b[b][:, 0:T], in0=ps[b][:, 0:T], scalar=scale, in1=cos, op0=ALU.mult, op1=ALU.mult),
                                    waits=[pt, t_xb, t12_free[b], t_rope])
                                psx_free[b] = t_1
                                t_2 = tr.op("dve", lambda e, b=b, sin=sin, scale=scale: e.scalar_tensor_tensor(
                                    out=t2b[b][:, 0:T], in0=ps[2 + b][:, 0:T], scalar=scale, in1=sin, op0=ALU.mult, op1=ALU.mult),
                                    waits=[pr])
                                psr_free[b] = t_2
                                rope_free = t_2
                                t_o = tr.op("dve", lambda e, o=o, b=b: e.tensor_tensor(out=o[:, 0:T], in0=t1b[b][:, 0:T],
                                                                                      in1=t2b[b][:, 0:T], op=ALU.add),
                                            waits=[t_1, t_2, ofree])
                                t12_free[b] = t_o
                                tcol = t0
                            drow = (col % 256) if (stream == 0 and nm == "dk") else col
                            ob_free[obi % 3] = tr.dma("sp", dstt[drow:drow + 128, tcol:tcol + T], o[:, 0:T], obsem[obi % 3], waits=[t_o])
                            obi += 1
                    ws.release(wi, pt)
                    last_pe_proj = pt
            h_free = last_pe_proj
        tr.barrier(mk)

    _ck(2)
    groups = [[0, 1, 2, 3], [4, 5, 6, 7]]
    t_kv = []
    for h in range(c.DHH):
        ksem = nc.alloc_semaphore("agk%d" % h)
        vsem = nc.alloc_semaphore("agv%d" % h)
        nc.gpsimd.collective_compute("AllGather", ALU.bypass, replica_groups=groups, ins=[dk_loc[h].opt()], outs=[dk_all[h].opt()]).then_inc(ksem, 1)
        nc.gpsimd.collective_compute("AllGather", ALU.bypass, replica_groups=groups, ins=[dv_loc[h].opt()], outs=[dv_all[h].opt()]).then_inc(vsem, 1)
        t_kv.append([(ksem, 1, "agk%d" % h), (vsem, 1, "agv%d" % h)])
    if stop == 3:
        tr.wait("pool", t_kv)
        tr.barrier(mk)
    _ck(3)
    RH = c.RH
    NCH = c.NCH
    RH = c.RH
    NCH = c.NCH
    lg = nc.alloc_sbuf_tensor("lg", [128, 2 * RH], F32)
    ksc = nc.alloc_sbuf_tensor("ksc", [128, 4, RH], F32)
    cwt = nc.alloc_sbuf_tensor("cwt", [128, 2, c.LC, RH], F32)
    coef = nc.alloc_sbuf_tensor("coef", [128, 10, RH], F32)
    t_st = []

    def ret_phase(part):
        with ExitStack() as es:
            cst = es.enter_context(nc.sbuf_tensor("cstr_p%d" % part, [128, c.NCONST], F32))
            rdec = es.enter_context(nc.sbuf_tensor("rdec_sb_p%d" % part, [128, 2 * RH], F32))
            meta = es.enter_context(nc.sbuf_tensor("meta_p%d" % part, [128, 20], F32))
            dint = es.enter_context(nc.sbuf_tensor("dint_p%d" % part, [128, 128], F32))
            dqf = es.enter_context(nc.sbuf_tensor("dqf_p%d" % part, [128, 128], F32))
            dqb = es.enter_context(nc.sbuf_tensor("dqb_p%d" % part, [128, 128], F32))
            tmpd = es.enter_context(nc.sbuf_tensor("tmpd_p%d" % part, [128, 128], F32))
            gnw = es.enter_context(nc.sbuf_tensor("gnw_p%d" % part, [128, c.RW // 128], F32))
            qT = es.enter_context(nc.sbuf_tensor("qTr_p%d" % part, [128, 2, NT], BF16))
            kT = es.enter_context(nc.sbuf_tensor("kTr_p%d" % part, [128, 2, NT], BF16))
            vt = es.enter_context(nc.sbuf_tensor("vtr_p%d" % part, [128, NCH, 256], BF16))
            kf = es.enter_context(nc.sbuf_tensor("kfr_p%d" % part, [128, NCH, 256], BF16))
            kb = es.enter_context(nc.sbuf_tensor("kbr_p%d" % part, [128, NCH, 256], BF16))
            qf = es.enter_context(nc.sbuf_tensor("qfr_p%d" % part, [128, 2, NT], BF16))
            qb = es.enter_context(nc.sbuf_tensor("qbr_p%d" % part, [128, 2, NT], BF16))
            gT = qf
            ckt = es.enter_context(nc.sbuf_tensor("ckt_p%d" % part, [128, c.LC, 256], BF16))
            cvt = es.enter_context(nc.sbuf_tensor("cvt_p%d" % part, [128, c.LC, 256], BF16))
            ckw = es.enter_context(nc.sbuf_tensor("ckw_p%d" % part, [128, 2, c.LC, 256], BF16))
            Sloc = es.enter_context(nc.sbuf_tensor("Sloc_p%d" % part, [128, 2, 2, 256], F32))
            Sg = es.enter_context(nc.sbuf_tensor("Sg_p%d" % part, [128, 4, 2, 2, 256], F32))
            Sst = es.enter_context(nc.sbuf_tensor("Sst_p%d" % part, [128, 2, 2, 256], F32))
            Sbf = es.enter_context(nc.sbuf_tensor("Sbf_p%d" % part, [128, 2, 2, 256], BF16))
            sdt = es.enter_context(nc.sbuf_tensor("sdt_p%d" % part, [128, 128], BF16))
            oacc = es.enter_context(nc.sbuf_tensor("oacc_p%d" % part, [128, 2, NT], F32))
            osq = es.enter_context(nc.sbuf_tensor("osq_p%d" % part, [128, 2, 512], BF16))
            obf = es.enter_context(nc.sbuf_tensor("obf_p%d" % part, [128, 2, 512], BF16))
            stat = es.enter_context(nc.sbuf_tensor("stat_p%d" % part, [128, 3, 512], F32))
            ymix = es.enter_context(nc.sbuf_tensor("ymix_p%d" % part, [128, 2, NT], BF16))
            wfull = es.enter_context(nc.sbuf_tensor("wfull_p%d" % part, [128, 2, RH, NCH], F32))
            ps = [es.enter_context(nc.psum_tensor("psr%d_%d" % (k, part), [128, 512], F32)) for k in range(7)]
            pst = es.enter_context(nc.psum_tensor("pstr%d" % part, [128, 1024], BF16))

            tr.dma("sp", cst[:], I["consts"], ld)
            tr.dma("sp", rdec[:], I["rdec"].partition_broadcast(128)[:, 0, :], ld)
            tr.dma("sp", meta[:], I["segmeta"].partition_broadcast(128)[:, 0, :], ld)
            tr.dma("sp", gnw[:], I["gnwT"], ld)
            t_l = tr.tot(ld)
            OFF = 384
            relf, maskf = cst[:, OFF:OFF + 128], cst[:, OFF + 128:OFF + 256]
            relb, maskb = cst[:, OFF + 256:OFF + 384], cst[:, OFF + 384:OFF + 512]
            posqf, posqb = cst[:, OFF + 512:OFF + 640], cst[:, OFF + 640:OFF + 768]
            pkf, pkb = cst[:, OFF + 768:OFF + 769], cst[:, OFF + 769:OFF + 770]
            ctxf = cst[:, OFF + 770:OFF + 770 + c.LC]
            ctxb = cst[:, OFF + 770 + c.LC:OFF + 770 + 2 * c.LC]
            t_setup = None
            t5_ = None
            if part == 1:
                t_e = tr.op("act", lambda e: e.activation(out=lg[:], in_=rdec[:], func=AF.Exp), waits=[t_l])
                t_lg = tr.op("dve", lambda e: e.tensor_scalar(out=lg[:], in0=lg[:], scalar1=-1.0, scalar2=0.0, op0=ALU.mult, op1=ALU.add), waits=[t_e])
                tr.wait("act", t_lg)
                for h in range(RH):
                    lf, lb = lg[:, h:h + 1], lg[:, RH + h:RH + h + 1]
                    nc.scalar.activation(out=ksc[:, 0, h:h + 1], in_=pkf, func=AF.Exp, scale=lf)
                    nc.scalar.activation(out=ksc[:, 1, h:h + 1], in_=pkb, func=AF.Exp, scale=lb)
                    nc.scalar.activation(out=ksc[:, 2, h:h + 1], in_=lf, func=AF.Exp, scale=128.0)
                    nc.scalar.activation(out=ksc[:, 3, h:h + 1], in_=lb, func=AF.Exp, scale=128.0)
                    for jc in range(c.LC):
                        nc.scalar.activation(out=cwt[:, 0, jc, h:h + 1], in_=ctxf[:, jc:jc + 1], func=AF.Exp, scale=lf)
                        nc.scalar.activation(out=cwt[:, 1, jc, h:h + 1], in_=ctxb[:, jc:jc + 1], func=AF.Exp, scale=lb)
                    for r in range(4):
                        nc.scalar.activation(out=coef[:, r, h:h + 1], in_=meta[:, r:r + 1], func=AF.Exp, scale=lf)
                        nc.scalar.activation(out=coef[:, 5 + r, h:h + 1], in_=meta[:, 9 + r:10 + r], func=AF.Exp, scale=lb)
                    nc.scalar.activation(out=coef[:, 4, h:h + 1], in_=meta[:, 8:9], func=AF.Exp, scale=lf)
                    i_ = nc.scalar.activation(out=coef[:, 9, h:h + 1], in_=meta[:, 17:18], func=AF.Exp, scale=lb)
                t3_ = tr.mark("act", i_)
                tr.wait("dve", t3_)
                for r in range(4):
                    nc.vector.tensor_scalar(out=coef[:, r, :], in0=coef[:, r, :], scalar1=meta[:, 4 + r:5 + r], scalar2=0.0, op0=ALU.mult, op1=ALU.add)
                    i_ = nc.vector.tensor_scalar(out=coef[:, 5 + r, :], in0=coef[:, 5 + r, :], scalar1=meta[:, 13 + r:14 + r], scalar2=0.0,
                                                 op0=ALU.mult, op1=ALU.add)
                t5_ = tr.mark("dve", i_)
                t_setup = t5_
            for e_ in ("act", "pe", "pool", "dve", "sp"):
                tr.wait(e_, t_setup, t5_, t_l)

            hl = tr.dsem("ret_ld%d" % part)
            slsem = tr.dsem("slsem%d" % part)
            gsem = tr.dsem("gsem%d" % part)
            ysem = tr.dsem("ysem%d" % part)

            def load_head(h, extra=None):
                tr.dma("sp", qT[:], rqT[h * 256:(h + 1) * 256, :].rearrange("(a p) n -> p a n", p=128), hl)
                tr.dma("sp", kT[:], rkT[h * 256:(h + 1) * 256, :].rearrange("(a p) n -> p a n", p=128), hl)
                tr.dma("sp", vt[:], rv[:, h * 256:(h + 1) * 256].rearrange("(a p) n -> p a n", p=128), hl)
                tr.dma("sp", ckt[:], ckd[:, h * 256:(h + 1) * 256].rearrange("(a p) n -> p a n", p=128), hl)
                tr.dma("sp", cvt[:], cvd[:, h * 256:(h + 1) * 256].rearrange("(a p) n -> p a n", p=128), hl)
                if extra is not None:
                    extra()
                return tr.tot(hl)

            def prep_head(h, t_ld_h, sc_f=None, sc_b=None):
                tk = None
                for i in range(NCH):
                    tr.wait("pe", t_ld_h, tk)
                    for dc in range(2):
                        mm = nc.tensor.transpose(pst[:, dc * 128:(dc + 1) * 128], kT[:, dc, i * 128:(i + 1) * 128], ident_bf)
                    pt = tr.mark("pe", mm)
                    s_f = ksc[:, 0, h:h + 1] if sc_f is None else sc_f(i)
                    s_b = ksc[:, 1, h:h + 1] if sc_b is None else sc_b(i)
                    tr.op("act", lambda e, i=i: e.activation(out=kf[:, i, :], in_=pst[:, 0:256], func=AF.Identity, scale=s_f),
                          waits=[pt], mark=False)
                    tk = tr.op("act", lambda e, i=i: e.activation(out=kb[:, i, :], in_=pst[:, 0:256], func=AF.Identity, scale=s_b))
                return tk

            def state_update(dirn, i, h, first, waits):
                kk = kf if dirn == 0 else kb
                bank = ps[5 + dirn]
                waits = [w for w in waits if w is not None]
                tr.wait("pe", *waits)
                for dc in range(2):
                    mm = nc.tensor.matmul(bank[:, dc * 256:(dc + 1) * 256], lhsT=kk[:, i, dc * 128:(dc + 1) * 128], rhs=vt[:, i, :],
                                          start=True, stop=True)
                pt = tr.mark("pe", mm)
                tr.wait("dve", pt, *waits)
                for dc in range(2):
                    if first:
                        i_ = nc.vector.tensor_copy(out=Sst[:, dirn, dc, :], in_=bank[:, dc * 256:(dc + 1) * 256])
                    else:
                        i_ = nc.vector.scalar_tensor_tensor(out=Sst[:, dirn, dc, :], in0=Sst[:, dirn, dc, :],
                                                            scalar=ksc[:, 2 + dirn, h:h + 1], in1=bank[:, dc * 256:(dc + 1) * 256],
                                                            op0=ALU.mult, op1=ALU.add)
                return tr.mark("dve", i_)

            if part == 1:
                OFP = OFF + 770 + 2 * c.LC
                posff, posfb = cst[:, OFP:OFP + NCH], cst[:, OFP + NCH:OFP + 2 * NCH]
                tr.wait("act", t_setup, t5_, t_l)
                for h in range(RH):
                    nc.scalar.activation(out=wfull[:, 0, h, :], in_=posff, func=AF.Exp, scale=lg[:, h:h + 1])
                    i_ = nc.scalar.activation(out=wfull[:, 1, h, :], in_=posfb, func=AF.Exp, scale=lg[:, RH + h:RH + h + 1])
                t_wf = tr.mark("act", i_)
                tr.wait("act", t_wf)
                t_slst = None
                t_ld_next = None
                t_cp = None
                for h in range(RH):
                    t_ld_h = load_head(h) if h == 0 else t_ld_next
                    tk = prep_head(h, t_ld_h, sc_f=lambda i, h=h: wfull[:, 0, h, i:i + 1], sc_b=lambda i, h=h: wfull[:, 1, h, i:i + 1])
                    n_rest = NADA - ada_state["g"]
                    for _ in range(n_rest if h == RH - 1 else min(n_rest, -(-(NADA - NADA0) // RH))):
                        ada_tile(ps[0:4])
                    tr.wait("pe", tk, t_cp)
                    for dirn in range(2):
                        kk = kf if dirn == 0 else kb
                        for dc in range(2):
                            for i in range(NCH):
                                mm = nc.tensor.matmul(ps[5 + dirn][:, dc * 256:(dc + 1) * 256], lhsT=kk[:, i, dc * 128:(dc + 1) * 128],
                                                      rhs=vt[:, i, :], start=(i == 0), stop=(i == NCH - 1))
                    pt = tr.mark("pe", mm)
                    tr.wait("dve", pt, t_slst)
                    for dirn in range(2):
                        i_ = nc.vector.tensor_copy(out=Sloc[:, dirn, :, :].rearrange("p c e -> p (c e)"), in_=ps[5 + dirn][:, 0:512])
                    t_cp = tr.mark("dve", i_)
                    t_slst = tr.dma("sp", st_loc[h].rearrange("(a p) e -> p a e", p=128), Sloc[:].rearrange("p d c e -> p (d c) e"), slsem,
                                    waits=[t_cp])
                    tr.wait("pool", t_slst)
                    ssem = nc.alloc_semaphore("ags%d" % h)
                    nc.gpsimd.collective_compute("AllGather", ALU.bypass, replica_groups=groups, ins=[st_loc[h].opt()],
                                                 outs=[st_all[h].opt()]).then_inc(ssem, 1)
                    t_st.append((ssem, 1, "ags%d" % h))
                    tr.wait("sp", tk, pt)
                    if h + 1 < RH:
                        t_ld_next = load_head(h + 1)
            if part == 2:
                t_head_free = None
                o_free = None
                y_free = None
                for h in range(RH):
                    tr.wait("sp", t_head_free, t_st[h])
                    stall_h = st_all[h].rearrange("(r a p) e -> p r a e", r=4, p=128)
                    t_ld_h = load_head(h, extra=lambda: [tr.dma("sp", Sg[:, r, :, :, :].rearrange("p d c e -> p (d c) e"), stall_h[:, r], hl)
                                                         for r in range(4)])
                    tk = prep_head(h, t_ld_h)
                    lf, lb = lg[:, h:h + 1], lg[:, RH + h:RH + h + 1]
                    tr.wait("act", t_head_free)
                    nc.scalar.activation(out=dint[:], in_=relf, func=AF.Exp, scale=lf)
                    nc.scalar.activation(out=tmpd[:], in_=relb, func=AF.Exp, scale=lb)
                    nc.scalar.activation(out=dqf[:], in_=posqf, func=AF.Exp, scale=lf)
                    i_ = nc.scalar.activation(out=dqb[:], in_=posqb, func=AF.Exp, scale=lb)
                    t_dq = tr.mark("act", i_)
                    tr.wait("dve", t_dq, t_head_free)
                    nc.vector.tensor_tensor(out=dint[:], in0=dint[:], in1=maskf, op=ALU.mult)
                    i_ = nc.vector.tensor_tensor(out=tmpd[:], in0=tmpd[:], in1=maskb, op=ALU.mult)
                    t_di = tr.op("dve", lambda e: e.tensor_tensor(out=dint[:], in0=dint[:], in1=tmpd[:], op=ALU.add), waits=[tr.mark("dve", i_)])
                    tr.wait("dve", t_ld_h, t_head_free)
                    for dirn in range(2):
                        for jc in range(c.LC):
                            i_ = nc.vector.tensor_scalar(out=ckw[:, dirn, jc, :], in0=ckt[:, jc, :], scalar1=cwt[:, dirn, jc, h:h + 1],
                                                         scalar2=0.0, op0=ALU.mult, op1=ALU.add)
                    t_cw = tr.mark("dve", i_)
                    t_s0 = None
                    for dirn in range(2):
                        tr.wait("pe", t_cw, t_s0)
                        for dc in range(2):
                            for jc in range(c.LC):
                                mm = nc.tensor.matmul(ps[5 + dc][:, 0:256], lhsT=ckw[:, dirn, jc, dc * 128:(dc + 1) * 128], rhs=cvt[:, jc, :],
                                                      start=(jc == 0), stop=(jc == c.LC - 1))
                        pt = tr.mark("pe", mm)
                        cbase = 0 if dirn == 0 else 5
                        tr.wait("dve", pt)
                        for dc in range(2):
                            i_ = nc.vector.tensor_scalar(out=Sst[:, dirn, dc, :], in0=ps[5 + dc][:, 0:256], scalar1=coef[:, cbase + 4, h:h + 1],
                                                         scalar2=0.0, op0=ALU.mult, op1=ALU.add)
                        t_s0 = tr.mark("dve", i_)
                        t_s0 = tr.chain("dve", [
                            (lambda e, r=r: e.scalar_tensor_tensor(out=Sst[:, dirn, :, :], in0=Sg[:, r, dirn, :, :], scalar=coef[:, cbase + r, h:h + 1],
                                                                   in1=Sst[:, dirn, :, :], op0=ALU.mult, op1=ALU.add)) for r in range(4)], waits=[t_s0])
                    tr.wait("pool", t_ld_h, t_head_free, t_dq)
                    for i in range(NCH):
                        for dc in range(2):
                            nc.gpsimd.tensor_tensor(out=qf[:, dc, i * 128:(i + 1) * 128], in0=qT[:, dc, i * 128:(i + 1) * 128], in1=dqf[:], op=ALU.mult)
                            i_ = nc.gpsimd.tensor_tensor(out=qb[:, dc, i * 128:(i + 1) * 128], in0=qT[:, dc, i * 128:(i + 1) * 128], in1=dqb[:],
                                                         op=ALU.mult)
                    t_q = tr.mark("pool", i_)
                    t_sd_ = [t_s0, t_s0]
                    bf_free = [None, None]
                    acc_tok = {}
                    psb_free = None
                    psf_free = None
                    sd_free = None
                    t_ev = None
                    p2 = None
                    for s_ in range(NCH):
                        ib = NCH - 1 - s_
                        i = s_
                        t_bfb = tr.op("act", lambda e: e.activation(out=Sbf[:, 1, :, :], in_=Sst[:, 1, :, :], func=AF.Identity),
                                      waits=[t_sd_[1], bf_free[1]])
                        tr.wait("pe", t_bfb, t_q, psb_free, o_free)
                        for ec in range(2):
                            for dc in range(2):
                                mm = nc.tensor.matmul(ps[ec][:, 0:128], lhsT=Sbf[:, 1, dc, ec * 128:(ec + 1) * 128],
                                                      rhs=qb[:, dc, ib * 128:(ib + 1) * 128], start=(dc == 0), stop=(dc == 1))
                        ptb = tr.mark("pe", mm)
                        bf_free[1] = ptb
                        tr.wait("dve", ptb, acc_tok.get(ib), o_free)
                        for ec in range(2):
                            if ib in acc_tok:
                                i_ = nc.vector.tensor_tensor(out=oacc[:, ec, ib * 128:(ib + 1) * 128], in0=oacc[:, ec, ib * 128:(ib + 1) * 128],
                                                             in1=ps[ec][:, 0:128], op=ALU.add)
                            else:
                                i_ = nc.vector.tensor_copy(out=oacc[:, ec, ib * 128:(ib + 1) * 128], in_=ps[ec][:, 0:128])
                        t_ev = tr.mark("dve", i_)
                        acc_tok[ib] = t_ev
                        psb_free = t_ev
                        if ib > 0:
                            t_sd_[1] = state_update(1, ib, h, False, [tk, t_sd_[1], ptb])
                        t_bff = tr.op("act", lambda e: e.activation(out=Sbf[:, 0, :, :], in_=Sst[:, 0, :, :], func=AF.Identity),
                                      waits=[t_sd_[0], bf_free[0]])
                        tr.wait("pe", t_ld_h, sd_free)
                        for dc in range(2):
                            mm = nc.tensor.matmul(ps[2][:, 0:128], lhsT=kT[:, dc, i * 128:(i + 1) * 128], rhs=qT[:, dc, i * 128:(i + 1) * 128],
                                                  start=(dc == 0), stop=(dc == 1))
                        p1 = tr.mark("pe", mm)
                        t_sd = tr.op("dve", lambda e: e.tensor_tensor(out=sdt[:], in0=ps[2][:, 0:128], in1=dint[:], op=ALU.mult),
                                     waits=[p1, sd_free, t_di])
                        tr.wait("pe", t_sd, t_bff, t_q, psf_free)
                        for ec in range(2):
                            nc.tensor.matmul(ps[3 + ec][:, 0:128], lhsT=vt[:, i, ec * 128:(ec + 1) * 128], rhs=sdt[:], start=True, stop=False)
                            for dc in range(2):
                                mm = nc.tensor.matmul(ps[3 + ec][:, 0:128], lhsT=Sbf[:, 0, dc, ec * 128:(ec + 1) * 128],
                                                      rhs=qf[:, dc, i * 128:(i + 1) * 128], start=False, stop=(dc == 1))
                        p2 = tr.mark("pe", mm)
                        sd_free = p2
                        bf_free[0] = p2
                        tr.wait("dve", p2, acc_tok.get(i), o_free)
                        for ec in range(2):
                            if i in acc_tok:
                                i_ = nc.vector.tensor_tensor(out=oacc[:, ec, i * 128:(i + 1) * 128], in0=oacc[:, ec, i * 128:(i + 1) * 128],
                                                             in1=ps[3 + ec][:, 0:128], op=ALU.add)
                            else:
                                i_ = nc.vector.tensor_copy(out=oacc[:, ec, i * 128:(i + 1) * 128], in_=ps[3 + ec][:, 0:128])
                        t_ev = tr.mark("dve", i_)
                        acc_tok[i] = t_ev
                        psf_free = t_ev
                        if i < NCH - 1:
                            t_sd_[0] = state_update(0, i, h, False, [tk, t_sd_[0], p2])
                    t_gl = tr.dma("sp", gT[:], rgT[h * 256:(h + 1) * 256, :].rearrange("(a p) n -> p a n", p=128), gsem, waits=[p2, t_ev])
                    t_y = None
                    pt = None
                    for q0 in range(0, NT, 512):
                        Tq = min(512, NT - q0)
                        tr.wait("act", t_ev, pt)
                        nc.scalar.activation(out=osq[:, :, 0:Tq], in_=oacc[:, :, q0:q0 + Tq], func=AF.Square)
                        i_ = nc.scalar.activation(out=obf[:, :, 0:Tq], in_=oacc[:, :, q0:q0 + Tq], func=AF.Identity)
                        t_sq = tr.mark("act", i_)
                        tr.wait("pe", t_sq, t_y)
                        for ec in range(2):
                            nc.tensor.matmul(ps[0][:, 0:Tq], lhsT=ones_bf, rhs=obf[:, ec, 0:Tq], start=(ec == 0), stop=(ec == 1))
                        for ec in range(2):
                            mm = nc.tensor.matmul(ps[1][:, 0:Tq], lhsT=ones_bf, rhs=osq[:, ec, 0:Tq], start=(ec == 0), stop=(ec == 1))
                        pt = tr.mark("pe", mm)
                        t_c = tr.chain("dve", [
                            lambda e: e.tensor_scalar(out=stat[:, 0, 0:Tq], in0=ps[0][:, 0:Tq], scalar1=1.0 / 256, scalar2=0.0, op0=ALU.mult, op1=ALU.add),
                            lambda e: e.tensor_tensor(out=stat[:, 1, 0:Tq], in0=stat[:, 0, 0:Tq], in1=stat[:, 0, 0:Tq], op=ALU.mult),
                            lambda e: e.scalar_tensor_tensor(out=stat[:, 1, 0:Tq], in0=ps[1][:, 0:Tq], scalar=1.0 / 256, in1=stat[:, 1, 0:Tq],
                                                             op0=ALU.mult, op1=ALU.subtract),
                            lambda e: e.tensor_scalar(out=stat[:, 1, 0:Tq], in0=stat[:, 1, 0:Tq], scalar1=EPS, scalar2=1.0, op0=ALU.add, op1=ALU.mult),
                        ], waits=[pt, t_y, y_free, t_gl])
                        t_c = tr.op("dve", lambda e: e.reciprocal(out=stat[:, 1, 0:Tq], in_=stat[:, 1, 0:Tq]), waits=[tr.op("act", lambda e: e.activation(out=stat[:, 1, 0:Tq], in_=stat[:, 1, 0:Tq], func=AF.Sqrt), waits=[t_c])])
                        for ec in range(2):
                            t_c = tr.chain("dve", [
                                lambda e, ec=ec: e.tensor_tensor(out=stat[:, 2, 0:Tq], in0=oacc[:, ec, q0:q0 + Tq], in1=stat[:, 0, 0:Tq], op=ALU.subtract),
                                lambda e, ec=ec: e.tensor_tensor(out=stat[:, 2, 0:Tq], in0=stat[:, 2, 0:Tq], in1=stat[:, 1, 0:Tq], op=ALU.mult),
                                lambda e, ec=ec: e.scalar_tensor_tensor(out=ymix[:, ec, q0:q0 + Tq], in0=stat[:, 2, 0:Tq],
                                                                        scalar=gnw[:, 2 * h + ec:2 * h + ec + 1], in1=gT[:, ec, q0:q0 + Tq],
                                                                        op0=ALU.mult, op1=ALU.mult),
                            ], waits=[t_c])
                        t_y = t_c
                    o_free = t_y
                    y_free = tr.dma("sp", mixT[h * 256:(h + 1) * 256, :].rearrange("(a p) n -> p a n", p=128), ymix[:], ysem, waits=[t_y])
                    t_head_free = t_y
            tr.barrier(mk)

    ret_phase(1)
    _ck(4)
    DHH = c.DHH
    NKC = c.NKC
    QT = 256
    with ExitStack() as es:
        lam_t = es.enter_context(nc.sbuf_tensor("lam_t", [128, 512], F32))
        lam_p = es.enter_context(nc.sbuf_tensor("lam_p", [128, 256], F32))
        lam_s = es.enter_context(nc.sbuf_tensor("lam_s", [128, 4], F32))
        subw = es.enter_context(nc.sbuf_tensor("subw_sb", [128, 256], F32))
        NB = 1
        KTs = [es.enter_context(nc.sbuf_tensor("KTs%d" % k, [128, 2, NKC * 128], BF16)) for k in range(NB)]
        Vs = [es.enter_context(nc.sbuf_tensor("Vs%d" % k, [128, NKC, 257], BF16)) for k in range(NB)]
        Qs = [es.enter_context(nc.sbuf_tensor("Qs%d" % k, [128, 2, NT], BF16)) for k in range(NB)]
        PT = [es.enter_context(nc.sbuf_tensor("PT%d" % k, [128, 2, QT], BF16)) for k in range(3)]
        o1 = es.enter_context(nc.sbuf_tensor("o1", [128, 2, 256], F32))
        comb = es.enter_context(nc.sbuf_tensor("comb", [128, 2, 256], F32))
        junk = es.enter_context(nc.sbuf_tensor("junk", [128, 2, 256], F32))
        rr = es.enter_context(nc.sbuf_tensor("rr", [128, 2, 4], F32))
        ybf = es.enter_context(nc.sbuf_tensor("ybf", [128, 2, 256], BF16))
        ydT = es.enter_context(nc.sbuf_tensor("ydT", [128, 2, QT], BF16))
        psO = [es.enter_context(nc.psum_tensor("psO%d" % k, [128, 512], F32)) for k in range(4)]
        psS = [es.enter_context(nc.psum_tensor("psS%d" % k, [128, 512], F32)) for k in range(3)]
        pstd = es.enter_context(nc.psum_tensor("pstd", [128, 1024], BF16))

        tr.dma("sp", lam_t[:], I["lamp"].partition_broadcast(128)[:, 0, :], ld)
        tr.dma("sp", subw[:], I["subw"].partition_broadcast(128)[:, 0, :], ld)
        t_l = tr.tot(ld)
        tr.wait("dve", t_l)
        for k in range(NB):
            nc.vector.memset(Vs[k][:, :, 256:257], 1.0)
        lt4 = lam_t[:].rearrange("p (a t b) -> p a t b", a=2, t=2)
        t_a = tr.chain("dve", [
            lambda e: e.tensor_tensor(out=lam_p[:].rearrange("p (a b) -> p a b", a=2), in0=lt4[:, :, 0, :], in1=lt4[:, :, 1, :], op=ALU.mult),
            lambda e: e.tensor_reduce(out=lam_s[:, 0:2], in_=lam_p[:].rearrange("p (a b) -> p a b", a=2), axis=AX.X, op=ALU.add),
        ], waits=[t_l])
        t_b = tr.op("act", lambda e: e.activation(out=lam_s[:, 2:4], in_=lam_s[:, 0:2], func=AF.Exp), waits=[t_a])
        t_lam = tr.chain("dve", [
            lambda e: e.tensor_tensor(out=lam_s[:, 0:1], in0=lam_s[:, 3:4], in1=lam_s[:, 2:3], op=ALU.subtract),
            lambda e: e.tensor_scalar(out=lam_s[:, 0:1], in0=lam_s[:, 0:1], scalar1=-0.2, scalar2=1.0, op0=ALU.add, op1=ALU.mult),
            lambda e: e.tensor_scalar(out=subw[:], in0=subw[:], scalar1=0.8, scalar2=0.0, op0=ALU.mult, op1=ALU.add),
        ], waits=[t_b])
        nlam = lam_s[:, 0:1]

        dl = [tr.dsem("dl0"), tr.dsem("dl1")]
        ydsem = tr.dsem("ydsem")
        def load_dhead(h, b, waits):
            tr.wait("sp", t_kv[h], *waits)
            r0 = h * 256
            for m in range(2):
                tr.dma("sp", KTs[b][:, m, 0:L], cdkT[r0 + m * 128:r0 + (m + 1) * 128, :], dl[b])
                tr.dma("sp", KTs[b][:, m, L:L + 4 * NT].rearrange("p (r n) -> p r n", r=4),
                       dk_all[h].rearrange("(r f) n -> f r n", r=4)[m * 128:(m + 1) * 128, :, :], dl[b])
                tr.dma("sp", Qs[b][:, m, :], dqT[r0 + m * 128:r0 + (m + 1) * 128, :], dl[b])
            tr.dma("sp", Vs[b][:, 0:c.LC, 0:256], cdv[:, r0:r0 + 256].rearrange("(a p) n -> p a n", p=128), dl[b])
            tr.dma("sp", Vs[b][:, c.LC:NKC, 0:256], dv_all[h].rearrange("(a p) n -> p a n", p=128), dl[b])
            return (dl[b].h, dl[b].v, "d_%d" % id(dl[b]))

        head_free = [None, None]
        t_ldh = load_dhead(0, 0, [])
        pt_free = [None, None, None]
        ps_free = [None, None, None]
        st8 = {"O_free": None, "y_st": None, "yd_free": None, "p3": None, "git": 0}
        nq = QT // 128
        for h in range(DHH):
            b = 0
            if h > 0:
                t_ldh = load_dhead(h, 0, [head_free[0]])
            KT, V, Q = KTs[b], Vs[b], Qs[b]
            its = [(q0, kc) for q0 in range(0, NT, QT) for kc in range(NKC)]
            nit = len(its)
            base = st8["git"]
            st8["git"] += nit
            t_es = {}
            pending = []

            def stageA(i):
                q0, kc = its[i]
                sb = (base + i) % 3
                tr.wait("pe", t_ldh, ps_free[sb], t_lam)
                for m in range(2):
                    mm = nc.tensor.matmul(psS[sb][:, m * QT:(m + 1) * QT], lhsT=KT[:, m, kc * 128:(kc + 1) * 128], rhs=Q[:, m, q0:q0 + QT],
                                          start=True, stop=True)
                p1 = tr.mark("pe", mm)
                t_e = tr.op("act", lambda e: e.activation(out=PT[sb][:].rearrange("p a b -> p (a b)"), in_=psS[sb][:, 0:2 * QT],
                                                          func=AF.Exp), waits=[p1, pt_free[sb]])
                ps_free[sb] = t_e
                t_es[i] = t_e

            def epilogue(q0, p2):
                toks = [[p2, st8["p3"], t_lam] for _ in range(nq)]
                steps = [
                    lambda e, qs: e.reciprocal(out=rr[:, qs, 0:1], in_=psO[qs][:, 256:257]),
                    lambda e, qs: e.reciprocal(out=rr[:, qs, 1:2], in_=psO[nq + qs][:, 256:257]),
                    lambda e, qs: e.tensor_tensor(out=rr[:, qs, 1:2], in0=rr[:, qs, 1:2], in1=nlam, op=ALU.mult),
                    lambda e, qs: e.tensor_scalar(out=o1[:, qs, :], in0=psO[qs][:, 0:256], scalar1=rr[:, qs, 0:1], scalar2=0.0,
                                                  op0=ALU.mult, op1=ALU.add),
                    lambda e, qs: e.scalar_tensor_tensor(out=comb[:, qs, :], in0=psO[nq + qs][:, 0:256], scalar=rr[:, qs, 1:2], in1=o1[:, qs, :],
                                                         op0=ALU.mult, op1=ALU.add),
                    lambda e, qs: e.tensor_tensor(out=junk[:, qs, :], in0=comb[:, qs, :], in1=comb[:, qs, :], op=ALU.mult),
                    lambda e, qs: e.tensor_reduce(out=rr[:, qs, 2:3], in_=junk[:, qs, :], axis=AX.X, op=ALU.add),
                    lambda e, qs: e.tensor_scalar(out=rr[:, qs, 2:3], in0=rr[:, qs, 2:3], scalar1=1.0 / 256, scalar2=EPS, op0=ALU.mult, op1=ALU.add),
                ]
                for k, fn in enumerate(steps):
                    for qs in range(nq):
                        toks[qs] = tr.op("dve", lambda e, fn=fn, qs=qs: fn(e, qs), waits=toks[qs] if isinstance(toks[qs], list) else [toks[qs]])
                    if k == 4:
                        st8["O_free"] = toks[nq - 1]
                for qs in range(nq):
                    t_ = tr.op("act", lambda e, qs=qs: e.activation(out=rr[:, qs, 3:4], in_=rr[:, qs, 2:3], func=AF.Ln), waits=[toks[qs]])
                    toks[qs] = tr.op("act", lambda e, qs=qs: e.activation(out=rr[:, qs, 2:3], in_=rr[:, qs, 3:4], func=AF.Exp, scale=-0.5), waits=[t_])
                for qs in range(nq):
                    toks[qs] = tr.op("dve", lambda e, qs=qs: e.scalar_tensor_tensor(out=ybf[:, qs, :], in0=comb[:, qs, :], scalar=rr[:, qs, 2:3],
                                                                                   in1=subw[:], op0=ALU.mult, op1=ALU.mult), waits=[toks[qs]])
                t_yb = list(toks)

                def transposes():
                    for qs in range(nq):
                        tr.wait("pe", t_yb[qs], st8["yd_free"])
                        for ec in range(2):
                            mm = nc.tensor.transpose(pstd[:, ec * 128:(ec + 1) * 128], ybf[:, qs, ec * 128:(ec + 1) * 128], ident_bf)
                        p3 = tr.mark("pe", mm)
                        st8["p3"] = p3
                        tr.wait("act", p3, st8["y_st"])
                        for ec in range(2):
                            i_ = nc.scalar.activation(out=ydT[:, ec, qs * 128:(qs + 1) * 128], in_=pstd[:, ec * 128:(ec + 1) * 128],
                                                      func=AF.Identity)
                        st8["yd_free"] = tr.mark("act", i_)
                    st8["y_st"] = tr.dma("sp", mixT[c.RW + h * 256:c.RW + (h + 1) * 256, q0:q0 + QT].rearrange("(a p) n -> p a n", p=128),
                                         ydT[:], ydsem, waits=[st8["yd_free"]])
                return transposes

            stageA(0)
            if nit > 1:
                stageA(1)
            t_last_pv = None
            for i in range(nit):
                if i + 2 < nit:
                    stageA(i + 2)
                q0, kc = its[i]
                sb = (base + i) % 3
                tr.wait("pe", t_es[i])
                if kc == 0:
                    tr.wait("pe", st8["O_free"])
                for m in range(2):
                    for qs in range(nq):
                        mm = nc.tensor.matmul(psO[m * nq + qs][:, 0:257], lhsT=PT[sb][:, m, qs * 128:(qs + 1) * 128], rhs=V[:, kc, :],
                                              start=(kc == 0), stop=(kc == NKC - 1))
                p2 = tr.mark("pe", mm)
                pt_free[sb] = p2
                t_last_pv = p2
                if kc == NKC - 1:
                    pending.append((i + 3, epilogue(q0, p2)))
                while pending and (pending[0][0] <= i or i == nit - 1):
                    pending.pop(0)[1]()
            head_free[b] = t_last_pv
        tr.barrier(mk)

    ret_phase(2)
    _ck(5)
    with ExitStack() as es:
        mT = es.enter_context(nc.sbuf_tensor("mTo", [128, c.MC, TT], BF16))
        rbufs = [es.enter_context(nc.sbuf_tensor("rbo%d" % k, [128, NDJ, TT], F32)) for k in range(2)]
        psd = [es.enter_context(nc.psum_tensor("pso%d" % k, [128, 512], F32)) for k in range(NDJ)]
        rb_state = make_rb_state(rbufs, "rbo")
        mfree = None
        msem = tr.dsem("msem")
        for (t0, T, stream) in tiles_lat:
            t_m = tr.dma("sp", mT[:, :, 0:T], mixT.rearrange("(a p) n -> p a n", p=128)[:, :, t0:t0 + T], msem, waits=[mfree])
            mfree = down_proj(mT, c.MC, T, x1T, x2T, MV(1, 2, 0), t0, psd, rb_state, [t_m])
        tr.barrier(mk)

    _ck(6)
    ffn_phase(1, x2T, x3T, tiles_lat, 2)
    _ck(7)

    with ExitStack() as es:
        xbuf = es.enter_context(nc.sbuf_tensor("xbf", [128, KC, TT], F32))
        sqb = es.enter_context(nc.sbuf_tensor("sqf", [128, KC, TT], BF16))
        rstd = es.enter_context(nc.sbuf_tensor("rstdf", [128, TT], F32))
        finw = es.enter_context(nc.sbuf_tensor("finw", [128, KC], F32))
        pss = es.enter_context(nc.psum_tensor("psfin", [128, 512], F32))
        t_fw = tr.dma("sp", finw[:], I["finT"], ld)
        fsem = tr.dsem("fsem")
        xfree = None
        for (t0, T, stream) in tiles_lat:
            xv = xbuf[:, :, 0:T]
            t_x = tr.dma("sp", xv, x3T.rearrange("(kc p) n -> p kc n", p=128)[:, :, t0:t0 + T], xsems[0], waits=[xfree])
            t_sq = tr.op("act", lambda e: e.activation(out=sqb[:, :, 0:T], in_=xv, func=AF.Square), waits=[t_x, xfree])
            tr.wait("pe", t_sq)
            for kc in range(KC):
                mm = nc.tensor.matmul(pss[:, 0:T], lhsT=ones_bf, rhs=sqb[:, kc, 0:T], start=(kc == 0), stop=(kc == KC - 1))
            t_ss = tr.mark("pe", mm)
            t_r = tr.op("dve", lambda e: e.tensor_scalar(out=rstd[:, 0:T], in0=pss[:, 0:T], scalar1=1.0 / D, scalar2=EPS,
                                                         op0=ALU.mult, op1=ALU.add), waits=[t_ss, t_x, t_fw])
            t_r = tr.op("dve", lambda e: e.reciprocal(out=rstd[:, 0:T], in_=rstd[:, 0:T]), waits=[tr.op("act", lambda e: e.activation(out=rstd[:, 0:T], in_=rstd[:, 0:T], func=AF.Sqrt), waits=[t_r])])
            tr.wait("dve", t_r)
            for kc in range(KC):
                i_ = nc.vector.scalar_tensor_tensor(out=xv[:, kc, :], in0=xv[:, kc, :], scalar=finw[:, kc:kc + 1], in1=rstd[:, 0:T],
                                                    op0=ALU.mult, op1=ALU.mult)
            t_y = tr.mark("dve", i_)
            xfree = tr.dma("sp", yT.rearrange("(kc p) n -> p kc n", p=128)[:, :, t0:t0 + T], xv, fsem, waits=[t_y])
        tr.barrier(mk)
    assert ws.cur == len(ws.tiles), (ws.cur, len(ws.tiles))


def _rope_tables(c: Cfg, seg, head_dim):
    t = np.arange(seg * c.NT, (seg + 1) * c.NT)
    rows = (t // c.GW).astype(np.float32)
    cols = (t % c.GW).astype(np.float32)
    n_freq = head_dim // 4
    inv_freq = (np.float32(10000.0) ** (-np.arange(n_freq, dtype=np.float32) / np.float32(n_freq))).astype(np.float32)
    ang_r = rows[:, None] * inv_freq
    ang_c = cols[:, None] * inv_freq
    ang = np.concatenate([ang_r, ang_r, ang_c, ang_c], axis=-1).astype(np.float32)
    return np.cos(ang).astype(np.float32), np.sin(ang).astype(np.float32)


def _consts(c: Cfg):
    m = np.arange(128)[:, None]
    n = np.arange(128)[None, :]
    ident = np.eye(128, dtype=np.float32)
    P = np.zeros((128, 128), np.float32)
    for i in range(64):
        P[i, i + 64] = -1.0
        P[i + 64, i] = 1.0
    Pd = np.zeros((128, 128), np.float32)
    for base in (0, 64):
        for i in range(32):
            Pd[base + i, base + i + 32] = -1.0
            Pd[base + i + 32, base + i] = 1.0
    relf = np.where(m <= n, n - m, 0).astype(np.float32)
    maskf = (m <= n).astype(np.float32)
    relb = np.where(m > n, m - n, 0).astype(np.float32)
    maskb = (m > n).astype(np.float32)
    posqf = np.broadcast_to(n + 1, (128, 128)).astype(np.float32)
    posqb = np.broadcast_to(128 - n, (128, 128)).astype(np.float32)
    pkf = (127 - m).astype(np.float32)
    pkb = m.astype(np.float32)
    j = np.arange(c.LC)[None, :] * 128 + m
    ctxf = (c.L - 1 - j).astype(np.float32)
    ctxb = j.astype(np.float32)
    ii = np.arange(c.NCH)[None, :] * 128 + m
    posff = (c.NT - 1 - ii).astype(np.float32)
    posfb = ii.astype(np.float32)
    return np.ascontiguousarray(np.concatenate([ident, P.T, Pd.T, relf, maskf, relb, maskb, posqf, posqb, pkf, pkb, ctxf, ctxb, posff, posfb], axis=1))


def fm(v, nchunks):
    return np.ascontiguousarray(np.asarray(v, np.float32).reshape(nchunks, 128).T)


def make_in_maps(c: Cfg, inp):
    KC = c.KC
    f = lambda a: np.ascontiguousarray(np.asarray(a, dtype=np.float32))
    x, cc, ctx, c_ctx = f(inp["x"]), f(inp["c"]), f(inp["ctx"]), f(inp["c_ctx"])
    shared = {
        "ada_w": f(inp["ada_w"][0]),
        "ada_bT": fm(inp["ada_b"][0], 9 * KC),
        "nwT": fm(np.asarray(inp["norm_w"][0]).reshape(-1), 3 * KC),
        "finT": fm(inp["final_norm_w"], KC),
        "w_in": f(inp["w_in"][0]), "w_out": f(inp["w_out"][0]),
        "rdec": f(np.asarray(inp["ret_decay"][0]).reshape(1, -1)),
        "lamp": f(np.asarray(inp["diff_lambda"][0]).reshape(1, -1)),
        "subw": f(np.asarray(inp["diff_subln_w"][0]).reshape(1, -1)),
        "gnwT": fm(inp["ret_gn_w"][0], c.RW // 128),
        "consts": _consts(c),
    }
    for i in range(2):
        shared["wg%d" % i] = f(inp["ffn_gate"][0, i])
        shared["wu%d" % i] = f(inp["ffn_up"][0, i])
        shared["wd%d" % i] = f(inp["ffn_down"][0, i])
    maps = []
    NT = c.NT
    for j in range(8):
        b, seg = j // 4, j % 4
        m = dict(shared)
        xs = x[b, seg * NT:(seg + 1) * NT, :]
        m["xin"] = np.ascontiguousarray(np.concatenate([xs, ctx[b]], axis=0).T)
        m["cT"] = np.ascontiguousarray(np.stack([fm(cc[b], KC), fm(c_ctx, KC)], axis=-1).reshape(128, KC * 2))
        cr, sr = _rope_tables(c, seg, 256)
        cd, sd = _rope_tables(c, seg, 128)
        m["cosr"] = np.ascontiguousarray(cr.T.reshape(2, 128, NT).transpose(1, 0, 2).reshape(128, 2 * NT))
        m["sinr"] = np.ascontiguousarray(sr.T.reshape(2, 128, NT).transpose(1, 0, 2).reshape(128, 2 * NT))
        m["cosd"] = np.ascontiguousarray(cd.T)
        m["sind"] = np.ascontiguousarray(sd.T)
        meta = np.zeros((1, 20), np.float32)
        for r in range(4):
            if r < seg:
                meta[0, r] = (seg - 1 - r) * NT
                meta[0, 4 + r] = 1.0
            if r > seg:
                meta[0, 9 + r] = (r - seg - 1) * NT
                meta[0, 13 + r] = 1.0
        meta[0, 8] = seg * NT
        meta[0, 17] = (3 - seg) * NT
        m["segmeta"] = meta
        maps.append(m)
    return maps


_NC_CACHE = {}


def run_cfg(c: Cfg, inp, stop=None, dbg=False, raw=False):
    key = (c.D, c.S, c.L, c.DFF, c.RH, c.DHH, c.TT, stop, dbg)
    if key not in _NC_CACHE:
        _NC_CACHE[key] = build(c, stop, dbg)
    nc = _NC_CACHE[key]
    maps = make_in_maps(c, inp)
    res = run_bass_kernel_spmd(nc, maps, core_ids=list(range(8)))
    if raw:
        return res.results
    B = 2
    out = np.empty((B, c.S, c.D), np.float32)
    for j in range(8):
        b, seg = j // 4, j % 4
        out[b, seg * c.NT:(seg + 1) * c.NT, :] = res.results[j]["yT"].T
    return out


def kernel(**inputs):
    return run_cfg(FULL, inputs)
```

```python
import numpy as np
from contextlib import ExitStack
import concourse.bass as bass
import concourse.mybir as mybir
from concourse.bass_utils import run_bass_kernel_spmd

F32 = mybir.dt.float32
BF16 = mybir.dt.bfloat16
AF = mybir.ActivationFunctionType
ALU = mybir.AluOpType
AX = mybir.AxisListType
EPS = 1e-6


class Cfg:
    def __init__(s, D, S, L, DFF, RH, DHH, TT, GW=64):
        s.D, s.S, s.L, s.DFF, s.RH, s.DHH, s.TT, s.GW = D, S, L, DFF, RH, DHH, TT, GW
        s.NT = S // 4
        s.KC = D // 128
        s.FC = DFF // 128
        s.RW = RH * 256
        s.DW = DHH * 256
        s.MIX = s.RW + s.DW
        s.MC = s.MIX // 128
        s.INC = 4 * s.RW + 3 * s.DW
        s.NTA = s.NT + L
        s.NCH = s.NT // 128
        s.LC = L // 128
        s.NKC = (L + 4 * s.NT) // 128
        s.NCONST = 128 * 9 + 2 + 2 * s.LC + 2 * s.NCH
        s.WT = 256 if D >= 256 else 128


FULL = Cfg(D=4096, S=8192, L=256, DFF=11008, RH=8, DHH=8, TT=512)


class DSem:
    def __init__(s, nc, name):
        s.h = nc.alloc_semaphore(name)
        s.v = 0


class Tracker:
    def __init__(s, nc):
        s.nc = nc
        s.E = {"pe": nc.tensor, "act": nc.scalar, "dve": nc.vector, "pool": nc.gpsimd, "sp": nc.sync}
        s.esem = {k: nc.alloc_semaphore("e_" + k) for k in s.E}
        s.ecnt = {k: 0 for k in s.E}
        s.seen = {}
        s.dsems = []
        s.semname = {}

    def dsem(s, name):
        d = DSem(s.nc, name)
        s.dsems.append(d)
        return d

    def wait(s, e, *toks):
        for tok in toks:
            if tok is None:
                continue
            if isinstance(tok, list):
                s.wait(e, *tok)
                continue
            sem, val, key = tok
            k = (e, key)
            if s.seen.get(k, 0) >= val:
                continue
            s.E[e].wait_ge(sem, val)
            s.seen[k] = val

    def mark(s, e, inst):
        s.ecnt[e] += 1
        inst.then_inc(s.esem[e], 1)
        return (s.esem[e], s.ecnt[e], "e_" + e)

    def op(s, e, inst_fn, waits=(), mark=True):
        s.wait(e, *waits)
        inst = inst_fn(s.E[e])
        if mark:
            return s.mark(e, inst)
        return None

    def dma(s, e, out, in_, sem, waits=()):
        s.wait(e, *waits)
        s.E[e].dma_start(out=out, in_=in_).then_inc(sem.h, 16)
        sem.v += 16
        return (sem.h, sem.v, "d_%d" % id(sem))

    def chain(s, e, fns, waits=()):
        tok = None
        for k, fn in enumerate(fns):
            tok = s.op(e, fn, waits=(list(waits) if k == 0 else [tok]))
        return tok

    def tot(s, sem):
        return (sem.h, sem.v, "d_%d" % id(sem))

    def barrier(s, markers):
        toks = []
        toks.append(s.op("act", lambda en: en.copy(out=markers["act"][:, 0:1], in_=markers["act"][:, 1:2])))
        for e in ("dve", "pool"):
            toks.append(s.op(e, lambda en, e=e: en.memset(markers[e][:, 0:1], 0.0)))
        toks.append((s.esem["pe"], s.ecnt["pe"], "e_pe"))
        for d in s.dsems:
            if d.v > 0:
                toks.append((d.h, d.v, "d_%d" % id(d)))
        for e in s.E:
            s.wait(e, *toks)


class WStream:
    def __init__(s, tr, nslots, slot_elems):
        s.tr = tr
        s.n = nslots
        s.slots = [tr.nc.alloc_sbuf_tensor("wslot%d" % i, [128, slot_elems], BF16) for i in range(nslots)]
        s.sems = [tr.dsem("wsem%d" % i) for i in range(nslots)]
        s.tiles = []
        s.issued = 0
        s.free_tok = {}
        s.load_tok = {}
        s.cur = 0
        s.slot_elems = slot_elems

    def add(s, dram_ap, a, b, tag):
        assert a * b <= s.slot_elems, (a, b)
        s.tiles.append((dram_ap, a, b, tag))

    def view(s, i):
        _, a, b, _ = s.tiles[i]
        return s.slots[i % s.n][:, 0:a * b].rearrange("p (a b) -> p a b", a=a)

    def next(s, tag):
        i = s.cur
        s.cur += 1
        assert s.tiles[i][3] == tag, (i, s.tiles[i][3], tag)
        upto = min(len(s.tiles), i + s.n)
        while s.issued < upto and (s.issued < s.n or (s.issued - s.n) in s.free_tok):
            j = s.issued
            waits = [s.free_tok[j - s.n]] if j >= s.n else []
            s.load_tok[j] = s.tr.dma("pool", s.view(j), s.tiles[j][0], s.sems[j % s.n], waits)
            s.issued += 1
        assert i in s.load_tok, ("weight tile not issued (too many tiles held)", i)
        return i, s.view(i), s.load_tok[i]

    def release(s, i, tok):
        s.free_tok[i] = tok


def wtile_cols(w, c0, ncols):
    return w.rearrange("(kc p) n -> p kc n", p=128)[:, :, c0:c0 + ncols]


def wtile_rows(w, r0, nrows, c0, ncols):
    return w[r0:r0 + nrows, c0:c0 + ncols].rearrange("(fc p) n -> p fc n", p=128)


class _Stop(Exception):
    pass


def build(c, stop=None, dbg=False):
    nc = bass.Bass("TRN2", target_bir_lowering=False)
    try:
        _build(c, nc, stop, dbg)
    except _Stop:
        pass
    return nc


def _build(c, nc, stop, dbg):
    tr = Tracker(nc)

    def _ck(k):
        if stop == k:
            raise _Stop()

    KC, FC, TT, NT, L, D = c.KC, c.FC, c.TT, c.NT, c.L, c.D
    WT = c.WT

    def din(name, shape, dt=F32):
        return nc.dram_tensor(name, list(shape), dt, kind="ExternalInput").ap()

    def dsc(name, shape, dt):
        if dbg and name in ("x1T", "x2T", "x3T", "rqT", "rkT", "rgT", "rv", "dqT", "cdkT", "cdv", "ckd", "cvd", "mixT"):
            return nc.dram_tensor(name, list(shape), dt, kind="ExternalOutput").ap()
        return nc.dram_tensor(name, list(shape), dt).ap()

    I = {}
    I["xin"] = din("xin", [D, c.NTA])
    I["cT"] = din("cT", [128, KC * 2])
    assert (9 * KC) % 4 == 0
    QC = 9 * KC // 4
    AQ = 256 if (QC * 128) % 256 == 0 else 128
    I["ada_wq"] = din("ada_wq", [D, QC * 128])
    I["ada_bqT"] = din("ada_bqT", [128, QC])
    mod_loc = dsc("mod_loc", [128, QC * 2], F32)
    mod_all = dsc("mod_all", [4 * 128, QC * 2], F32)
    I["nwT"] = din("nwT", [128, 3 * KC])
    I["finT"] = din("finT", [128, KC])
    for i in range(2):
        I["wg%d" % i] = din("wg%d" % i, [D, c.DFF])
        I["wu%d" % i] = din("wu%d" % i, [D, c.DFF])
        I["wd%d" % i] = din("wd%d" % i, [c.DFF, D])
    I["w_in"] = din("w_in", [D, c.INC])
    I["w_out"] = din("w_out", [c.MIX, D])
    I["rdec"] = din("rdec", [1, 2 * c.RH])
    I["lamp"] = din("lamp", [1, 512])
    I["subw"] = din("subw", [1, 256])
    I["gnwT"] = din("gnwT", [128, c.RW // 128])
    I["segmeta"] = din("segmeta", [1, 20])
    I["cosr"] = din("cosr", [128, 2 * NT])
    I["sinr"] = din("sinr", [128, 2 * NT])
    I["cosd"] = din("cosd", [128, NT])
    I["sind"] = din("sind", [128, NT])
    I["consts"] = din("consts", [128, c.NCONST])
    yT = nc.dram_tensor("yT", [D, NT], F32, kind="ExternalOutput").ap()

    x1T = dsc("x1T", [D, c.NTA], F32)
    x2T = dsc("x2T", [D, NT], F32)
    x3T = dsc("x3T", [D, NT], F32)
    rqT = dsc("rqT", [c.RW, NT], BF16)
    rkT = dsc("rkT", [c.RW, NT], BF16)
    rgT = dsc("rgT", [c.RW, NT], BF16)
    rv = dsc("rv", [NT, c.RW], BF16)
    dqT = dsc("dqT", [c.DW, NT], BF16)
    dk_loc = [dsc("dk_loc%d" % h, [256, NT], BF16) for h in range(c.DHH)]
    dv_loc = [dsc("dv_loc%d" % h, [NT, 256], BF16) for h in range(c.DHH)]
    dk_all = [dsc("dk_all%d" % h, [4 * 256, NT], BF16) for h in range(c.DHH)]
    dv_all = [dsc("dv_all%d" % h, [4 * NT, 256], BF16) for h in range(c.DHH)]
    cdkT = dsc("cdkT", [c.DW, L], BF16)
    cdv = dsc("cdv", [L, c.DW], BF16)
    ckd = dsc("ckd", [L, c.RW], BF16)
    cvd = dsc("cvd", [L, c.RW], BF16)
    st_loc = [dsc("st_loc%d" % h, [512, 256], F32) for h in range(c.RH)]
    st_all = [dsc("st_all%d" % h, [4 * 512, 256], F32) for h in range(c.RH)]
    mixT = dsc("mixT", [c.MIX, NT], BF16)

    DBW = 512 if D >= 512 else D
    NDJ = DBW // 128
    FSB = 16
    ws = WStream(tr, 4, max(KC * WT, min(FSB, max(FC, c.MC)) * DBW))
    cmat = nc.alloc_sbuf_tensor("cmat", [128, 4, 128], BF16)
    ones_bf, ident_bf, permr_bf, permd_bf = cmat[:, 0, :], cmat[:, 1, :], cmat[:, 2, :], cmat[:, 3, :]
    modv = nc.alloc_sbuf_tensor("modv", [128, 18, KC], F32)
    mk = {e: nc.alloc_sbuf_tensor("mk_" + e, [128, 2], F32) for e in ("act", "dve", "pool")}
    ld = tr.dsem("ld_misc")
    st_sem = tr.dsem("st_sem")

    def MV(sub, kind, stream):
        return modv[:, sub * 6 + kind * 2 + stream, :]

    tiles_lat = [(t0, min(TT, NT - t0), 0) for t0 in range(0, NT, TT)]
    tiles_pre = tiles_lat + [(NT, L, 1)]

    def plan_down(w, nin):
        for db in range(D // DBW):
            for f0 in range(0, nin, FSB):
                nf = min(FSB, nin - f0)
                ws.add(wtile_rows(w, f0 * 128, nf * 128, db * DBW, DBW), nf, DBW, "dn")

    def plan_ffn(i, tiles):
        pg = 0
        for _ in tiles:
            for p, f0 in enumerate(range(0, c.DFF, WT)):
                ws.add(wtile_cols(I["wg%d" % i], f0, WT), KC, WT, "g")
                ws.add(wtile_cols(I["wu%d" % i], f0, WT), KC, WT, "u")
                if i == 0 and ada_plan["g"] < NADA and ada_after_pair(pg):
                    plan_ada_tile()
                pg += 1
            plan_down(I["wd%d" % i], FC)

    RW, DW = c.RW, c.DW
    segs = [("rq", 0, RW), ("rk", RW, RW), ("rv", 2 * RW, RW), ("rg", 3 * RW, RW),
            ("dq", 4 * RW, DW), ("dk", 4 * RW + DW, DW), ("dv", 4 * RW + 2 * DW, DW)]

    def plan_proj():
        for (t0, T, stream) in tiles_pre:
            for (nm, c0, w) in segs:
                if stream == 1 and nm in ("rq", "rg", "dq"):
                    continue
                for cc in range(0, w, WT):
                    ws.add(wtile_cols(I["w_in"], c0 + cc, WT), KC, WT, "in")

    AW, NKH, KH = WT, 1, KC
    NADA = 0
    NADA0 = 0
    ada_plan = {"g": 0}

    def plan_ada_tile():
        g = ada_plan["g"]
        ada_plan["g"] += 1
        for kh in range(NKH):
            ws.add(wtile_rows(I["ada_w"], kh * KH * 128, KH * 128, g * AW, AW), KH, AW, "ada")

    NPAIR0 = len(tiles_pre) * (c.DFF // WT)

    def ada_after_pair(pg):
        return False

    for g in range(QC * 128 // AQ):
        ws.add(wtile_cols(I["ada_wq"], g * AQ, AQ), KC, AQ, "adaq")
    plan_ffn(0, tiles_pre)
    plan_proj()
    while ada_plan["g"] < NADA:
        plan_ada_tile()
    for _ in tiles_lat:
        plan_down(I["w_out"], c.MC)
    plan_ffn(1, tiles_lat)

    sc = nc.alloc_sbuf_tensor("sc", [128, KC, 2], BF16)
    nw = nc.alloc_sbuf_tensor("nw", [128, 3 * KC], F32)
    ada_state = {"g": 0}

    def ada_tile(*a_, **k_):
        raise AssertionError("in-stream ada tiles are disabled")

    groups = [[0, 1, 2, 3], [4, 5, 6, 7]]
    with ExitStack() as es:
        cst = es.enter_context(nc.sbuf_tensor("cst0", [128, 4 * 128], F32))
        cTt = es.enter_context(nc.sbuf_tensor("cTt", [128, KC * 2], F32))
        adabq = es.enter_context(nc.sbuf_tensor("adabq", [128, QC], F32))
        modq = es.enter_context(nc.sbuf_tensor("modq", [128, QC, 2], F32))
        mod4 = es.enter_context(nc.sbuf_tensor("mod4", [128, 4, QC * 2], F32))
        psa = [es.enter_context(nc.psum_tensor("psa%d" % k, [128, 512], F32)) for k in range(2)]

        t_c = tr.dma("sp", cTt[:], I["cT"], ld)
        tr.dma("sp", adabq[:], I["ada_bqT"], ld)
        tr.dma("sp", nw[:], I["nwT"], ld)
        tr.dma("sp", cst[:, 0:384], I["consts"][:, 0:384], ld)
        t_ld = tr.tot(ld)
        tr.op("dve", lambda e: e.memset(ones_bf, 1.0), mark=False)
        t_cm = tr.op("dve", lambda e: e.tensor_copy(out=cmat[:, 1:4, :], in_=cst[:, 0:384].rearrange("p (a b) -> p a b", a=3)),
                     waits=[t_ld])
        t_sc = tr.op("act", lambda e: e.activation(out=sc[:].rearrange("p a b -> p (a b)"), in_=cTt[:], func=AF.Silu),
                     waits=[t_ld])
        njq = AQ // 128
        ev_tok = [None, None]
        t_mod = None
        for g in range(QC * 128 // AQ):
            wi, wv, wtok = ws.next("adaq")
            ps = psa[g % 2]
            tr.wait("pe", wtok, t_sc, ev_tok[g % 2])
            for j in range(njq):
                for kc in range(KC):
                    mm = nc.tensor.matmul(ps[:, j * 2:(j + 1) * 2], lhsT=wv[:, kc, j * 128:(j + 1) * 128], rhs=sc[:, kc, :],
                                          start=(kc == 0), stop=(kc == KC - 1))
            t_pe = tr.mark("pe", mm)
            ws.release(wi, t_pe)
            ch0 = g * njq
            for col in range(2):
                t_mod = tr.op("dve", lambda e, col=col: e.tensor_tensor(
                    out=modq[:, ch0:ch0 + njq, col], in0=ps[:, 0:2 * njq].rearrange("p (j t) -> p j t", t=2)[:, :, col],
                    in1=adabq[:, ch0:ch0 + njq], op=ALU.add), waits=[t_pe, t_ld], mark=(col == 1))
            ev_tok[g % 2] = t_mod
        msem = tr.dsem("modsem")
        t_ms = tr.dma("sp", mod_loc, modq[:].rearrange("p n c -> p (n c)"), msem, waits=[t_mod])
        tr.wait("pool", t_ms)
        agm = nc.alloc_semaphore("agmod")
        nc.gpsimd.collective_compute("AllGather", ALU.bypass, replica_groups=groups, ins=[mod_loc.opt()], outs=[mod_all.opt()]).then_inc(agm, 1)
        tr.wait("sp", (agm, 1, "agmod"))
        t_ml = tr.dma("sp", mod4[:], mod_all.rearrange("(r p) x -> p r x", p=128), msem)
        mod = mod4[:].rearrange("p r (n c) -> p (r n) c", c=2)
        tr.wait("dve", t_ml, t_ld)
        for sub in range(3):
            for stream in range(2):
                nc.vector.scalar_tensor_tensor(out=MV(sub, 0, stream), in0=mod[:, (3 * sub + 1) * KC:(3 * sub + 2) * KC, stream],
                                               scalar=1.0, in1=nw[:, sub * KC:(sub + 1) * KC], op0=ALU.add, op1=ALU.mult)
                nc.vector.tensor_copy(out=MV(sub, 1, stream), in_=mod[:, (3 * sub) * KC:(3 * sub + 1) * KC, stream])
                nc.vector.tensor_scalar(out=MV(sub, 2, stream), in0=mod[:, (3 * sub + 2) * KC:(3 * sub + 3) * KC, stream],
                                        scalar1=(1.0 if sub == 1 else 0.5), scalar2=0.0, op0=ALU.mult, op1=ALU.add)
        tr.barrier(mk)

    NGX = 4 if KC >= 4 else 1
    xsems = [tr.dsem("xsem%d" % g) for g in range(NGX)]

    def norm_tile(src, t0, T, A, B, xbuf, hT, ps_ss, rstd, pre_waits):
        pre_waits = [w for w in pre_waits if w is not None]
        xv = xbuf[:, :, 0:T]
        srcv = src.rearrange("(kc p) n -> p kc n", p=128)[:, :, t0:t0 + T]
        NG = NGX
        gk = KC // NG
        lt = []
        for g in range(NG):
            lt.append(tr.dma("sp", xv[:, g * gk:(g + 1) * gk, :], srcv[:, g * gk:(g + 1) * gk, :], xsems[g], waits=pre_waits))
        sq = []
        for g in range(NG):
            sq.append(tr.op("act", lambda e, g=g: e.activation(out=hT[:, g * gk:(g + 1) * gk, 0:T], in_=xv[:, g * gk:(g + 1) * gk, :],
                                                             func=AF.Square), waits=[lt[g]] + pre_waits))
        for kc in range(KC):
            if kc % gk == 0:
                tr.wait("pe", sq[kc // gk], *pre_waits)
            mm = nc.tensor.matmul(ps_ss[:, 0:T], lhsT=ones_bf, rhs=hT[:, kc, 0:T], start=(kc == 0), stop=(kc == KC - 1))
        t_ss = tr.mark("pe", mm)
        t_r0 = tr.op("dve", lambda e: e.tensor_scalar(out=rstd[:, 0:T], in0=ps_ss[:, 0:T], scalar1=1.0 / D, scalar2=EPS,
                                                      op0=ALU.mult, op1=ALU.add), waits=[t_ss] + pre_waits)
        t_r0 = tr.op("dve", lambda e: e.reciprocal(out=rstd[:, 0:T], in_=rstd[:, 0:T]), waits=[tr.op("act", lambda e: e.activation(out=rstd[:, 0:T], in_=rstd[:, 0:T], func=AF.Sqrt), waits=[t_r0])])
        tr.wait("dve", t_r0, *lt)
        toks = []
        for g in range(NG):
            for kc in range(g * gk, (g + 1) * gk):
                i1 = nc.vector.scalar_tensor_tensor(out=xv[:, kc, :], in0=xv[:, kc, :], scalar=A[:, kc:kc + 1], in1=rstd[:, 0:T],
                                                    op0=ALU.mult, op1=ALU.mult)
            t1 = tr.mark("dve", i1)
            tr.wait("act", t1, t_ss)
            for kc in range(g * gk, (g + 1) * gk):
                i2 = nc.scalar.activation(out=hT[:, kc, 0:T], in_=xv[:, kc, :], func=AF.Identity, bias=B[:, kc:kc + 1], scale=1.0)
            toks.append(tr.mark("act", i2))
        return toks

    def make_rb_state(rbufs, name):
        return {"i": 0, "free": [None, None], "psfree": None, "bufs": rbufs,
                "lsem": [tr.dsem(name + "_l%d" % k) for k in range(2)], "ssem": [tr.dsem(name + "_s%d" % k) for k in range(2)]}

    def down_proj(aT, nin, T, resid, dst, G, t0, psd, rb_state, a_ready, tag="dn"):
        rv_ = resid.rearrange("(kc p) n -> p kc n", p=128)
        dv_ = dst.rearrange("(kc p) n -> p kc n", p=128)
        a_ready = [w for w in a_ready if w is not None]
        last_pe = None
        for db in range(D // DBW):
            k = rb_state["i"] % 2
            rb = rb_state["bufs"][k]
            rfree = rb_state["free"][k]
            rb_state["i"] += 1
            t_res = tr.dma("sp", rb[:, :, 0:T], rv_[:, db * NDJ:(db + 1) * NDJ, t0:t0 + T], rb_state["lsem"][k], waits=[rfree])
            nsub = (nin + FSB - 1) // FSB
            ev = None
            for si in range(nsub):
                f0 = si * FSB
                nf = min(FSB, nin - f0)
                wi, wv, wtok = ws.next(tag)
                tr.wait("pe", wtok, *a_ready)
                if si == 0:
                    tr.wait("pe", rb_state["psfree"])
                for dj in range(NDJ):
                    for fl in range(nf):
                        f = f0 + fl
                        mm = nc.tensor.matmul(psd[dj][:, 0:T], lhsT=wv[:, fl, dj * 128:(dj + 1) * 128], rhs=aT[:, f, 0:T],
                                              start=(f == 0), stop=(f == nin - 1))
                    if si == nsub - 1:
                        pt = tr.mark("pe", mm)
                        kc = db * NDJ + dj
                        ev = tr.op("dve", lambda e, dj=dj, kc=kc: e.scalar_tensor_tensor(
                            out=rb[:, dj, 0:T], in0=psd[dj][:, 0:T], scalar=G[:, kc:kc + 1], in1=rb[:, dj, 0:T],
                            op0=ALU.mult, op1=ALU.add), waits=[pt, t_res])
                last_pe = tr.mark("pe", mm) if si < nsub - 1 else pt
                ws.release(wi, last_pe)
            rb_state["psfree"] = ev
            rb_state["free"][k] = tr.dma("sp", dv_[:, db * NDJ:(db + 1) * NDJ, t0:t0 + T], rb[:, :, 0:T], rb_state["ssem"][k], waits=[ev])
        return last_pe

    def ffn_phase(i, src, dst, tiles, sub):
        with ExitStack() as es:
            aT = es.enter_context(nc.sbuf_tensor("aT%d" % i, [128, FC, TT], BF16))
            hT = es.enter_context(nc.sbuf_tensor("hT%d" % i, [128, KC, TT], BF16))
            rbufs = [es.enter_context(nc.sbuf_tensor("rb%d_%d" % (i, k), [128, NDJ, TT], F32)) for k in range(2)]
            rstd = es.enter_context(nc.sbuf_tensor("rstd%d" % i, [128, TT], F32))
            sg1 = es.enter_context(nc.sbuf_tensor("sg%d" % i, [128, TT], F32))
            sg = [sg1, sg1]
            ps = [es.enter_context(nc.psum_tensor("psf%d_%d" % (i, k), [128, 512], F32)) for k in range(8)]
            assert KC * TT * 2 <= FC * TT, "x tile must fit inside aT"
            xbuf = aT[:].rearrange("p a b -> p (a b)")[:, 0:KC * TT * 2].bitcast(F32).rearrange("p (a b) -> p a b", a=KC)
            rb_state = make_rb_state(rbufs, "rbf%d" % i)
            pg_ = [0]
            last_ada = [None]
            a_free = None
            gu_free = [None, None]
            sg_free = [None, None]
            cnt = 0
            for (t0, T, stream) in tiles:
                A, B, G = MV(sub, 0, stream), MV(sub, 1, stream), MV(sub, 2, stream)
                pre = [a_free, rb_state["psfree"]]
                h_ready = norm_tile(src, t0, T, A, B, xbuf, hT, ps[4], rstd, pre)
                a_toks = []
                for p_, f0 in enumerate(range(0, c.DFF, WT)):
                    gi, gv, gtok = ws.next("g")
                    ui, uv, utok = ws.next("u")
                    for j in range(WT // 128):
                        f = f0 // 128 + j
                        if f >= FC:
                            break
                        b = cnt % 2
                        cnt += 1
                        psg, psu = ps[2 * b], ps[2 * b + 1]
                        tr.wait("pe", gtok, utok, gu_free[b], *h_ready)
                        for kc in range(KC):
                            nc.tensor.matmul(psg[:, 0:T], lhsT=gv[:, kc, j * 128:(j + 1) * 128], rhs=hT[:, kc, 0:T],
                                             start=(kc == 0), stop=(kc == KC - 1))
                        for kc in range(KC):
                            mm = nc.tensor.matmul(psu[:, 0:T], lhsT=uv[:, kc, j * 128:(j + 1) * 128], rhs=hT[:, kc, 0:T],
                                                  start=(kc == 0), stop=(kc == KC - 1))
                        pt = tr.mark("pe", mm)
                        t_s = tr.op("act", lambda e: e.activation(out=sg[b][:, 0:T], in_=psg[:, 0:T], func=AF.Silu),
                                    waits=[pt, sg_free[b]])
                        t_a = tr.op("dve", lambda e: e.tensor_tensor(out=aT[:, f, 0:T], in0=sg[b][:, 0:T], in1=psu[:, 0:T],
                                                                      op=ALU.mult), waits=[t_s, pt] + h_ready)
                        gu_free[b] = t_a
                        sg_free[0] = t_a
                        sg_free[1] = t_a
                        a_toks = [t_a]
                    ws.release(gi, pt)
                    ws.release(ui, pt)
                    if i == 0 and ada_state["g"] < NADA and ada_after_pair(pg_[0]):
                        last_ada[0] = ada_tile(ps[4:8], [rb_state["psfree"]])
                    pg_[0] += 1
                a_free = down_proj(aT, FC, T, src, dst, G, t0, ps[4:4 + NDJ], rb_state, a_toks + [gu_free[0], gu_free[1], last_ada[0]])
            tr.barrier(mk)

    _ck(0)
    ffn_phase(0, I["xin"], x1T, tiles_pre, 0)
    _ck(1)

    with ExitStack() as es:
        hT = es.enter_context(nc.sbuf_tensor("hTp", [128, KC, TT], BF16))
        xbuf = es.enter_context(nc.sbuf_tensor("xbp", [128, KC, TT], F32))
        rstd = es.enter_context(nc.sbuf_tensor("rstdp", [128, TT], F32))
        ropes = es.enter_context(nc.sbuf_tensor("ropes", [128, 6, TT], F32))
        xb = [es.enter_context(nc.sbuf_tensor("xbb%d" % k, [128, TT], BF16)) for k in range(2)]
        t1b = [es.enter_context(nc.sbuf_tensor("t1b%d" % k, [128, TT], F32)) for k in range(2)]
        t2b = [es.enter_context(nc.sbuf_tensor("t2b%d" % k, [128, TT], F32)) for k in range(2)]
        ob = [es.enter_context(nc.sbuf_tensor("ob%d" % k, [128, TT], BF16)) for k in range(3)]
        ps = [es.enter_context(nc.psum_tensor("psp%d" % k, [128, 512], F32)) for k in range(8)]
        rsem = tr.dsem("ropesem")
        t_rope = None
        rope_free = None
        ob_free = [None, None, None]
        obsem = [tr.dsem("obsem%d" % k) for k in range(3)]
        xb_free = [None, None]
        t12_free = [None, None]
        psx_free = [None, None]
        psr_free = [None, None]
        obi = 0
        ci = 0
        h_free = None
        last_pe_proj = None
        for (t0, T, stream) in tiles_pre:
            A, B = MV(1, 0, stream), MV(1, 1, stream)
            pre = [h_free]
            h_ready = norm_tile(x1T, t0, T, A, B, xbuf, hT, ps[7], rstd, pre)
            if stream == 0:
                tr.wait("sp", rope_free)
                cr_ = I["cosr"].rearrange("p (a n) -> p a n", a=2)
                sr_ = I["sinr"].rearrange("p (a n) -> p a n", a=2)
                tr.dma("sp", ropes[:, 0:2, 0:T], cr_[:, :, t0:t0 + T], rsem)
                tr.dma("sp", ropes[:, 2:4, 0:T], sr_[:, :, t0:t0 + T], rsem)
                tr.dma("sp", ropes[:, 4, 0:T], I["cosd"][:, t0:t0 + T], rsem)
                tr.dma("sp", ropes[:, 5, 0:T], I["sind"][:, t0:t0 + T], rsem)
                t_rope = tr.tot(rsem)
            if stop == 20:
                tr.barrier(mk)
                raise _Stop()
            for si_, (nm, c0, w) in enumerate(segs):
                if stop is not None and 21 <= stop <= 27 and si_ >= stop - 20:
                    tr.barrier(mk)
                    raise _Stop()
                if stream == 1 and nm in ("rq", "rg", "dq"):
                    continue
                token_major = nm in ("rv", "dv") or (stream == 1 and nm == "rk")
                for cc in range(0, w, WT):
                    wi, wv, wtok = ws.next("in")
                    if token_major:
                        assert WT == 256
                        dstt = {"rv": rv, "dv": dv_loc[cc // 256], "rk": ckd}[nm] if stream == 0 else {"rv": cvd, "dv": cdv, "rk": ckd}[nm]
                        dcol = 0 if (stream == 0 and nm == "dv") else cc
                        for ts in range(T // 128):
                            b = ci % 2
                            ci += 1
                            tr.wait("pe", wtok, psx_free[b], *h_ready)
                            for kc in range(KC):
                                mm = nc.tensor.matmul(ps[b][:, 0:WT], lhsT=hT[:, kc, ts * 128:(ts + 1) * 128], rhs=wv[:, kc, :],
                                                      start=(kc == 0), stop=(kc == KC - 1))
                            pt = tr.mark("pe", mm)
                            o = ob[obi % 3]
                            ofree = ob_free[obi % 3]
                            scale = (1.0 / 16.0) if nm == "rk" else 1.0
                            t_o = tr.op("act", lambda e, o=o, b=b, scale=scale: e.activation(
                                out=o[:, 0:WT], in_=ps[b][:, 0:WT], func=AF.Identity, scale=scale), waits=[pt, ofree])
                            psx_free[b] = t_o
                            trow = (t0 - NT if stream == 1 else t0) + ts * 128
                            ob_free[obi % 3] = tr.dma("sp", dstt[trow:trow + 128, dcol:dcol + WT], o[:, 0:WT], obsem[obi % 3], waits=[t_o])
                            obi += 1
                    else:
                        for j in range(WT // 128):
                            col = cc + j * 128
                            b = ci % 2
                            ci += 1
                            tr.wait("pe", wtok, psx_free[b], *h_ready)
                            for kc in range(KC):
                                mm = nc.tensor.matmul(ps[b][:, 0:T], lhsT=wv[:, kc, j * 128:(j + 1) * 128], rhs=hT[:, kc, 0:T],
                                                      start=(kc == 0), stop=(kc == KC - 1))
                            pt = tr.mark("pe", mm)
                            o = ob[obi % 3]
                            ofree = ob_free[obi % 3]
                            rope = (stream == 0) and nm in ("rq", "rk", "dq", "dk")
                            if not rope:
                                if nm == "rg":
                                    t_o = tr.op("act", lambda e, o=o, b=b: e.activation(out=o[:, 0:T], in_=ps[b][:, 0:T], func=AF.Silu),
                                                waits=[pt, ofree])
                                    dstt = rgT
                                else:
                                    t_o = tr.op("act", lambda e, o=o, b=b: e.activation(out=o[:, 0:T], in_=ps[b][:, 0:T], func=AF.Identity),
                                                waits=[pt, ofree])
                                    dstt = cdkT
                                psx_free[b] = t_o
                                tcol = t0 - NT if stream == 1 else t0
                            else:
                                if nm in ("rq", "rk"):
                                    chunk = (col // 128) % 2
                                    cos = ropes[:, chunk, 0:T]
                                    sin = ropes[:, 2 + chunk, 0:T]
                                    perm = permr_bf
                                    scale = 1.0 if nm == "rq" else 1.0 / 16.0
                                    dstt = rqT if nm == "rq" else rkT
                                else:
                                    cos = ropes[:, 4, 0:T]
                                    sin = ropes[:, 5, 0:T]
                                    perm = permd_bf
                                    scale = 128.0 ** -0.5 if nm == "dq" else 1.0
                                    dstt = dqT if nm == "dq" else dk_loc[col // 256]
                                t_xb = tr.op("act", lambda e, b=b: e.activation(out=xb[b][:, 0:T], in_=ps[b][:, 0:T], func=AF.Identity),
                                             waits=[pt, xb_free[b]])
                                tr.wait("pe", t_xb, psr_free[b])
                                mm = nc.tensor.matmul(ps[2 + b][:, 0:T], lhsT=perm, rhs=xb[b][:, 0:T], start=True, stop=True)
                                pr = tr.mark("pe", mm)
                                xb_free[b] = pr
                                t_1 = tr.op("dve", lambda e, b=b, cos=cos, scale=scale: e.scalar_tensor_tensor(
                                    out=t1b[b][:, 0:T], in0=ps[b][:, 0:T], scalar=scale, in1=cos, op0=ALU.mult, op1=ALU.mult),
                                    waits=[pt, t_xb, t12_free[b], t_rope])
                                psx_free[b] = t_1
                                t_2 = tr.op("dve", lambda e, b=b, sin=sin, scale=scale: e.scalar_tensor_tensor(
                                    out=t2b[b][:, 0:T], in0=ps[2 + b][:, 0:T], scalar=scale, in1=sin, op0=ALU.mult, op1=ALU.mult),
                                    waits=[pr])
                                psr_free[b] = t_2
                                rope_free = t_2
                                t_o = tr.op("dve", lambda e, o=o, b=b: e.tensor_tensor(out=o[:, 0:T], in0=t1b[b][:, 0:T],
                                                                                      in1=t2b[b][:, 0:T], op=ALU.add),
                                            waits=[t_1, t_2, ofree])
                                t12_free[b] = t_o
                                tcol = t0
                            drow = (col % 256) if (stream == 0 and nm == "dk") else col
                            ob_free[obi % 3] = tr.dma("sp", dstt[drow:drow + 128, tcol:tcol + T], o[:, 0:T], obsem[obi % 3], waits=[t_o])
                            obi += 1
                    ws.release(wi, pt)
                    last_pe_proj = pt
            h_free = last_pe_proj
        tr.barrier(mk)

    _ck(2)
    groups = [[0, 1, 2, 3], [4, 5, 6, 7]]
    t_kv = []
    for h in range(c.DHH):
        ksem = nc.alloc_semaphore("agk%d" % h)
        vsem = nc.alloc_semaphore("agv%d" % h)
        nc.gpsimd.collective_compute("AllGather", ALU.bypass, replica_groups=groups, ins=[dk_loc[h].opt()], outs=[dk_all[h].opt()]).then_inc(ksem, 1)
        nc.gpsimd.collective_compute("AllGather", ALU.bypass, replica_groups=groups, ins=[dv_loc[h].opt()], outs=[dv_all[h].opt()]).then_inc(vsem, 1)
        t_kv.append([(ksem, 1, "agk%d" % h), (vsem, 1, "agv%d" % h)])
    if stop == 3:
        tr.wait("pool", t_kv)
        tr.barrier(mk)
    _ck(3)
    RH = c.RH
    NCH = c.NCH
    RH = c.RH
    NCH = c.NCH
    lg = nc.alloc_sbuf_tensor("lg", [128, 2 * RH], F32)
    ksc = nc.alloc_sbuf_tensor("ksc", [128, 4, RH], F32)
    cwt = nc.alloc_sbuf_tensor("cwt", [128, 2, c.LC, RH], F32)
    coef = nc.alloc_sbuf_tensor("coef", [128, 10, RH], F32)
    t_st = []

    def ret_phase(part):
        with ExitStack() as es:
            cst = es.enter_context(nc.sbuf_tensor("cstr_p%d" % part, [128, c.NCONST], F32))
            rdec = es.enter_context(nc.sbuf_tensor("rdec_sb_p%d" % part, [128, 2 * RH], F32))
            meta = es.enter_context(nc.sbuf_tensor("meta_p%d" % part, [128, 20], F32))
            dint = es.enter_context(nc.sbuf_tensor("dint_p%d" % part, [128, 128], F32))
            dqf = es.enter_context(nc.sbuf_tensor("dqf_p%d" % part, [128, 128], F32))
            dqb = es.enter_context(nc.sbuf_tensor("dqb_p%d" % part, [128, 128], F32))
            tmpd = es.enter_context(nc.sbuf_tensor("tmpd_p%d" % part, [128, 128], F32))
            gnw = es.enter_context(nc.sbuf_tensor("gnw_p%d" % part, [128, c.RW // 128], F32))
            qT = es.enter_context(nc.sbuf_tensor("qTr_p%d" % part, [128, 2, NT], BF16))
            kT = es.enter_context(nc.sbuf_tensor("kTr_p%d" % part, [128, 2, NT], BF16))
            vt = es.enter_context(nc.sbuf_tensor("vtr_p%d" % part, [128, NCH, 256], BF16))
            kf = es.enter_context(nc.sbuf_tensor("kfr_p%d" % part, [128, NCH, 256], BF16))
            kb = es.enter_context(nc.sbuf_tensor("kbr_p%d" % part, [128, NCH, 256], BF16))
            qf = es.enter_context(nc.sbuf_tensor("qfr_p%d" % part, [128, 2, NT], BF16))
            qb = es.enter_context(nc.sbuf_tensor("qbr_p%d" % part, [128, 2, NT], BF16))
            gT = qf
            ckt = es.enter_context(nc.sbuf_tensor("ckt_p%d" % part, [128, c.LC, 256], BF16))
            cvt = es.enter_context(nc.sbuf_tensor("cvt_p%d" % part, [128, c.LC, 256], BF16))
            ckw = es.enter_context(nc.sbuf_tensor("ckw_p%d" % part, [128, 2, c.LC, 256], BF16))
            Sloc = es.enter_context(nc.sbuf_tensor("Sloc_p%d" % part, [128, 2, 2, 256], F32))
            Sg = es.enter_context(nc.sbuf_tensor("Sg_p%d" % part, [128, 4, 2, 2, 256], F32))
            Sst = es.enter_context(nc.sbuf_tensor("Sst_p%d" % part, [128, 2, 2, 256], F32))
            Sbf = es.enter_context(nc.sbuf_tensor("Sbf_p%d" % part, [128, 2, 2, 256], BF16))
            sdt = es.enter_context(nc.sbuf_tensor("sdt_p%d" % part, [128, 128], BF16))
            oacc = es.enter_context(nc.sbuf_tensor("oacc_p%d" % part, [128, 2, NT], F32))
            osq = es.enter_context(nc.sbuf_tensor("osq_p%d" % part, [128, 2, 512], BF16))
            obf = es.enter_context(nc.sbuf_tensor("obf_p%d" % part, [128, 2, 512], BF16))
            stat = es.enter_context(nc.sbuf_tensor("stat_p%d" % part, [128, 3, 512], F32))
            ymix = es.enter_context(nc.sbuf_tensor("ymix_p%d" % part, [128, 2, NT], BF16))
            wfull = es.enter_context(nc.sbuf_tensor("wfull_p%d" % part, [128, 2, RH, NCH], F32))
            ps = [es.enter_context(nc.psum_tensor("psr%d_%d" % (k, part), [128, 512], F32)) for k in range(7)]
            pst = es.enter_context(nc.psum_tensor("pstr%d" % part, [128, 1024], BF16))

            tr.dma("sp", cst[:], I["consts"], ld)
            tr.dma("sp", rdec[:], I["rdec"].partition_broadcast(128)[:, 0, :], ld)
            tr.dma("sp", meta[:], I["segmeta"].partition_broadcast(128)[:, 0, :], ld)
            tr.dma("sp", gnw[:], I["gnwT"], ld)
            t_l = tr.tot(ld)
            OFF = 384
            relf, maskf = cst[:, OFF:OFF + 128], cst[:, OFF + 128:OFF + 256]
            relb, maskb = cst[:, OFF + 256:OFF + 384], cst[:, OFF + 384:OFF + 512]
            posqf, posqb = cst[:, OFF + 512:OFF + 640], cst[:, OFF + 640:OFF + 768]
            pkf, pkb = cst[:, OFF + 768:OFF + 769], cst[:, OFF + 769:OFF + 770]
            ctxf = cst[:, OFF + 770:OFF + 770 + c.LC]
            ctxb = cst[:, OFF + 770 + c.LC:OFF + 770 + 2 * c.LC]
            t_setup = None
            t5_ = None
            if part == 1:
                t_e = tr.op("act", lambda e: e.activation(out=lg[:], in_=rdec[:], func=AF.Exp), waits=[t_l])
                t_lg = tr.op("dve", lambda e: e.tensor_scalar(out=lg[:], in0=lg[:], scalar1=-1.0, scalar2=0.0, op0=ALU.mult, op1=ALU.add), waits=[t_e])
                tr.wait("act", t_lg)
                for h in range(RH):
                    lf, lb = lg[:, h:h + 1], lg[:, RH + h:RH + h + 1]
                    nc.scalar.activation(out=ksc[:, 0, h:h + 1], in_=pkf, func=AF.Exp, scale=lf)
                    nc.scalar.activation(out=ksc[:, 1, h:h + 1], in_=pkb, func=AF.Exp, scale=lb)
                    nc.scalar.activation(out=ksc[:, 2, h:h + 1], in_=lf, func=AF.Exp, scale=128.0)
                    nc.scalar.activation(out=ksc[:, 3, h:h + 1], in_=lb, func=AF.Exp, scale=128.0)
                    for jc in range(c.LC):
                        nc.scalar.activation(out=cwt[:, 0, jc, h:h + 1], in_=ctxf[:, jc:jc + 1], func=AF.Exp, scale=lf)
                        nc.scalar.activation(out=cwt[:, 1, jc, h:h + 1], in_=ctxb[:, jc:jc + 1], func=AF.Exp, scale=lb)
                    for r in range(4):
                        nc.scalar.activation(out=coef[:, r, h:h + 1], in_=meta[:, r:r + 1], func=AF.Exp, scale=lf)
                        nc.scalar.activation(out=coef[:, 5 + r, h:h + 1], in_=meta[:, 9 + r:10 + r], func=AF.Exp, scale=lb)
                    nc.scalar.activation(out=coef[:, 4, h:h + 1], in_=meta[:, 8:9], func=AF.Exp, scale=lf)
                    i_ = nc.scalar.activation(out=coef[:, 9, h:h + 1], in_=meta[:, 17:18], func=AF.Exp, scale=lb)
                t3_ = tr.mark("act", i_)
                tr.wait("dve", t3_)
                for r in range(4):
                    nc.vector.tensor_scalar(out=coef[:, r, :], in0=coef[:, r, :], scalar1=meta[:, 4 + r:5 + r], scalar2=0.0, op0=ALU.mult, op1=ALU.add)
                    i_ = nc.vector.tensor_scalar(out=coef[:, 5 + r, :], in0=coef[:, 5 + r, :], scalar1=meta[:, 13 + r:14 + r], scalar2=0.0,
                                                 op0=ALU.mult, op1=ALU.add)
                t5_ = tr.mark("dve", i_)
                t_setup = t5_
            for e_ in ("act", "pe", "pool", "dve", "sp"):
                tr.wait(e_, t_setup, t5_, t_l)

            hl = tr.dsem("ret_ld%d" % part)
            slsem = tr.dsem("slsem%d" % part)
            gsem = tr.dsem("gsem%d" % part)
            ysem = tr.dsem("ysem%d" % part)

            def load_head(h, extra=None):
                tr.dma("sp", qT[:], rqT[h * 256:(h + 1) * 256, :].rearrange("(a p) n -> p a n", p=128), hl)
                tr.dma("sp", kT[:], rkT[h * 256:(h + 1) * 256, :].rearrange("(a p) n -> p a n", p=128), hl)
                tr.dma("sp", vt[:], rv[:, h * 256:(h + 1) * 256].rearrange("(a p) n -> p a n", p=128), hl)
                tr.dma("sp", ckt[:], ckd[:, h * 256:(h + 1) * 256].rearrange("(a p) n -> p a n", p=128), hl)
                tr.dma("sp", cvt[:], cvd[:, h * 256:(h + 1) * 256].rearrange("(a p) n -> p a n", p=128), hl)
                if extra is not None:
                    extra()
                return tr.tot(hl)

            def prep_head(h, t_ld_h, sc_f=None, sc_b=None):
                tk = None
                for i in range(NCH):
                    tr.wait("pe", t_ld_h, tk)
                    for dc in range(2):
                        mm = nc.tensor.transpose(pst[:, dc * 128:(dc + 1) * 128], kT[:, dc, i * 128:(i + 1) * 128], ident_bf)
                    pt = tr.mark("pe", mm)
                    s_f = ksc[:, 0, h:h + 1] if sc_f is None else sc_f(i)
                    s_b = ksc[:, 1, h:h + 1] if sc_b is None else sc_b(i)
                    tr.op("act", lambda e, i=i: e.activation(out=kf[:, i, :], in_=pst[:, 0:256], func=AF.Identity, scale=s_f),
                          waits=[pt], mark=False)
                    tk = tr.op("act", lambda e, i=i: e.activation(out=kb[:, i, :], in_=pst[:, 0:256], func=AF.Identity, scale=s_b))
                return tk

            def state_update(dirn, i, h, first, waits):
                kk = kf if dirn == 0 else kb
                bank = ps[5 + dirn]
                waits = [w for w in waits if w is not None]
                tr.wait("pe", *waits)
                for dc in range(2):
                    mm = nc.tensor.matmul(bank[:, dc * 256:(dc + 1) * 256], lhsT=kk[:, i, dc * 128:(dc + 1) * 128], rhs=vt[:, i, :],
                                          start=True, stop=True)
                pt = tr.mark("pe", mm)
                tr.wait("dve", pt, *waits)
                for dc in range(2):
                    if first:
                        i_ = nc.vector.tensor_copy(out=Sst[:, dirn, dc, :], in_=bank[:, dc * 256:(dc + 1) * 256])
                    else:
                        i_ = nc.vector.scalar_tensor_tensor(out=Sst[:, dirn, dc, :], in0=Sst[:, dirn, dc, :],
                                                            scalar=ksc[:, 2 + dirn, h:h + 1], in1=bank[:, dc * 256:(dc + 1) * 256],
                                                            op0=ALU.mult, op1=ALU.add)
                return tr.mark("dve", i_)

            if part == 1:
                OFP = OFF + 770 + 2 * c.LC
                posff, posfb = cst[:, OFP:OFP + NCH], cst[:, OFP + NCH:OFP + 2 * NCH]
                tr.wait("act", t_setup, t5_, t_l)
                for h in range(RH):
                    nc.scalar.activation(out=wfull[:, 0, h, :], in_=posff, func=AF.Exp, scale=lg[:, h:h + 1])
                    i_ = nc.scalar.activation(out=wfull[:, 1, h, :], in_=posfb, func=AF.Exp, scale=lg[:, RH + h:RH + h + 1])
                t_wf = tr.mark("act", i_)
                tr.wait("act", t_wf)
                t_slst = None
                t_ld_next = None
                t_cp = None
                for h in range(RH):
                    t_ld_h = load_head(h) if h == 0 else t_ld_next
                    tk = prep_head(h, t_ld_h, sc_f=lambda i, h=h: wfull[:, 0, h, i:i + 1], sc_b=lambda i, h=h: wfull[:, 1, h, i:i + 1])
                    n_rest = NADA - ada_state["g"]
                    for _ in range(n_rest if h == RH - 1 else min(n_rest, -(-(NADA - NADA0) // RH))):
                        ada_tile(ps[0:4])
                    tr.wait("pe", tk, t_cp)
                    for dirn in range(2):
                        kk = kf if dirn == 0 else kb
                        for dc in range(2):
                            for i in range(NCH):
                                mm = nc.tensor.matmul(ps[5 + dirn][:, dc * 256:(dc + 1) * 256], lhsT=kk[:, i, dc * 128:(dc + 1) * 128],
                                                      rhs=vt[:, i, :], start=(i == 0), stop=(i == NCH - 1))
                    pt = tr.mark("pe", mm)
                    tr.wait("dve", pt, t_slst)
                    for dirn in range(2):
                        i_ = nc.vector.tensor_copy(out=Sloc[:, dirn, :, :].rearrange("p c e -> p (c e)"), in_=ps[5 + dirn][:, 0:512])
                    t_cp = tr.mark("dve", i_)
                    t_slst = tr.dma("sp", st_loc[h].rearrange("(a p) e -> p a e", p=128), Sloc[:].rearrange("p d c e -> p (d c) e"), slsem,
                                    waits=[t_cp])
                    tr.wait("pool", t_slst)
                    ssem = nc.alloc_semaphore("ags%d" % h)
                    nc.gpsimd.collective_compute("AllGather", ALU.bypass, replica_groups=groups, ins=[st_loc[h].opt()],
                                                 outs=[st_all[h].opt()]).then_inc(ssem, 1)
                    t_st.append((ssem, 1, "ags%d" % h))
                    tr.wait("sp", tk, pt)
                    if h + 1 < RH:
                        t_ld_next = load_head(h + 1)
            if part == 2:
                t_head_free = None
                o_free = None
                y_free = None
                for h in range(RH):
                    tr.wait("sp", t_head_free, t_st[h])
                    stall_h = st_all[h].rearrange("(r a p) e -> p r a e", r=4, p=128)
                    t_ld_h = load_head(h, extra=lambda: [tr.dma("sp", Sg[:, r, :, :, :].rearrange("p d c e -> p (d c) e"), stall_h[:, r], hl)
                                                         for r in range(4)])
                    tk = prep_head(h, t_ld_h)
                    lf, lb = lg[:, h:h + 1], lg[:, RH + h:RH + h + 1]
                    tr.wait("act", t_head_free)
                    nc.scalar.activation(out=dint[:], in_=relf, func=AF.Exp, scale=lf)
                    nc.scalar.activation(out=tmpd[:], in_=relb, func=AF.Exp, scale=lb)
                    nc.scalar.activation(out=dqf[:], in_=posqf, func=AF.Exp, scale=lf)
                    i_ = nc.scalar.activation(out=dqb[:], in_=posqb, func=AF.Exp, scale=lb)
                    t_dq = tr.mark("act", i_)
                    tr.wait("dve", t_dq, t_head_free)
                    nc.vector.tensor_tensor(out=dint[:], in0=dint[:], in1=maskf, op=ALU.mult)
                    i_ = nc.vector.tensor_tensor(out=tmpd[:], in0=tmpd[:], in1=maskb, op=ALU.mult)
                    t_di = tr.op("dve", lambda e: e.tensor_tensor(out=dint[:], in0=dint[:], in1=tmpd[:], op=ALU.add), waits=[tr.mark("dve", i_)])
                    tr.wait("dve", t_ld_h, t_head_free)
                    for dirn in range(2):
                        for jc in range(c.LC):
                            i_ = nc.vector.tensor_scalar(out=ckw[:, dirn, jc, :], in0=ckt[:, jc, :], scalar1=cwt[:, dirn, jc, h:h + 1],
                                                         scalar2=0.0, op0=ALU.mult, op1=ALU.add)
                    t_cw = tr.mark("dve", i_)
                    t_s0 = None
                    for dirn in range(2):
                        tr.wait("pe", t_cw, t_s0)
                        for dc in range(2):
                            for jc in range(c.LC):
                                mm = nc.tensor.matmul(ps[5 + dc][:, 0:256], lhsT=ckw[:, dirn, jc, dc * 128:(dc + 1) * 128], rhs=cvt[:, jc, :],
                                                      start=(jc == 0), stop=(jc == c.LC - 1))
                        pt = tr.mark("pe", mm)
                        cbase = 0 if dirn == 0 else 5
                        tr.wait("dve", pt)
                        for dc in range(2):
                            i_ = nc.vector.tensor_scalar(out=Sst[:, dirn, dc, :], in0=ps[5 + dc][:, 0:256], scalar1=coef[:, cbase + 4, h:h + 1],
                                                         scalar2=0.0, op0=ALU.mult, op1=ALU.add)
                        t_s0 = tr.mark("dve", i_)
                        t_s0 = tr.chain("dve", [
                            (lambda e, r=r: e.scalar_tensor_tensor(out=Sst[:, dirn, :, :], in0=Sg[:, r, dirn, :, :], scalar=coef[:, cbase + r, h:h + 1],
                                                                   in1=Sst[:, dirn, :, :], op0=ALU.mult, op1=ALU.add)) for r in range(4)], waits=[t_s0])
                    tr.wait("pool", t_ld_h, t_head_free, t_dq)
                    for i in range(NCH):
                        for dc in range(2):
                            nc.gpsimd.tensor_tensor(out=qf[:, dc, i * 128:(i + 1) * 128], in0=qT[:, dc, i * 128:(i + 1) * 128], in1=dqf[:], op=ALU.mult)
                            i_ = nc.gpsimd.tensor_tensor(out=qb[:, dc, i * 128:(i + 1) * 128], in0=qT[:, dc, i * 128:(i + 1) * 128], in1=dqb[:],
                                                         op=ALU.mult)
                    t_q = tr.mark("pool", i_)
                    t_sd_ = [t_s0, t_s0]
                    bf_free = [None, None]
                    acc_tok = {}
                    psb_free = None
                    psf_free = None
                    sd_free = None
                    t_ev = None
                    p2 = None
                    for s_ in range(NCH):
                        ib = NCH - 1 - s_
                        i = s_
                        t_bfb = tr.op("act", lambda e: e.activation(out=Sbf[:, 1, :, :], in_=Sst[:, 1, :, :], func=AF.Identity),
                                      waits=[t_sd_[1], bf_free[1]])
                        tr.wait("pe", t_bfb, t_q, psb_free, o_free)
                        for ec in range(2):
                            for dc in range(2):
                                mm = nc.tensor.matmul(ps[ec][:, 0:128], lhsT=Sbf[:, 1, dc, ec * 128:(ec + 1) * 128],
                                                      rhs=qb[:, dc, ib * 128:(ib + 1) * 128], start=(dc == 0), stop=(dc == 1))
                        ptb = tr.mark("pe", mm)
                        bf_free[1] = ptb
                        tr.wait("dve", ptb, acc_tok.get(ib), o_free)
                        for ec in range(2):
                            if ib in acc_tok:
                                i_ = nc.vector.tensor_tensor(out=oacc[:, ec, ib * 128:(ib + 1) * 128], in0=oacc[:, ec, ib * 128:(ib + 1) * 128],
                                                             in1=ps[ec][:, 0:128], op=ALU.add)
                            else:
                                i_ = nc.vector.tensor_copy(out=oacc[:, ec, ib * 128:(ib + 1) * 128], in_=ps[ec][:, 0:128])
                        t_ev = tr.mark("dve", i_)
                        acc_tok[ib] = t_ev
                        psb_free = t_ev
                        if ib > 0:
                            t_sd_[1] = state_update(1, ib, h, False, [tk, t_sd_[1], ptb])
                        t_bff = tr.op("act", lambda e: e.activation(out=Sbf[:, 0, :, :], in_=Sst[:, 0, :, :], func=AF.Identity),
                                      waits=[t_sd_[0], bf_free[0]])
                        tr.wait("pe", t_ld_h, sd_free)
                        for dc in range(2):
                            mm = nc.tensor.matmul(ps[2][:, 0:128], lhsT=kT[:, dc, i * 128:(i + 1) * 128], rhs=qT[:, dc, i * 128:(i + 1) * 128],
                                                  start=(dc == 0), stop=(dc == 1))
                        p1 = tr.mark("pe", mm)
                        t_sd = tr.op("dve", lambda e: e.tensor_tensor(out=sdt[:], in0=ps[2][:, 0:128], in1=dint[:], op=ALU.mult),
                                     waits=[p1, sd_free, t_di])
                        tr.wait("pe", t_sd, t_bff, t_q, psf_free)
                        for ec in range(2):
                            nc.tensor.matmul(ps[3 + ec][:, 0:128], lhsT=vt[:, i, ec * 128:(ec + 1) * 128], rhs=sdt[:], start=True, stop=False)
                            for dc in range(2):
                                mm = nc.tensor.matmul(ps[3 + ec][:, 0:128], lhsT=Sbf[:, 0, dc, ec * 128:(ec + 1) * 128],
                                                      rhs=qf[:, dc, i * 128:(i + 1) * 128], start=False, stop=(dc == 1))
                        p2 = tr.mark("pe", mm)
                        sd_free = p2
                        bf_free[0] = p2
                        tr.wait("dve", p2, acc_tok.get(i), o_free)
                        for ec in range(2):
                            if i in acc_tok:
                                i_ = nc.vector.tensor_tensor(out=oacc[:, ec, i * 128:(i + 1) * 128], in0=oacc[:, ec, i * 128:(i + 1) * 128],
                                                             in1=ps[3 + ec][:, 0:128], op=ALU.add)
                            else:
                                i_ = nc.vector.tensor_copy(out=oacc[:, ec, i * 128:(i + 1) * 128], in_=ps[3 + ec][:, 0:128])
                        t_ev = tr.mark("dve", i_)
                        acc_tok[i] = t_ev
                        psf_free = t_ev
                        if i < NCH - 1:
                            t_sd_[0] = state_update(0, i, h, False, [tk, t_sd_[0], p2])
                    t_gl = tr.dma("sp", gT[:], rgT[h * 256:(h + 1) * 256, :].rearrange("(a p) n -> p a n", p=128), gsem, waits=[p2, t_ev])
                    t_y = None
                    pt = None
                    for q0 in range(0, NT, 512):
                        Tq = min(512, NT - q0)
                        tr.wait("act", t_ev, pt)
                        nc.scalar.activation(out=osq[:, :, 0:Tq], in_=oacc[:, :, q0:q0 + Tq], func=AF.Square)
                        i_ = nc.scalar.activation(out=obf[:, :, 0:Tq], in_=oacc[:, :, q0:q0 + Tq], func=AF.Identity)
                        t_sq = tr.mark("act", i_)
                        tr.wait("pe", t_sq, t_y)
                        for ec in range(2):
                            nc.tensor.matmul(ps[0][:, 0:Tq], lhsT=ones_bf, rhs=obf[:, ec, 0:Tq], start=(ec == 0), stop=(ec == 1))
                        for ec in range(2):
                            mm = nc.tensor.matmul(ps[1][:, 0:Tq], lhsT=ones_bf, rhs=osq[:, ec, 0:Tq], start=(ec == 0), stop=(ec == 1))
                        pt = tr.mark("pe", mm)
                        t_c = tr.chain("dve", [
                            lambda e: e.tensor_scalar(out=stat[:, 0, 0:Tq], in0=ps[0][:, 0:Tq], scalar1=1.0 / 256, scalar2=0.0, op0=ALU.mult, op1=ALU.add),
                            lambda e: e.tensor_tensor(out=stat[:, 1, 0:Tq], in0=stat[:, 0, 0:Tq], in1=stat[:, 0, 0:Tq], op=ALU.mult),
                            lambda e: e.scalar_tensor_tensor(out=stat[:, 1, 0:Tq], in0=ps[1][:, 0:Tq], scalar=1.0 / 256, in1=stat[:, 1, 0:Tq],
                                                             op0=ALU.mult, op1=ALU.subtract),
                            lambda e: e.tensor_scalar(out=stat[:, 1, 0:Tq], in0=stat[:, 1, 0:Tq], scalar1=EPS, scalar2=1.0, op0=ALU.add, op1=ALU.mult),
                        ], waits=[pt, t_y, y_free, t_gl])
                        t_c = tr.op("dve", lambda e: e.reciprocal(out=stat[:, 1, 0:Tq], in_=stat[:, 1, 0:Tq]), waits=[tr.op("act", lambda e: e.activation(out=stat[:, 1, 0:Tq], in_=stat[:, 1, 0:Tq], func=AF.Sqrt), waits=[t_c])])
                        for ec in range(2):
                            t_c = tr.chain("dve", [
                                lambda e, ec=ec: e.tensor_tensor(out=stat[:, 2, 0:Tq], in0=oacc[:, ec, q0:q0 + Tq], in1=stat[:, 0, 0:Tq], op=ALU.subtract),
                                lambda e, ec=ec: e.tensor_tensor(out=stat[:, 2, 0:Tq], in0=stat[:, 2, 0:Tq], in1=stat[:, 1, 0:Tq], op=ALU.mult),
                                lambda e, ec=ec: e.scalar_tensor_tensor(out=ymix[:, ec, q0:q0 + Tq], in0=stat[:, 2, 0:Tq],
                                                                        scalar=gnw[:, 2 * h + ec:2 * h + ec + 1], in1=gT[:, ec, q0:q0 + Tq],
                                                                        op0=ALU.mult, op1=ALU.mult),
                            ], waits=[t_c])
                        t_y = t_c
                    o_free = t_y
                    y_free = tr.dma("sp", mixT[h * 256:(h + 1) * 256, :].rearrange("(a p) n -> p a n", p=128), ymix[:], ysem, waits=[t_y])
                    t_head_free = t_y
            tr.barrier(mk)

    ret_phase(1)
    _ck(4)
    DHH = c.DHH
    NKC = c.NKC
    QT = 256
    with ExitStack() as es:
        lam_t = es.enter_context(nc.sbuf_tensor("lam_t", [128, 512], F32))
        lam_p = es.enter_context(nc.sbuf_tensor("lam_p", [128, 256], F32))
        lam_s = es.enter_context(nc.sbuf_tensor("lam_s", [128, 4], F32))
        subw = es.enter_context(nc.sbuf_tensor("subw_sb", [128, 256], F32))
        NB = 1
        KTs = [es.enter_context(nc.sbuf_tensor("KTs%d" % k, [128, 2, NKC * 128], BF16)) for k in range(NB)]
        Vs = [es.enter_context(nc.sbuf_tensor("Vs%d" % k, [128, NKC, 257], BF16)) for k in range(NB)]
        Qs = [es.enter_context(nc.sbuf_tensor("Qs%d" % k, [128, 2, NT], BF16)) for k in range(NB)]
        PT = [es.enter_context(nc.sbuf_tensor("PT%d" % k, [128, 2, QT], BF16)) for k in range(3)]
        o1 = es.enter_context(nc.sbuf_tensor("o1", [128, 2, 256], F32))
        comb = es.enter_context(nc.sbuf_tensor("comb", [128, 2, 256], F32))
        junk = es.enter_context(nc.sbuf_tensor("junk", [128, 2, 256], F32))
        rr = es.enter_context(nc.sbuf_tensor("rr", [128, 2, 4], F32))
        ybf = es.enter_context(nc.sbuf_tensor("ybf", [128, 2, 256], BF16))
        ydT = es.enter_context(nc.sbuf_tensor("ydT", [128, 2, QT], BF16))
        psO = [es.enter_context(nc.psum_tensor("psO%d" % k, [128, 512], F32)) for k in range(4)]
        psS = [es.enter_context(nc.psum_tensor("psS%d" % k, [128, 512], F32)) for k in range(3)]
        pstd = es.enter_context(nc.psum_tensor("pstd", [128, 1024], BF16))

        tr.dma("sp", lam_t[:], I["lamp"].partition_broadcast(128)[:, 0, :], ld)
        tr.dma("sp", subw[:], I["subw"].partition_broadcast(128)[:, 0, :], ld)
        t_l = tr.tot(ld)
        tr.wait("dve", t_l)
        for k in range(NB):
            nc.vector.memset(Vs[k][:, :, 256:257], 1.0)
        lt4 = lam_t[:].rearrange("p (a t b) -> p a t b", a=2, t=2)
        t_a = tr.chain("dve", [
            lambda e: e.tensor_tensor(out=lam_p[:].rearrange("p (a b) -> p a b", a=2), in0=lt4[:, :, 0, :], in1=lt4[:, :, 1, :], op=ALU.mult),
            lambda e: e.tensor_reduce(out=lam_s[:, 0:2], in_=lam_p[:].rearrange("p (a b) -> p a b", a=2), axis=AX.X, op=ALU.add),
        ], waits=[t_l])
        t_b = tr.op("act", lambda e: e.activation(out=lam_s[:, 2:4], in_=lam_s[:, 0:2], func=AF.Exp), waits=[t_a])
        t_lam = tr.chain("dve", [
            lambda e: e.tensor_tensor(out=lam_s[:, 0:1], in0=lam_s[:, 3:4], in1=lam_s[:, 2:3], op=ALU.subtract),
            lambda e: e.tensor_scalar(out=lam_s[:, 0:1], in0=lam_s[:, 0:1], scalar1=-0.2, scalar2=1.0, op0=ALU.add, op1=ALU.mult),
            lambda e: e.tensor_scalar(out=subw[:], in0=subw[:], scalar1=0.8, scalar2=0.0, op0=ALU.mult, op1=ALU.add),
        ], waits=[t_b])
        nlam = lam_s[:, 0:1]

        dl = [tr.dsem("dl0"), tr.dsem("dl1")]
        ydsem = tr.dsem("ydsem")
        def load_dhead(h, b, waits):
            tr.wait("sp", t_kv[h], *waits)
            r0 = h * 256
            for m in range(2):
                tr.dma("sp", KTs[b][:, m, 0:L], cdkT[r0 + m * 128:r0 + (m + 1) * 128, :], dl[b])
                tr.dma("sp", KTs[b][:, m, L:L + 4 * NT].rearrange("p (r n) -> p r n", r=4),
                       dk_all[h].rearrange("(r f) n -> f r n", r=4)[m * 128:(m + 1) * 128, :, :], dl[b])
                tr.dma("sp", Qs[b][:, m, :], dqT[r0 + m * 128:r0 + (m + 1) * 128, :], dl[b])
            tr.dma("sp", Vs[b][:, 0:c.LC, 0:256], cdv[:, r0:r0 + 256].rearrange("(a p) n -> p a n", p=128), dl[b])
            tr.dma("sp", Vs[b][:, c.LC:NKC, 0:256], dv_all[h].rearrange("(a p) n -> p a n", p=128), dl[b])
            return (dl[b].h, dl[b].v, "d_%d" % id(dl[b]))

        head_free = [None, None]
        t_ldh = load_dhead(0, 0, [])
        pt_free = [None, None, None]
        ps_free = [None, None, None]
        st8 = {"O_free": None, "y_st": None, "yd_free": None, "p3": None, "git": 0}
        nq = QT // 128
        for h in range(DHH):
            b = 0
            if h > 0:
                t_ldh = load_dhead(h, 0, [head_free[0]])
            KT, V, Q = KTs[b], Vs[b], Qs[b]
            its = [(q0, kc) for q0 in range(0, NT, QT) for kc in range(NKC)]
            nit = len(its)
            base = st8["git"]
            st8["git"] += nit
            t_es = {}
            pending = []

            def stageA(i):
                q0, kc = its[i]
                sb = (base + i) % 3
                tr.wait("pe", t_ldh, ps_free[sb], t_lam)
                for m in range(2):
                    mm = nc.tensor.matmul(psS[sb][:, m * QT:(m + 1) * QT], lhsT=KT[:, m, kc * 128:(kc + 1) * 128], rhs=Q[:, m, q0:q0 + QT],
                                          start=True, stop=True)
                p1 = tr.mark("pe", mm)
                t_e = tr.op("act", lambda e: e.activation(out=PT[sb][:].rearrange("p a b -> p (a b)"), in_=psS[sb][:, 0:2 * QT],
                                                          func=AF.Exp), waits=[p1, pt_free[sb]])
                ps_free[sb] = t_e
                t_es[i] = t_e

            def epilogue(q0, p2):
                toks = [[p2, st8["p3"], t_lam] for _ in range(nq)]
                steps = [
                    lambda e, qs: e.reciprocal(out=rr[:, qs, 0:1], in_=psO[qs][:, 256:257]),
                    lambda e, qs: e.reciprocal(out=rr[:, qs, 1:2], in_=psO[nq + qs][:, 256:257]),
                    lambda e, qs: e.tensor_tensor(out=rr[:, qs, 1:2], in0=rr[:, qs, 1:2], in1=nlam, op=ALU.mult),
                    lambda e, qs: e.tensor_scalar(out=o1[:, qs, :], in0=psO[qs][:, 0:256], scalar1=rr[:, qs, 0:1], scalar2=0.0,
                                                  op0=ALU.mult, op1=ALU.add),
                    lambda e, qs: e.scalar_tensor_tensor(out=comb[:, qs, :], in0=psO[nq + qs][:, 0:256], scalar=rr[:, qs, 1:2], in1=o1[:, qs, :],
                                                         op0=ALU.mult, op1=ALU.add),
                    lambda e, qs: e.tensor_tensor(out=junk[:, qs, :], in0=comb[:, qs, :], in1=comb[:, qs, :], op=ALU.mult),
                    lambda e, qs: e.tensor_reduce(out=rr[:, qs, 2:3], in_=junk[:, qs, :], axis=AX.X, op=ALU.add),
                    lambda e, qs: e.tensor_scalar(out=rr[:, qs, 2:3], in0=rr[:, qs, 2:3], scalar1=1.0 / 256, scalar2=EPS, op0=ALU.mult, op1=ALU.add),
                ]
                for k, fn in enumerate(steps):
                    for qs in range(nq):
                        toks[qs] = tr.op("dve", lambda e, fn=fn, qs=qs: fn(e, qs), waits=toks[qs] if isinstance(toks[qs], list) else [toks[qs]])
                    if k == 4:
                        st8["O_free"] = toks[nq - 1]
                for qs in range(nq):
                    t_ = tr.op("act", lambda e, qs=qs: e.activation(out=rr[:, qs, 3:4], in_=rr[:, qs, 2:3], func=AF.Ln), waits=[toks[qs]])
                    toks[qs] = tr.op("act", lambda e, qs=qs: e.activation(out=rr[:, qs, 2:3], in_=rr[:, qs, 3:4], func=AF.Exp, scale=-0.5), waits=[t_])
                for qs in range(nq):
                    toks[qs] = tr.op("dve", lambda e, qs=qs: e.scalar_tensor_tensor(out=ybf[:, qs, :], in0=comb[:, qs, :], scalar=rr[:, qs, 2:3],
                                                                                   in1=subw[:], op0=ALU.mult, op1=ALU.mult), waits=[toks[qs]])
                t_yb = list(toks)

                def transposes():
                    for qs in range(nq):
                        tr.wait("pe", t_yb[qs], st8["yd_free"])
                        for ec in range(2):
                            mm = nc.tensor.transpose(pstd[:, ec * 128:(ec + 1) * 128], ybf[:, qs, ec * 128:(ec + 1) * 128], ident_bf)
                        p3 = tr.mark("pe", mm)
                        st8["p3"] = p3
                        tr.wait("act", p3, st8["y_st"])
                        for ec in range(2):
                            i_ = nc.scalar.activation(out=ydT[:, ec, qs * 128:(qs + 1) * 128], in_=pstd[:, ec * 128:(ec + 1) * 128],
                                                      func=AF.Identity)
                        st8["yd_free"] = tr.mark("act", i_)
                    st8["y_st"] = tr.dma("sp", mixT[c.RW + h * 256:c.RW + (h + 1) * 256, q0:q0 + QT].rearrange("(a p) n -> p a n", p=128),
                                         ydT[:], ydsem, waits=[st8["yd_free"]])
                return transposes

            stageA(0)
            if nit > 1:
                stageA(1)
            t_last_pv = None
            for i in range(nit):
                if i + 2 < nit:
                    stageA(i + 2)
                q0, kc = its[i]
                sb = (base + i) % 3
                tr.wait("pe", t_es[i])
                if kc == 0:
                    tr.wait("pe", st8["O_free"])
                for m in range(2):
                    for qs in range(nq):
                        mm = nc.tensor.matmul(psO[m * nq + qs][:, 0:257], lhsT=PT[sb][:, m, qs * 128:(qs + 1) * 128], rhs=V[:, kc, :],
                                              start=(kc == 0), stop=(kc == NKC - 1))
                p2 = tr.mark("pe", mm)
                pt_free[sb] = p2
                t_last_pv = p2
                if kc == NKC - 1:
                    pending.append((i + 3, epilogue(q0, p2)))
                while pending and (pending[0][0] <= i or i == nit - 1):
                    pending.pop(0)[1]()
            head_free[b] = t_last_pv
        tr.barrier(mk)

    ret_phase(2)
    _ck(5)
    with ExitStack() as es:
        mT = es.enter_context(nc.sbuf_tensor("mTo", [128, c.MC, TT], BF16))
        rbufs = [es.enter_context(nc.sbuf_tensor("rbo%d" % k, [128, NDJ, TT], F32)) for k in range(2)]
        psd = [es.enter_context(nc.psum_tensor("pso%d" % k, [128, 512], F32)) for k in range(NDJ)]
        rb_state = make_rb_state(rbufs, "rbo")
        mfree = None
        msem = tr.dsem("msem")
        for (t0, T, stream) in tiles_lat:
            t_m = tr.dma("sp", mT[:, :, 0:T], mixT.rearrange("(a p) n -> p a n", p=128)[:, :, t0:t0 + T], msem, waits=[mfree])
            mfree = down_proj(mT, c.MC, T, x1T, x2T, MV(1, 2, 0), t0, psd, rb_state, [t_m])
        tr.barrier(mk)

    _ck(6)
    ffn_phase(1, x2T, x3T, tiles_lat, 2)
    _ck(7)

    with ExitStack() as es:
        xbuf = es.enter_context(nc.sbuf_tensor("xbf", [128, KC, TT], F32))
        sqb = es.enter_context(nc.sbuf_tensor("sqf", [128, KC, TT], BF16))
        rstd = es.enter_context(nc.sbuf_tensor("rstdf", [128, TT], F32))
        finw = es.enter_context(nc.sbuf_tensor("finw", [128, KC], F32))
        pss = es.enter_context(nc.psum_tensor("psfin", [128, 512], F32))
        t_fw = tr.dma("sp", finw[:], I["finT"], ld)
        fsem = tr.dsem("fsem")
        xfree = None
        for (t0, T, stream) in tiles_lat:
            xv = xbuf[:, :, 0:T]
            t_x = tr.dma("sp", xv, x3T.rearrange("(kc p) n -> p kc n", p=128)[:, :, t0:t0 + T], xsems[0], waits=[xfree])
            t_sq = tr.op("act", lambda e: e.activation(out=sqb[:, :, 0:T], in_=xv, func=AF.Square), waits=[t_x, xfree])
            tr.wait("pe", t_sq)
            for kc in range(KC):
                mm = nc.tensor.matmul(pss[:, 0:T], lhsT=ones_bf, rhs=sqb[:, kc, 0:T], start=(kc == 0), stop=(kc == KC - 1))
            t_ss = tr.mark("pe", mm)
            t_r = tr.op("dve", lambda e: e.tensor_scalar(out=rstd[:, 0:T], in0=pss[:, 0:T], scalar1=1.0 / D, scalar2=EPS,
                                                         op0=ALU.mult, op1=ALU.add), waits=[t_ss, t_x, t_fw])
            t_r = tr.op("dve", lambda e: e.reciprocal(out=rstd[:, 0:T], in_=rstd[:, 0:T]), waits=[tr.op("act", lambda e: e.activation(out=rstd[:, 0:T], in_=rstd[:, 0:T], func=AF.Sqrt), waits=[t_r])])
            tr.wait("dve", t_r)
            for kc in range(KC):
                i_ = nc.vector.scalar_tensor_tensor(out=xv[:, kc, :], in0=xv[:, kc, :], scalar=finw[:, kc:kc + 1], in1=rstd[:, 0:T],
                                                    op0=ALU.mult, op1=ALU.mult)
            t_y = tr.mark("dve", i_)
            xfree = tr.dma("sp", yT.rearrange("(kc p) n -> p kc n", p=128)[:, :, t0:t0 + T], xv, fsem, waits=[t_y])
        tr.barrier(mk)
    assert ws.cur == len(ws.tiles), (ws.cur, len(ws.tiles))


def _rope_tables(c: Cfg, seg, head_dim):
    t = np.arange(seg * c.NT, (seg + 1) * c.NT)
    rows = (t // c.GW).astype(np.float32)
    cols = (t % c.GW).astype(np.float32)
    n_freq = head_dim // 4
    inv_freq = (np.float32(10000.0) ** (-np.arange(n_freq, dtype=np.float32) / np.float32(n_freq))).astype(np.float32)
    ang_r = rows[:, None] * inv_freq
    ang_c = cols[:, None] * inv_freq
    ang = np.concatenate([ang_r, ang_r, ang_c, ang_c], axis=-1).astype(np.float32)
    return np.cos(ang).astype(np.float32), np.sin(ang).astype(np.float32)


def _consts(c: Cfg):
    m = np.arange(128)[:, None]
    n = np.arange(128)[None, :]
    ident = np.eye(128, dtype=np.float32)
    P = np.zeros((128, 128), np.float32)
    for i in range(64):
        P[i, i + 64] = -1.0
        P[i + 64, i] = 1.0
    Pd = np.zeros((128, 128), np.float32)
    for base in (0, 64):
        for i in range(32):
            Pd[base + i, base + i + 32] = -1.0
            Pd[base + i + 32, base + i] = 1.0
    relf = np.where(m <= n, n - m, 0).astype(np.float32)
    maskf = (m <= n).astype(np.float32)
    relb = np.where(m > n, m - n, 0).astype(np.float32)
    maskb = (m > n).astype(np.float32)
    posqf = np.broadcast_to(n + 1, (128, 128)).astype(np.float32)
    posqb = np.broadcast_to(128 - n, (128, 128)).astype(np.float32)
    pkf = (127 - m).astype(np.float32)
    pkb = m.astype(np.float32)
    j = np.arange(c.LC)[None, :] * 128 + m
    ctxf = (c.L - 1 - j).astype(np.float32)
    ctxb = j.astype(np.float32)
    ii = np.arange(c.NCH)[None, :] * 128 + m
    posff = (c.NT - 1 - ii).astype(np.float32)
    posfb = ii.astype(np.float32)
    return np.ascontiguousarray(np.concatenate([ident, P.T, Pd.T, relf, maskf, relb, maskb, posqf, posqb, pkf, pkb, ctxf, ctxb, posff, posfb], axis=1))


def fm(v, nchunks):
    return np.ascontiguousarray(np.asarray(v, np.float32).reshape(nchunks, 128).T)


def make_in_maps(c: Cfg, inp):
    KC = c.KC
    f = lambda a: np.ascontiguousarray(np.asarray(a, dtype=np.float32))
    x, cc, ctx, c_ctx = f(inp["x"]), f(inp["c"]), f(inp["ctx"]), f(inp["c_ctx"])
    shared = {
        "nwT": fm(np.asarray(inp["norm_w"][0]).reshape(-1), 3 * KC),
        "finT": fm(inp["final_norm_w"], KC),
        "w_in": f(inp["w_in"][0]), "w_out": f(inp["w_out"][0]),
        "rdec": f(np.asarray(inp["ret_decay"][0]).reshape(1, -1)),
        "lamp": f(np.asarray(inp["diff_lambda"][0]).reshape(1, -1)),
        "subw": f(np.asarray(inp["diff_subln_w"][0]).reshape(1, -1)),
        "gnwT": fm(inp["ret_gn_w"][0], c.RW // 128),
        "consts": _consts(c),
    }
    for i in range(2):
        shared["wg%d" % i] = f(inp["ffn_gate"][0, i])
        shared["wu%d" % i] = f(inp["ffn_up"][0, i])
        shared["wd%d" % i] = f(inp["ffn_down"][0, i])
    maps = []
    NT = c.NT
    QC = 9 * KC // 4
    Q = QC * 128
    ada_w = np.asarray(inp["ada_w"][0], np.float32)
    ada_b = np.asarray(inp["ada_b"][0], np.float32)
    ada_q = [np.ascontiguousarray(ada_w[:, r * Q:(r + 1) * Q]) for r in range(4)]
    ada_bq = [fm(ada_b[r * Q:(r + 1) * Q], QC) for r in range(4)]
    for j in range(8):
        b, seg = j // 4, j % 4
        m = dict(shared)
        m["ada_wq"] = ada_q[seg]
        m["ada_bqT"] = ada_bq[seg]
        xs = x[b, seg * NT:(seg + 1) * NT, :]
        m["xin"] = np.ascontiguousarray(np.concatenate([xs, ctx[b]], axis=0).T)
        m["cT"] = np.ascontiguousarray(np.stack([fm(cc[b], KC), fm(c_ctx, KC)], axis=-1).reshape(128, KC * 2))
        cr, sr = _rope_tables(c, seg, 256)
        cd, sd = _rope_tables(c, seg, 128)
        m["cosr"] = np.ascontiguousarray(cr.T.reshape(2, 128, NT).transpose(1, 0, 2).reshape(128, 2 * NT))
        m["sinr"] = np.ascontiguousarray(sr.T.reshape(2, 128, NT).transpose(1, 0, 2).reshape(128, 2 * NT))
        m["cosd"] = np.ascontiguousarray(cd.T)
        m["sind"] = np.ascontiguousarray(sd.T)
        meta = np.zeros((1, 20), np.float32)
        for r in range(4):
            if r < seg:
                meta[0, r] = (seg - 1 - r) * NT
                meta[0, 4 + r] = 1.0
            if r > seg:
                meta[0, 9 + r] = (r - seg - 1) * NT
                meta[0, 13 + r] = 1.0
        meta[0, 8] = seg * NT
        meta[0, 17] = (3 - seg) * NT
        m["segmeta"] = meta
        maps.append(m)
    return maps


_NC_CACHE = {}


def run_cfg(c: Cfg, inp, stop=None, dbg=False, raw=False):
    key = (c.D, c.S, c.L, c.DFF, c.RH, c.DHH, c.TT, stop, dbg)
    if key not in _NC_CACHE:
        _NC_CACHE[key] = build(c, stop, dbg)
    nc = _NC_CACHE[key]
    maps = make_in_maps(c, inp)
    res = run_bass_kernel_spmd(nc, maps, core_ids=list(range(8)))
    if raw:
        return res.results
    B = 2
    out = np.empty((B, c.S, c.D), np.float32)
    for j in range(8):
        b, seg = j // 4, j % 4
        out[b, seg * c.NT:(seg + 1) * c.NT, :] = res.results[j]["yT"].T
    return out


def kernel(**inputs):
    return run_cfg(FULL, inputs)
```

```python
import numpy as np
from contextlib import ExitStack
import concourse.bass as bass
import concourse.mybir as mybir
from concourse.bass_utils import run_bass_kernel_spmd

F32 = mybir.dt.float32
BF16 = mybir.dt.bfloat16
AF = mybir.ActivationFunctionType
ALU = mybir.AluOpType
AX = mybir.AxisListType
EPS = 1e-6


class Cfg:
    def __init__(s, D, S, L, DFF, RH, DHH, TT, GW=64):
        s.D, s.S, s.L, s.DFF, s.RH, s.DHH, s.TT, s.GW = D, S, L, DFF, RH, DHH, TT, GW
        s.NT = S // 4
        s.KC = D // 128
        s.FC = DFF // 128
        s.RW = RH * 256
        s.DW = DHH * 256
        s.MIX = s.RW + s.DW
        s.MC = s.MIX // 128
        s.INC = 4 * s.RW + 3 * s.DW
        s.NTA = s.NT + L
        s.NCH = s.NT // 128
        s.LC = L // 128
        s.NKC = (L + 4 * s.NT) // 128
        s.NCONST = 128 * 9 + 2 + 2 * s.LC + 2 * s.NCH
        s.WT = 256 if D >= 256 else 128


FULL = Cfg(D=4096, S=8192, L=256, DFF=11008, RH=8, DHH=8, TT=512)


class DSem:
    def __init__(s, nc, name):
        s.h = nc.alloc_semaphore(name)
        s.v = 0


class Tracker:
    def __init__(s, nc):
        s.nc = nc
        s.E = {"pe": nc.tensor, "act": nc.scalar, "dve": nc.vector, "pool": nc.gpsimd, "sp": nc.sync}
        s.esem = {k: nc.alloc_semaphore("e_" + k) for k in s.E}
        s.ecnt = {k: 0 for k in s.E}
        s.seen = {}
        s.dsems = []
        s.semname = {}

    def dsem(s, name):
        d = DSem(s.nc, name)
        s.dsems.append(d)
        return d

    def wait(s, e, *toks):
        for tok in toks:
            if tok is None:
                continue
            if isinstance(tok, list):
                s.wait(e, *tok)
                continue
            sem, val, key = tok
            k = (e, key)
            if s.seen.get(k, 0) >= val:
                continue
            s.E[e].wait_ge(sem, val)
            s.seen[k] = val

    def mark(s, e, inst):
        s.ecnt[e] += 1
        inst.then_inc(s.esem[e], 1)
        return (s.esem[e], s.ecnt[e], "e_" + e)

    def op(s, e, inst_fn, waits=(), mark=True):
        s.wait(e, *waits)
        inst = inst_fn(s.E[e])
        if mark:
            return s.mark(e, inst)
        return None

    def dma(s, e, out, in_, sem, waits=()):
        s.wait(e, *waits)
        s.E[e].dma_start(out=out, in_=in_).then_inc(sem.h, 16)
        sem.v += 16
        return (sem.h, sem.v, "d_%d" % id(sem))

    def chain(s, e, fns, waits=()):
        tok = None
        for k, fn in enumerate(fns):
            tok = s.op(e, fn, waits=(list(waits) if k == 0 else [tok]))
        return tok

    def tot(s, sem):
        return (sem.h, sem.v, "d_%d" % id(sem))

    def barrier(s, markers):
        toks = []
        toks.append(s.op("act", lambda en: en.copy(out=markers["act"][:, 0:1], in_=markers["act"][:, 1:2])))
        for e in ("dve", "pool"):
            toks.append(s.op(e, lambda en, e=e: en.memset(markers[e][:, 0:1], 0.0)))
        toks.append((s.esem["pe"], s.ecnt["pe"], "e_pe"))
        for d in s.dsems:
            if d.v > 0:
                toks.append((d.h, d.v, "d_%d" % id(d)))
        for e in s.E:
            s.wait(e, *toks)


class WStream:
    def __init__(s, tr, nslots, slot_elems):
        s.tr = tr
        s.n = nslots
        s.slots = [tr.nc.alloc_sbuf_tensor("wslot%d" % i, [128, slot_elems], BF16) for i in range(nslots)]
        s.sems = [tr.dsem("wsem%d" % i) for i in range(nslots)]
        s.tiles = []
        s.issued = 0
        s.free_tok = {}
        s.load_tok = {}
        s.cur = 0
        s.slot_elems = slot_elems

    def add(s, dram_ap, a, b, tag):
        assert a * b <= s.slot_elems, (a, b)
        s.tiles.append((dram_ap, a, b, tag))

    def view(s, i):
        _, a, b, _ = s.tiles[i]
        return s.slots[i % s.n][:, 0:a * b].rearrange("p (a b) -> p a b", a=a)

    def next(s, tag):
        i = s.cur
        s.cur += 1
        assert s.tiles[i][3] == tag, (i, s.tiles[i][3], tag)
        upto = min(len(s.tiles), i + s.n)
        while s.issued < upto and (s.issued < s.n or (s.issued - s.n) in s.free_tok):
            j = s.issued
            waits = [s.free_tok[j - s.n]] if j >= s.n else []
            s.load_tok[j] = s.tr.dma("pool", s.view(j), s.tiles[j][0], s.sems[j % s.n], waits)
            s.issued += 1
        assert i in s.load_tok, ("weight tile not issued (too many tiles held)", i)
        return i, s.view(i), s.load_tok[i]

    def release(s, i, tok):
        s.free_tok[i] = tok


def wtile_cols(w, c0, ncols):
    return w.rearrange("(kc p) n -> p kc n", p=128)[:, :, c0:c0 + ncols]


def wtile_rows(w, r0, nrows, c0, ncols):
    return w[r0:r0 + nrows, c0:c0 + ncols].rearrange("(fc p) n -> p fc n", p=128)


class _Stop(Exception):
    pass


def build(c, stop=None, dbg=False):
    nc = bass.Bass("TRN2", target_bir_lowering=False)
    try:
        _build(c, nc, stop, dbg)
    except _Stop:
        pass
    return nc


def _build(c, nc, stop, dbg):
    tr = Tracker(nc)

    def _ck(k):
        if stop == k:
            raise _Stop()

    KC, FC, TT, NT, L, D = c.KC, c.FC, c.TT, c.NT, c.L, c.D
    WT = c.WT

    def din(name, shape, dt=F32):
        return nc.dram_tensor(name, list(shape), dt, kind="ExternalInput").ap()

    def dsc(name, shape, dt):
        if dbg and name in ("x1T", "x2T", "x3T", "rqT", "rkT", "rgT", "rv", "dqT", "cdkT", "cdv", "ckd", "cvd", "mixT"):
            return nc.dram_tensor(name, list(shape), dt, kind="ExternalOutput").ap()
        return nc.dram_tensor(name, list(shape), dt).ap()

    I = {}
    I["xin"] = din("xin", [D, c.NTA])
    I["cT"] = din("cT", [128, KC * 2])
    assert (9 * KC) % 4 == 0
    QC = 9 * KC // 4
    AQ = 256 if (QC * 128) % 256 == 0 else 128
    I["ada_wq"] = din("ada_wq", [D, QC * 128])
    I["ada_bqT"] = din("ada_bqT", [128, QC])
    mod_loc = dsc("mod_loc", [128, QC * 2], F32)
    mod_all = dsc("mod_all", [4 * 128, QC * 2], F32)
    I["nwT"] = din("nwT", [128, 3 * KC])
    I["finT"] = din("finT", [128, KC])
    for i in range(2):
        I["wg%d" % i] = din("wg%d" % i, [D, c.DFF])
        I["wu%d" % i] = din("wu%d" % i, [D, c.DFF])
        I["wd%d" % i] = din("wd%d" % i, [c.DFF, D])
    I["w_in"] = din("w_in", [D, c.INC])
    I["w_out"] = din("w_out", [c.MIX, D])
    I["rdec"] = din("rdec", [1, 2 * c.RH])
    I["lamp"] = din("lamp", [1, 512])
    I["subw"] = din("subw", [1, 256])
    I["gnwT"] = din("gnwT", [128, c.RW // 128])
    I["segmeta"] = din("segmeta", [1, 20])
    I["cosr"] = din("cosr", [128, 2 * NT])
    I["sinr"] = din("sinr", [128, 2 * NT])
    I["cosd"] = din("cosd", [128, NT])
    I["sind"] = din("sind", [128, NT])
    I["consts"] = din("consts", [128, c.NCONST])
    yT = nc.dram_tensor("yT", [D, NT], F32, kind="ExternalOutput").ap()

    x1T = dsc("x1T", [D, c.NTA], F32)
    x2T = dsc("x2T", [D, NT], F32)
    x3T = dsc("x3T", [D, NT], F32)
    rqT = dsc("rqT", [c.RW, NT], BF16)
    rkT = dsc("rkT", [c.RW, NT], BF16)
    rgT = dsc("rgT", [c.RW, NT], BF16)
    rv = dsc("rv", [NT, c.RW], BF16)
    dqT = dsc("dqT", [c.DW, NT], BF16)
    dk_loc = [dsc("dk_loc%d" % h, [256, NT], BF16) for h in range(c.DHH)]
    dv_loc = [dsc("dv_loc%d" % h, [NT, 256], BF16) for h in range(c.DHH)]
    dk_all = [dsc("dk_all%d" % h, [4 * 256, NT], BF16) for h in range(c.DHH)]
    dv_all = [dsc("dv_all%d" % h, [4 * NT, 256], BF16) for h in range(c.DHH)]
    cdkT = dsc("cdkT", [c.DW, L], BF16)
    cdv = dsc("cdv", [L, c.DW], BF16)
    ckd = dsc("ckd", [L, c.RW], BF16)
    cvd = dsc("cvd", [L, c.RW], BF16)
    st_loc = [dsc("st_loc%d" % h, [512, 256], F32) for h in range(c.RH)]
    st_all = [dsc("st_all%d" % h, [4 * 512, 256], F32) for h in range(c.RH)]
    mixT = dsc("mixT", [c.MIX, NT], BF16)

    DBW = 512 if D >= 512 else D
    NDJ = DBW // 128
    FSB = 16
    ws = WStream(tr, 4, max(KC * WT, min(FSB, max(FC, c.MC)) * DBW))
    cmat = nc.alloc_sbuf_tensor("cmat", [128, 4, 128], BF16)
    ones_bf, ident_bf, permr_bf, permd_bf = cmat[:, 0, :], cmat[:, 1, :], cmat[:, 2, :], cmat[:, 3, :]
    modv = nc.alloc_sbuf_tensor("modv", [128, 18, KC], F32)
    mk = {e: nc.alloc_sbuf_tensor("mk_" + e, [128, 2], F32) for e in ("act", "dve", "pool")}
    ld = tr.dsem("ld_misc")
    st_sem = tr.dsem("st_sem")

    def MV(sub, kind, stream):
        return modv[:, sub * 6 + kind * 2 + stream, :]

    tiles_lat = [(t0, min(TT, NT - t0), 0) for t0 in range(0, NT, TT)]
    tiles_pre = tiles_lat + [(NT, L, 1)]

    def plan_down(w, nin):
        for db in range(D // DBW):
            for f0 in range(0, nin, FSB):
                nf = min(FSB, nin - f0)
                ws.add(wtile_rows(w, f0 * 128, nf * 128, db * DBW, DBW), nf, DBW, "dn")

    def plan_ffn(i, tiles):
        pg = 0
        for _ in tiles:
            for p, f0 in enumerate(range(0, c.DFF, WT)):
                ws.add(wtile_cols(I["wg%d" % i], f0, WT), KC, WT, "g")
                ws.add(wtile_cols(I["wu%d" % i], f0, WT), KC, WT, "u")
                if i == 0 and ada_plan["g"] < NADA and ada_after_pair(pg):
                    plan_ada_tile()
                pg += 1
            plan_down(I["wd%d" % i], FC)

    RW, DW = c.RW, c.DW
    segs = [("rq", 0, RW), ("rk", RW, RW), ("rv", 2 * RW, RW), ("rg", 3 * RW, RW),
            ("dq", 4 * RW, DW), ("dk", 4 * RW + DW, DW), ("dv", 4 * RW + 2 * DW, DW)]

    def plan_proj():
        for (t0, T, stream) in tiles_pre:
            for (nm, c0, w) in segs:
                if stream == 1 and nm in ("rq", "rg", "dq"):
                    continue
                for cc in range(0, w, WT):
                    ws.add(wtile_cols(I["w_in"], c0 + cc, WT), KC, WT, "in")

    AW, NKH, KH = WT, 1, KC
    NADA = 0
    NADA0 = 0
    ada_plan = {"g": 0}

    def plan_ada_tile():
        g = ada_plan["g"]
        ada_plan["g"] += 1
        for kh in range(NKH):
            ws.add(wtile_rows(I["ada_w"], kh * KH * 128, KH * 128, g * AW, AW), KH, AW, "ada")

    NPAIR0 = len(tiles_pre) * (c.DFF // WT)

    def ada_after_pair(pg):
        return False

    for g in range(QC * 128 // AQ):
        ws.add(wtile_cols(I["ada_wq"], g * AQ, AQ), KC, AQ, "adaq")
    plan_ffn(0, tiles_pre)
    plan_proj()
    while ada_plan["g"] < NADA:
        plan_ada_tile()
    for _ in tiles_lat:
        plan_down(I["w_out"], c.MC)
    plan_ffn(1, tiles_lat)

    sc = nc.alloc_sbuf_tensor("sc", [128, KC, 2], BF16)
    nw = nc.alloc_sbuf_tensor("nw", [128, 3 * KC], F32)
    ada_state = {"g": 0}

    def ada_tile(*a_, **k_):
        raise AssertionError("in-stream ada tiles are disabled")

    groups = [[0, 1, 2, 3], [4, 5, 6, 7]]
    with ExitStack() as es:
        cst = es.enter_context(nc.sbuf_tensor("cst0", [128, 4 * 128], F32))
        cTt = es.enter_context(nc.sbuf_tensor("cTt", [128, KC * 2], F32))
        adabq = es.enter_context(nc.sbuf_tensor("adabq", [128, QC], F32))
        modq = es.enter_context(nc.sbuf_tensor("modq", [128, QC, 2], F32))
        mod4 = es.enter_context(nc.sbuf_tensor("mod4", [128, 4, QC * 2], F32))
        psa = [es.enter_context(nc.psum_tensor("psa%d" % k, [128, 512], F32)) for k in range(2)]

        t_c = tr.dma("sp", cTt[:], I["cT"], ld)
        tr.dma("sp", adabq[:], I["ada_bqT"], ld)
        tr.dma("sp", nw[:], I["nwT"], ld)
        tr.dma("sp", cst[:, 0:384], I["consts"][:, 0:384], ld)
        t_ld = tr.tot(ld)
        tr.op("dve", lambda e: e.memset(ones_bf, 1.0), mark=False)
        t_cm = tr.op("dve", lambda e: e.tensor_copy(out=cmat[:, 1:4, :], in_=cst[:, 0:384].rearrange("p (a b) -> p a b", a=3)),
                     waits=[t_ld])
        t_sc = tr.op("act", lambda e: e.activation(out=sc[:].rearrange("p a b -> p (a b)"), in_=cTt[:], func=AF.Silu),
                     waits=[t_ld])
        njq = AQ // 128
        ev_tok = [None, None]
        t_mod = None
        for g in range(QC * 128 // AQ):
            wi, wv, wtok = ws.next("adaq")
            ps = psa[g % 2]
            tr.wait("pe", wtok, t_sc, ev_tok[g % 2])
            for j in range(njq):
                for kc in range(KC):
                    mm = nc.tensor.matmul(ps[:, j * 2:(j + 1) * 2], lhsT=wv[:, kc, j * 128:(j + 1) * 128], rhs=sc[:, kc, :],
                                          start=(kc == 0), stop=(kc == KC - 1))
            t_pe = tr.mark("pe", mm)
            ws.release(wi, t_pe)
            ch0 = g * njq
            for col in range(2):
                t_mod = tr.op("dve", lambda e, col=col: e.tensor_tensor(
                    out=modq[:, ch0:ch0 + njq, col], in0=ps[:, 0:2 * njq].rearrange("p (j t) -> p j t", t=2)[:, :, col],
                    in1=adabq[:, ch0:ch0 + njq], op=ALU.add), waits=[t_pe, t_ld], mark=(col == 1))
            ev_tok[g % 2] = t_mod
        msem = tr.dsem("modsem")
        t_ms = tr.dma("sp", mod_loc, modq[:].rearrange("p n c -> p (n c)"), msem, waits=[t_mod])
        tr.wait("pool", t_ms)
        agm = nc.alloc_semaphore("agmod")
        nc.gpsimd.collective_compute("AllGather", ALU.bypass, replica_groups=groups, ins=[mod_loc.opt()], outs=[mod_all.opt()]).then_inc(agm, 1)
        tr.wait("sp", (agm, 1, "agmod"))
        t_ml = tr.dma("sp", mod4[:], mod_all.rearrange("(r p) x -> p r x", p=128), msem)
        mod = mod4[:].rearrange("p r (n c) -> p (r n) c", c=2)
        tr.wait("dve", t_ml, t_ld)
        for sub in range(3):
            for stream in range(2):
                nc.vector.scalar_tensor_tensor(out=MV(sub, 0, stream), in0=mod[:, (3 * sub + 1) * KC:(3 * sub + 2) * KC, stream],
                                               scalar=1.0, in1=nw[:, sub * KC:(sub + 1) * KC], op0=ALU.add, op1=ALU.mult)
                nc.vector.tensor_copy(out=MV(sub, 1, stream), in_=mod[:, (3 * sub) * KC:(3 * sub + 1) * KC, stream])
                nc.vector.tensor_scalar(out=MV(sub, 2, stream), in0=mod[:, (3 * sub + 2) * KC:(3 * sub + 3) * KC, stream],
                                        scalar1=(1.0 if sub == 1 else 0.5), scalar2=0.0, op0=ALU.mult, op1=ALU.add)
        tr.barrier(mk)

    NGX = 4 if KC >= 4 else 1
    xsems = [tr.dsem("xsem%d" % g) for g in range(NGX)]

    def norm_tile(src, t0, T, A, B, xbuf, hT, ps_ss, rstd, pre_waits):
        pre_waits = [w for w in pre_waits if w is not None]
        xv = xbuf[:, :, 0:T]
        srcv = src.rearrange("(kc p) n -> p kc n", p=128)[:, :, t0:t0 + T]
        NG = NGX
        gk = KC // NG
        lt = []
        for g in range(NG):
            lt.append(tr.dma("sp", xv[:, g * gk:(g + 1) * gk, :], srcv[:, g * gk:(g + 1) * gk, :], xsems[g], waits=pre_waits))
        sq = []
        for g in range(NG):
            sq.append(tr.op("act", lambda e, g=g: e.activation(out=hT[:, g * gk:(g + 1) * gk, 0:T], in_=xv[:, g * gk:(g + 1) * gk, :],
                                                             func=AF.Square), waits=[lt[g]] + pre_waits))
        for kc in range(KC):
            if kc % gk == 0:
                tr.wait("pe", sq[kc // gk], *pre_waits)
            mm = nc.tensor.matmul(ps_ss[:, 0:T], lhsT=ones_bf, rhs=hT[:, kc, 0:T], start=(kc == 0), stop=(kc == KC - 1))
        t_ss = tr.mark("pe", mm)
        t_r0 = tr.op("dve", lambda e: e.tensor_scalar(out=rstd[:, 0:T], in0=ps_ss[:, 0:T], scalar1=1.0 / D, scalar2=EPS,
                                                      op0=ALU.mult, op1=ALU.add), waits=[t_ss] + pre_waits)
        t_r0 = tr.op("dve", lambda e: e.reciprocal(out=rstd[:, 0:T], in_=rstd[:, 0:T]), waits=[tr.op("act", lambda e: e.activation(out=rstd[:, 0:T], in_=rstd[:, 0:T], func=AF.Sqrt), waits=[t_r0])])
        tr.wait("dve", t_r0, *lt)
        toks = []
        for g in range(NG):
            for kc in range(g * gk, (g + 1) * gk):
                i1 = nc.vector.scalar_tensor_tensor(out=xv[:, kc, :], in0=xv[:, kc, :], scalar=A[:, kc:kc + 1], in1=rstd[:, 0:T],
                                                    op0=ALU.mult, op1=ALU.mult)
            t1 = tr.mark("dve", i1)
            tr.wait("act", t1, t_ss)
            for kc in range(g * gk, (g + 1) * gk):
                i2 = nc.scalar.activation(out=hT[:, kc, 0:T], in_=xv[:, kc, :], func=AF.Identity, bias=B[:, kc:kc + 1], scale=1.0)
            toks.append(tr.mark("act", i2))
        return toks

    def make_rb_state(rbufs, name):
        return {"i": 0, "free": [None, None], "psfree": None, "bufs": rbufs,
                "lsem": [tr.dsem(name + "_l%d" % k) for k in range(2)], "ssem": [tr.dsem(name + "_s%d" % k) for k in range(2)]}

    def down_proj(aT, nin, T, resid, dst, G, t0, psd, rb_state, a_ready, tag="dn"):
        rv_ = resid.rearrange("(kc p) n -> p kc n", p=128)
        dv_ = dst.rearrange("(kc p) n -> p kc n", p=128)
        a_ready = [w for w in a_ready if w is not None]
        last_pe = None
        for db in range(D // DBW):
            k = rb_state["i"] % 2
            rb = rb_state["bufs"][k]
            rfree = rb_state["free"][k]
            rb_state["i"] += 1
            t_res = tr.dma("sp", rb[:, :, 0:T], rv_[:, db * NDJ:(db + 1) * NDJ, t0:t0 + T], rb_state["lsem"][k], waits=[rfree])
            nsub = (nin + FSB - 1) // FSB
            ev = None
            for si in range(nsub):
                f0 = si * FSB
                nf = min(FSB, nin - f0)
                wi, wv, wtok = ws.next(tag)
                tr.wait("pe", wtok, *a_ready)
                if si == 0:
                    tr.wait("pe", rb_state["psfree"])
                for dj in range(NDJ):
                    for fl in range(nf):
                        f = f0 + fl
                        mm = nc.tensor.matmul(psd[dj][:, 0:T], lhsT=wv[:, fl, dj * 128:(dj + 1) * 128], rhs=aT[:, f, 0:T],
                                              start=(f == 0), stop=(f == nin - 1))
                    if si == nsub - 1:
                        pt = tr.mark("pe", mm)
                        kc = db * NDJ + dj
                        ev = tr.op("dve", lambda e, dj=dj, kc=kc: e.scalar_tensor_tensor(
                            out=rb[:, dj, 0:T], in0=psd[dj][:, 0:T], scalar=G[:, kc:kc + 1], in1=rb[:, dj, 0:T],
                            op0=ALU.mult, op1=ALU.add), waits=[pt, t_res])
                last_pe = tr.mark("pe", mm) if si < nsub - 1 else pt
                ws.release(wi, last_pe)
            rb_state["psfree"] = ev
            rb_state["free"][k] = tr.dma("sp", dv_[:, db * NDJ:(db + 1) * NDJ, t0:t0 + T], rb[:, :, 0:T], rb_state["ssem"][k], waits=[ev])
        return last_pe

    def ffn_phase(i, src, dst, tiles, sub):
        with ExitStack() as es:
            aT = es.enter_context(nc.sbuf_tensor("aT%d" % i, [128, FC, TT], BF16))
            hT = es.enter_context(nc.sbuf_tensor("hT%d" % i, [128, KC, TT], BF16))
            rbufs = [es.enter_context(nc.sbuf_tensor("rb%d_%d" % (i, k), [128, NDJ, TT], F32)) for k in range(2)]
            rstd = es.enter_context(nc.sbuf_tensor("rstd%d" % i, [128, TT], F32))
            sg1 = es.enter_context(nc.sbuf_tensor("sg%d" % i, [128, TT], F32))
            sg = [sg1, sg1]
            ps = [es.enter_context(nc.psum_tensor("psf%d_%d" % (i, k), [128, 512], F32)) for k in range(8)]
            assert KC * TT * 2 <= FC * TT, "x tile must fit inside aT"
            xbuf = aT[:].rearrange("p a b -> p (a b)")[:, 0:KC * TT * 2].bitcast(F32).rearrange("p (a b) -> p a b", a=KC)
            rb_state = make_rb_state(rbufs, "rbf%d" % i)
            pg_ = [0]
            last_ada = [None]
            a_free = None
            gu_free = [None, None]
            sg_free = [None, None]
            cnt = 0
            for (t0, T, stream) in tiles:
                A, B, G = MV(sub, 0, stream), MV(sub, 1, stream), MV(sub, 2, stream)
                pre = [a_free, rb_state["psfree"]]
                h_ready = norm_tile(src, t0, T, A, B, xbuf, hT, ps[4], rstd, pre)
                a_toks = []
                for p_, f0 in enumerate(range(0, c.DFF, WT)):
                    gi, gv, gtok = ws.next("g")
                    ui, uv, utok = ws.next("u")
                    for j in range(WT // 128):
                        f = f0 // 128 + j
                        if f >= FC:
                            break
                        b = cnt % 2
                        cnt += 1
                        psg, psu = ps[2 * b], ps[2 * b + 1]
                        tr.wait("pe", gtok, utok, gu_free[b], *h_ready)
                        for kc in range(KC):
                            nc.tensor.matmul(psg[:, 0:T], lhsT=gv[:, kc, j * 128:(j + 1) * 128], rhs=hT[:, kc, 0:T],
                                             start=(kc == 0), stop=(kc == KC - 1))
                        for kc in range(KC):
                            mm = nc.tensor.matmul(psu[:, 0:T], lhsT=uv[:, kc, j * 128:(j + 1) * 128], rhs=hT[:, kc, 0:T],
                                                  start=(kc == 0), stop=(kc == KC - 1))
                        pt = tr.mark("pe", mm)
                        t_s = tr.op("act", lambda e: e.activation(out=sg[b][:, 0:T], in_=psg[:, 0:T], func=AF.Silu),
                                    waits=[pt, sg_free[b]])
                        t_a = tr.op("dve", lambda e: e.tensor_tensor(out=aT[:, f, 0:T], in0=sg[b][:, 0:T], in1=psu[:, 0:T],
                                                                      op=ALU.mult), waits=[t_s, pt] + h_ready)
                        gu_free[b] = t_a
                        sg_free[0] = t_a
                        sg_free[1] = t_a
                        a_toks = [t_a]
                    ws.release(gi, pt)
                    ws.release(ui, pt)
                    if i == 0 and ada_state["g"] < NADA and ada_after_pair(pg_[0]):
                        last_ada[0] = ada_tile(ps[4:8], [rb_state["psfree"]])
                    pg_[0] += 1
                a_free = down_proj(aT, FC, T, src, dst, G, t0, ps[4:4 + NDJ], rb_state, a_toks + [gu_free[0], gu_free[1], last_ada[0]])
            tr.barrier(mk)

    _ck(0)
    ffn_phase(0, I["xin"], x1T, tiles_pre, 0)
    _ck(1)

    with ExitStack() as es:
        hT = es.enter_context(nc.sbuf_tensor("hTp", [128, KC, TT], BF16))
        xbuf = es.enter_context(nc.sbuf_tensor("xbp", [128, KC, TT], F32))
        rstd = es.enter_context(nc.sbuf_tensor("rstdp", [128, TT], F32))
        ropes = es.enter_context(nc.sbuf_tensor("ropes", [128, 6, TT], F32))
        xb = [es.enter_context(nc.sbuf_tensor("xbb%d" % k, [128, TT], BF16)) for k in range(2)]
        t1b = [es.enter_context(nc.sbuf_tensor("t1b%d" % k, [128, TT], F32)) for k in range(2)]
        t2b = [es.enter_context(nc.sbuf_tensor("t2b%d" % k, [128, TT], F32)) for k in range(2)]
        ob = [es.enter_context(nc.sbuf_tensor("ob%d" % k, [128, TT], BF16)) for k in range(3)]
        ps = [es.enter_context(nc.psum_tensor("psp%d" % k, [128, 512], F32)) for k in range(8)]
        rsem = tr.dsem("ropesem")
        t_rope = None
        st9 = {"obi": 0, "rope_free": None}
        pending_rope = []
        ob_free = [None, None, None]
        obsem = [tr.dsem("obsem%d" % k) for k in range(3)]
        xb_free = [None, None]
        t12_free = [None, None]
        psx_free = [None, None]
        psr_free = [None, None]
        ci = 0
        h_free = None
        last_pe_proj = None
        for (t0, T, stream) in tiles_pre:
            A, B = MV(1, 0, stream), MV(1, 1, stream)
            pre = [h_free]
            h_ready = norm_tile(x1T, t0, T, A, B, xbuf, hT, ps[7], rstd, pre)
            if stream == 0:
                tr.wait("sp", st9["rope_free"])
                cr_ = I["cosr"].rearrange("p (a n) -> p a n", a=2)
                sr_ = I["sinr"].rearrange("p (a n) -> p a n", a=2)
                tr.dma("sp", ropes[:, 0:2, 0:T], cr_[:, :, t0:t0 + T], rsem)
                tr.dma("sp", ropes[:, 2:4, 0:T], sr_[:, :, t0:t0 + T], rsem)
                tr.dma("sp", ropes[:, 4, 0:T], I["cosd"][:, t0:t0 + T], rsem)
                tr.dma("sp", ropes[:, 5, 0:T], I["sind"][:, t0:t0 + T], rsem)
                t_rope = tr.tot(rsem)
            if stop == 20:
                tr.barrier(mk)
                raise _Stop()
            for si_, (nm, c0, w) in enumerate(segs):
                if stop is not None and 21 <= stop <= 27 and si_ >= stop - 20:
                    tr.barrier(mk)
                    raise _Stop()
                if stream == 1 and nm in ("rq", "rg", "dq"):
                    continue
                token_major = nm in ("rv", "dv") or (stream == 1 and nm == "rk")
                for cc in range(0, w, WT):
                    wi, wv, wtok = ws.next("in")
                    if token_major:
                        assert WT == 256
                        dstt = {"rv": rv, "dv": dv_loc[cc // 256], "rk": ckd}[nm] if stream == 0 else {"rv": cvd, "dv": cdv, "rk": ckd}[nm]
                        dcol = 0 if (stream == 0 and nm == "dv") else cc
                        for ts in range(T // 128):
                            b = ci % 2
                            ci += 1
                            tr.wait("pe", wtok, psx_free[b], *h_ready)
                            for kc in range(KC):
                                mm = nc.tensor.matmul(ps[b][:, 0:WT], lhsT=hT[:, kc, ts * 128:(ts + 1) * 128], rhs=wv[:, kc, :],
                                                      start=(kc == 0), stop=(kc == KC - 1))
                            pt = tr.mark("pe", mm)
                            while pending_rope:
                                pending_rope.pop(0)()
                            o = ob[st9["obi"] % 3]
                            ofree = ob_free[st9["obi"] % 3]
                            scale = (1.0 / 16.0) if nm == "rk" else 1.0
                            t_o = tr.op("act", lambda e, o=o, b=b, scale=scale: e.activation(
                                out=o[:, 0:WT], in_=ps[b][:, 0:WT], func=AF.Identity, scale=scale), waits=[pt, ofree])
                            psx_free[b] = t_o
                            trow = (t0 - NT if stream == 1 else t0) + ts * 128
                            ob_free[st9["obi"] % 3] = tr.dma("sp", dstt[trow:trow + 128, dcol:dcol + WT], o[:, 0:WT], obsem[st9["obi"] % 3], waits=[t_o])
                            st9["obi"] += 1
                    else:
                        for j in range(WT // 128):
                            col = cc + j * 128
                            b = ci % 2
                            ci += 1
                            tr.wait("pe", wtok, psx_free[b], *h_ready)
                            for kc in range(KC):
                                mm = nc.tensor.matmul(ps[b][:, 0:T], lhsT=wv[:, kc, j * 128:(j + 1) * 128], rhs=hT[:, kc, 0:T],
                                                      start=(kc == 0), stop=(kc == KC - 1))
                            pt = tr.mark("pe", mm)
                            while pending_rope:
                                pending_rope.pop(0)()
                            o = ob[st9["obi"] % 3]
                            ofree = ob_free[st9["obi"] % 3]
                            rope = (stream == 0) and nm in ("rq", "rk", "dq", "dk")
                            if not rope:
                                if nm == "rg":
                                    t_o = tr.op("act", lambda e, o=o, b=b: e.activation(out=o[:, 0:T], in_=ps[b][:, 0:T], func=AF.Silu),
                                                waits=[pt, ofree])
                                    dstt = rgT
                                else:
                                    t_o = tr.op("act", lambda e, o=o, b=b: e.activation(out=o[:, 0:T], in_=ps[b][:, 0:T], func=AF.Identity),
                                                waits=[pt, ofree])
                                    dstt = cdkT
                                psx_free[b] = t_o
                                tcol = t0 - NT if stream == 1 else t0
                            else:
                                if nm in ("rq", "rk"):
                                    chunk = (col // 128) % 2
                                    cos = ropes[:, chunk, 0:T]
                                    sin = ropes[:, 2 + chunk, 0:T]
                                    perm = permr_bf
                                    scale = 1.0 if nm == "rq" else 1.0 / 16.0
                                    dstt = rqT if nm == "rq" else rkT
                                else:
                                    cos = ropes[:, 4, 0:T]
                                    sin = ropes[:, 5, 0:T]
                                    perm = permd_bf
                                    scale = 128.0 ** -0.5 if nm == "dq" else 1.0
                                    dstt = dqT if nm == "dq" else dk_loc[col // 256]
                                t_xb = tr.op("act", lambda e, b=b: e.activation(out=xb[b][:, 0:T], in_=ps[b][:, 0:T], func=AF.Identity),
                                             waits=[pt, xb_free[b]])
                                t_1 = tr.op("dve", lambda e, b=b, cos=cos, scale=scale: e.scalar_tensor_tensor(
                                    out=t1b[b][:, 0:T], in0=ps[b][:, 0:T], scalar=scale, in1=cos, op0=ALU.mult, op1=ALU.mult),
                                    waits=[pt, t_xb, t12_free[b], t_rope])
                                psx_free[b] = t_1

                                def rope_rest(b=b, T=T, perm=perm, sin=sin, scale=scale, t_xb=t_xb, t_1=t_1, dstt=dstt, col=col, nm=nm, t0=t0):
                                    tr.wait("pe", t_xb, psr_free[b])
                                    mm_ = nc.tensor.matmul(ps[2 + b][:, 0:T], lhsT=perm, rhs=xb[b][:, 0:T], start=True, stop=True)
                                    pr = tr.mark("pe", mm_)
                                    xb_free[b] = pr
                                    t_2 = tr.op("dve", lambda e: e.scalar_tensor_tensor(
                                        out=t2b[b][:, 0:T], in0=ps[2 + b][:, 0:T], scalar=scale, in1=sin, op0=ALU.mult, op1=ALU.mult), waits=[pr])
                                    psr_free[b] = t_2
                                    st9["rope_free"] = t_2
                                    k_ = st9["obi"] % 3
                                    o_ = ob[k_]
                                    t_o_ = tr.op("dve", lambda e: e.tensor_tensor(out=o_[:, 0:T], in0=t1b[b][:, 0:T], in1=t2b[b][:, 0:T], op=ALU.add),
                                                 waits=[t_1, t_2, ob_free[k_]])
                                    t12_free[b] = t_o_
                                    drow_ = (col % 256) if nm == "dk" else col
                                    ob_free[k_] = tr.dma("sp", dstt[drow_:drow_ + 128, t0:t0 + T], o_[:, 0:T], obsem[k_], waits=[t_o_])
                                    st9["obi"] += 1
                                pending_rope.append(rope_rest)
                                continue
                                tcol = t0
                            drow = (col % 256) if (stream == 0 and nm == "dk") else col
                            ob_free[st9["obi"] % 3] = tr.dma("sp", dstt[drow:drow + 128, tcol:tcol + T], o[:, 0:T], obsem[st9["obi"] % 3], waits=[t_o])
                            st9["obi"] += 1
                    ws.release(wi, pt)
                    last_pe_proj = pt
            while pending_rope:
                pending_rope.pop(0)()
            h_free = (tr.esem["pe"], tr.ecnt["pe"], "e_pe")
        tr.barrier(mk)

    _ck(2)
    groups = [[0, 1, 2, 3], [4, 5, 6, 7]]
    t_kv = []
    for h in range(c.DHH):
        ksem = nc.alloc_semaphore("agk%d" % h)
        vsem = nc.alloc_semaphore("agv%d" % h)
        nc.gpsimd.collective_compute("AllGather", ALU.bypass, replica_groups=groups, ins=[dk_loc[h].opt()], outs=[dk_all[h].opt()]).then_inc(ksem, 1)
        nc.gpsimd.collective_compute("AllGather", ALU.bypass, replica_groups=groups, ins=[dv_loc[h].opt()], outs=[dv_all[h].opt()]).then_inc(vsem, 1)
        t_kv.append([(ksem, 1, "agk%d" % h), (vsem, 1, "agv%d" % h)])
    if stop == 3:
        tr.wait("pool", t_kv)
        tr.barrier(mk)
    _ck(3)
    RH = c.RH
    NCH = c.NCH
    RH = c.RH
    NCH = c.NCH
    lg = nc.alloc_sbuf_tensor("lg", [128, 2 * RH], F32)
    ksc = nc.alloc_sbuf_tensor("ksc", [128, 4, RH], F32)
    cwt = nc.alloc_sbuf_tensor("cwt", [128, 2, c.LC, RH], F32)
    coef = nc.alloc_sbuf_tensor("coef", [128, 10, RH], F32)
    t_st = []

    def ret_phase(part):
        with ExitStack() as es:
            cst = es.enter_context(nc.sbuf_tensor("cstr_p%d" % part, [128, c.NCONST], F32))
            rdec = es.enter_context(nc.sbuf_tensor("rdec_sb_p%d" % part, [128, 2 * RH], F32))
            meta = es.enter_context(nc.sbuf_tensor("meta_p%d" % part, [128, 20], F32))
            dint = es.enter_context(nc.sbuf_tensor("dint_p%d" % part, [128, 128], F32))
            dqf = es.enter_context(nc.sbuf_tensor("dqf_p%d" % part, [128, 128], F32))
            dqb = es.enter_context(nc.sbuf_tensor("dqb_p%d" % part, [128, 128], F32))
            tmpd = es.enter_context(nc.sbuf_tensor("tmpd_p%d" % part, [128, 128], F32))
            gnw = es.enter_context(nc.sbuf_tensor("gnw_p%d" % part, [128, c.RW // 128], F32))
            qT = es.enter_context(nc.sbuf_tensor("qTr_p%d" % part, [128, 2, NT], BF16))
            kT = es.enter_context(nc.sbuf_tensor("kTr_p%d" % part, [128, 2, NT], BF16))
            vt = es.enter_context(nc.sbuf_tensor("vtr_p%d" % part, [128, NCH, 256], BF16))
            kf = es.enter_context(nc.sbuf_tensor("kfr_p%d" % part, [128, NCH, 256], BF16))
            kb = es.enter_context(nc.sbuf_tensor("kbr_p%d" % part, [128, NCH, 256], BF16))
            qf = es.enter_context(nc.sbuf_tensor("qfr_p%d" % part, [128, 2, NT], BF16))
            qb = es.enter_context(nc.sbuf_tensor("qbr_p%d" % part, [128, 2, NT], BF16))
            gT = qf
            ckt = es.enter_context(nc.sbuf_tensor("ckt_p%d" % part, [128, c.LC, 256], BF16))
            cvt = es.enter_context(nc.sbuf_tensor("cvt_p%d" % part, [128, c.LC, 256], BF16))
            ckw = es.enter_context(nc.sbuf_tensor("ckw_p%d" % part, [128, 2, c.LC, 256], BF16))
            Sloc = es.enter_context(nc.sbuf_tensor("Sloc_p%d" % part, [128, 2, 2, 256], F32))
            Sg = es.enter_context(nc.sbuf_tensor("Sg_p%d" % part, [128, 4, 2, 2, 256], F32))
            Sst = es.enter_context(nc.sbuf_tensor("Sst_p%d" % part, [128, 2, 2, 256], F32))
            Sbf = es.enter_context(nc.sbuf_tensor("Sbf_p%d" % part, [128, 2, 2, 256], BF16))
            sdt = es.enter_context(nc.sbuf_tensor("sdt_p%d" % part, [128, 128], BF16))
            oacc = es.enter_context(nc.sbuf_tensor("oacc_p%d" % part, [128, 2, NT], F32))
            osq = es.enter_context(nc.sbuf_tensor("osq_p%d" % part, [128, 2, 512], BF16))
            obf = es.enter_context(nc.sbuf_tensor("obf_p%d" % part, [128, 2, 512], BF16))
            stat = es.enter_context(nc.sbuf_tensor("stat_p%d" % part, [128, 3, 512], F32))
            ymix = es.enter_context(nc.sbuf_tensor("ymix_p%d" % part, [128, 2, NT], BF16))
            wfull = es.enter_context(nc.sbuf_tensor("wfull_p%d" % part, [128, 2, RH, NCH], F32))
            ps = [es.enter_context(nc.psum_tensor("psr%d_%d" % (k, part), [128, 512], F32)) for k in range(7)]
            pst = es.enter_context(nc.psum_tensor("pstr%d" % part, [128, 1024], BF16))

            tr.dma("sp", cst[:], I["consts"], ld)
            tr.dma("sp", rdec[:], I["rdec"].partition_broadcast(128)[:, 0, :], ld)
            tr.dma("sp", meta[:], I["segmeta"].partition_broadcast(128)[:, 0, :], ld)
            tr.dma("sp", gnw[:], I["gnwT"], ld)
            t_l = tr.tot(ld)
            OFF = 384
            relf, maskf = cst[:, OFF:OFF + 128], cst[:, OFF + 128:OFF + 256]
            relb, maskb = cst[:, OFF + 256:OFF + 384], cst[:, OFF + 384:OFF + 512]
            posqf, posqb = cst[:, OFF + 512:OFF + 640], cst[:, OFF + 640:OFF + 768]
            pkf, pkb = cst[:, OFF + 768:OFF + 769], cst[:, OFF + 769:OFF + 770]
            ctxf = cst[:, OFF + 770:OFF + 770 + c.LC]
            ctxb = cst[:, OFF + 770 + c.LC:OFF + 770 + 2 * c.LC]
            t_setup = None
            t5_ = None
            if part == 1:
                t_e = tr.op("act", lambda e: e.activation(out=lg[:], in_=rdec[:], func=AF.Exp), waits=[t_l])
                t_lg = tr.op("dve", lambda e: e.tensor_scalar(out=lg[:], in0=lg[:], scalar1=-1.0, scalar2=0.0, op0=ALU.mult, op1=ALU.add), waits=[t_e])
                tr.wait("act", t_lg)
                for h in range(RH):
                    lf, lb = lg[:, h:h + 1], lg[:, RH + h:RH + h + 1]
                    nc.scalar.activation(out=ksc[:, 0, h:h + 1], in_=pkf, func=AF.Exp, scale=lf)
                    nc.scalar.activation(out=ksc[:, 1, h:h + 1], in_=pkb, func=AF.Exp, scale=lb)
                    nc.scalar.activation(out=ksc[:, 2, h:h + 1], in_=lf, func=AF.Exp, scale=128.0)
                    nc.scalar.activation(out=ksc[:, 3, h:h + 1], in_=lb, func=AF.Exp, scale=128.0)
                    for jc in range(c.LC):
                        nc.scalar.activation(out=cwt[:, 0, jc, h:h + 1], in_=ctxf[:, jc:jc + 1], func=AF.Exp, scale=lf)
                        nc.scalar.activation(out=cwt[:, 1, jc, h:h + 1], in_=ctxb[:, jc:jc + 1], func=AF.Exp, scale=lb)
                    for r in range(4):
                        nc.scalar.activation(out=coef[:, r, h:h + 1], in_=meta[:, r:r + 1], func=AF.Exp, scale=lf)
                        nc.scalar.activation(out=coef[:, 5 + r, h:h + 1], in_=meta[:, 9 + r:10 + r], func=AF.Exp, scale=lb)
                    nc.scalar.activation(out=coef[:, 4, h:h + 1], in_=meta[:, 8:9], func=AF.Exp, scale=lf)
                    i_ = nc.scalar.activation(out=coef[:, 9, h:h + 1], in_=meta[:, 17:18], func=AF.Exp, scale=lb)
                t3_ = tr.mark("act", i_)
                tr.wait("dve", t3_)
                for r in range(4):
                    nc.vector.tensor_scalar(out=coef[:, r, :], in0=coef[:, r, :], scalar1=meta[:, 4 + r:5 + r], scalar2=0.0, op0=ALU.mult, op1=ALU.add)
                    i_ = nc.vector.tensor_scalar(out=coef[:, 5 + r, :], in0=coef[:, 5 + r, :], scalar1=meta[:, 13 + r:14 + r], scalar2=0.0,
                                                 op0=ALU.mult, op1=ALU.add)
                t5_ = tr.mark("dve", i_)
                t_setup = t5_
            for e_ in ("act", "pe", "pool", "dve", "sp"):
                tr.wait(e_, t_setup, t5_, t_l)

            hl = tr.dsem("ret_ld%d" % part)
            slsem = tr.dsem("slsem%d" % part)
            gsem = tr.dsem("gsem%d" % part)
            ysem = tr.dsem("ysem%d" % part)

            def load_head(h, extra=None):
                tr.dma("sp", qT[:], rqT[h * 256:(h + 1) * 256, :].rearrange("(a p) n -> p a n", p=128), hl)
                tr.dma("sp", kT[:], rkT[h * 256:(h + 1) * 256, :].rearrange("(a p) n -> p a n", p=128), hl)
                tr.dma("sp", vt[:], rv[:, h * 256:(h + 1) * 256].rearrange("(a p) n -> p a n", p=128), hl)
                tr.dma("sp", ckt[:], ckd[:, h * 256:(h + 1) * 256].rearrange("(a p) n -> p a n", p=128), hl)
                tr.dma("sp", cvt[:], cvd[:, h * 256:(h + 1) * 256].rearrange("(a p) n -> p a n", p=128), hl)
                if extra is not None:
                    extra()
                return tr.tot(hl)

            def prep_head(h, t_ld_h, sc_f=None, sc_b=None):
                tk = None
                for i in range(NCH):
                    tr.wait("pe", t_ld_h, tk)
                    for dc in range(2):
                        mm = nc.tensor.transpose(pst[:, dc * 128:(dc + 1) * 128], kT[:, dc, i * 128:(i + 1) * 128], ident_bf)
                    pt = tr.mark("pe", mm)
                    s_f = ksc[:, 0, h:h + 1] if sc_f is None else sc_f(i)
                    s_b = ksc[:, 1, h:h + 1] if sc_b is None else sc_b(i)
                    tr.op("act", lambda e, i=i: e.activation(out=kf[:, i, :], in_=pst[:, 0:256], func=AF.Identity, scale=s_f),
                          waits=[pt], mark=False)
                    tk = tr.op("act", lambda e, i=i: e.activation(out=kb[:, i, :], in_=pst[:, 0:256], func=AF.Identity, scale=s_b))
                return tk

            def state_update(dirn, i, h, first, waits):
                kk = kf if dirn == 0 else kb
                bank = ps[5 + dirn]
                waits = [w for w in waits if w is not None]
                tr.wait("pe", *waits)
                for dc in range(2):
                    mm = nc.tensor.matmul(bank[:, dc * 256:(dc + 1) * 256], lhsT=kk[:, i, dc * 128:(dc + 1) * 128], rhs=vt[:, i, :],
                                          start=True, stop=True)
                pt = tr.mark("pe", mm)
                tr.wait("dve", pt, *waits)
                for dc in range(2):
                    if first:
                        i_ = nc.vector.tensor_copy(out=Sst[:, dirn, dc, :], in_=bank[:, dc * 256:(dc + 1) * 256])
                    else:
                        i_ = nc.vector.scalar_tensor_tensor(out=Sst[:, dirn, dc, :], in0=Sst[:, dirn, dc, :],
                                                            scalar=ksc[:, 2 + dirn, h:h + 1], in1=bank[:, dc * 256:(dc + 1) * 256],
                                                            op0=ALU.mult, op1=ALU.add)
                return tr.mark("dve", i_)

            if part == 1:
                OFP = OFF + 770 + 2 * c.LC
                posff, posfb = cst[:, OFP:OFP + NCH], cst[:, OFP + NCH:OFP + 2 * NCH]
                tr.wait("act", t_setup, t5_, t_l)
                for h in range(RH):
                    nc.scalar.activation(out=wfull[:, 0, h, :], in_=posff, func=AF.Exp, scale=lg[:, h:h + 1])
                    i_ = nc.scalar.activation(out=wfull[:, 1, h, :], in_=posfb, func=AF.Exp, scale=lg[:, RH + h:RH + h + 1])
                t_wf = tr.mark("act", i_)
                tr.wait("act", t_wf)
                t_slst = None
                t_ld_next = None
                t_cp = None
                for h in range(RH):
                    t_ld_h = load_head(h) if h == 0 else t_ld_next
                    tk = prep_head(h, t_ld_h, sc_f=lambda i, h=h: wfull[:, 0, h, i:i + 1], sc_b=lambda i, h=h: wfull[:, 1, h, i:i + 1])
                    n_rest = NADA - ada_state["g"]
                    for _ in range(n_rest if h == RH - 1 else min(n_rest, -(-(NADA - NADA0) // RH))):
                        ada_tile(ps[0:4])
                    tr.wait("pe", tk, t_cp)
                    for dirn in range(2):
                        kk = kf if dirn == 0 else kb
                        for dc in range(2):
                            for i in range(NCH):
                                mm = nc.tensor.matmul(ps[5 + dirn][:, dc * 256:(dc + 1) * 256], lhsT=kk[:, i, dc * 128:(dc + 1) * 128],
                                                      rhs=vt[:, i, :], start=(i == 0), stop=(i == NCH - 1))
                    pt = tr.mark("pe", mm)
                    tr.wait("dve", pt, t_slst)
                    for dirn in range(2):
                        i_ = nc.vector.tensor_copy(out=Sloc[:, dirn, :, :].rearrange("p c e -> p (c e)"), in_=ps[5 + dirn][:, 0:512])
                    t_cp = tr.mark("dve", i_)
                    t_slst = tr.dma("sp", st_loc[h].rearrange("(a p) e -> p a e", p=128), Sloc[:].rearrange("p d c e -> p (d c) e"), slsem,
                                    waits=[t_cp])
                    tr.wait("pool", t_slst)
                    ssem = nc.alloc_semaphore("ags%d" % h)
                    nc.gpsimd.collective_compute("AllGather", ALU.bypass, replica_groups=groups, ins=[st_loc[h].opt()],
                                                 outs=[st_all[h].opt()]).then_inc(ssem, 1)
                    t_st.append((ssem, 1, "ags%d" % h))
                    tr.wait("sp", tk, pt)
                    if h + 1 < RH:
                        t_ld_next = load_head(h + 1)
            if part == 2:
                t_head_free = None
                o_free = None
                y_free = None
                for h in range(RH):
                    tr.wait("sp", t_head_free, t_st[h])
                    stall_h = st_all[h].rearrange("(r a p) e -> p r a e", r=4, p=128)
                    t_ld_h = load_head(h, extra=lambda: [tr.dma("sp", Sg[:, r, :, :, :].rearrange("p d c e -> p (d c) e"), stall_h[:, r], hl)
                                                         for r in range(4)])
                    tk = prep_head(h, t_ld_h)
                    lf, lb = lg[:, h:h + 1], lg[:, RH + h:RH + h + 1]
                    tr.wait("act", t_head_free)
                    nc.scalar.activation(out=dint[:], in_=relf, func=AF.Exp, scale=lf)
                    nc.scalar.activation(out=tmpd[:], in_=relb, func=AF.Exp, scale=lb)
                    nc.scalar.activation(out=dqf[:], in_=posqf, func=AF.Exp, scale=lf)
                    i_ = nc.scalar.activation(out=dqb[:], in_=posqb, func=AF.Exp, scale=lb)
                    t_dq = tr.mark("act", i_)
                    tr.wait("dve", t_dq, t_head_free)
                    nc.vector.tensor_tensor(out=dint[:], in0=dint[:], in1=maskf, op=ALU.mult)
                    i_ = nc.vector.tensor_tensor(out=tmpd[:], in0=tmpd[:], in1=maskb, op=ALU.mult)
                    t_di = tr.op("dve", lambda e: e.tensor_tensor(out=dint[:], in0=dint[:], in1=tmpd[:], op=ALU.add), waits=[tr.mark("dve", i_)])
                    tr.wait("dve", t_ld_h, t_head_free)
                    for dirn in range(2):
                        for jc in range(c.LC):
                            i_ = nc.vector.tensor_scalar(out=ckw[:, dirn, jc, :], in0=ckt[:, jc, :], scalar1=cwt[:, dirn, jc, h:h + 1],
                                                         scalar2=0.0, op0=ALU.mult, op1=ALU.add)
                    t_cw = tr.mark("dve", i_)
                    t_s0 = None
                    for dirn in range(2):
                        tr.wait("pe", t_cw, t_s0)
                        for dc in range(2):
                            for jc in range(c.LC):
                                mm = nc.tensor.matmul(ps[5 + dc][:, 0:256], lhsT=ckw[:, dirn, jc, dc * 128:(dc + 1) * 128], rhs=cvt[:, jc, :],
                                                      start=(jc == 0), stop=(jc == c.LC - 1))
                        pt = tr.mark("pe", mm)
                        cbase = 0 if dirn == 0 else 5
                        tr.wait("dve", pt)
                        for dc in range(2):
                            i_ = nc.vector.tensor_scalar(out=Sst[:, dirn, dc, :], in0=ps[5 + dc][:, 0:256], scalar1=coef[:, cbase + 4, h:h + 1],
                                                         scalar2=0.0, op0=ALU.mult, op1=ALU.add)
                        t_s0 = tr.mark("dve", i_)
                        t_s0 = tr.chain("dve", [
                            (lambda e, r=r: e.scalar_tensor_tensor(out=Sst[:, dirn, :, :], in0=Sg[:, r, dirn, :, :], scalar=coef[:, cbase + r, h:h + 1],
                                                                   in1=Sst[:, dirn, :, :], op0=ALU.mult, op1=ALU.add)) for r in range(4)], waits=[t_s0])
                    tr.wait("pool", t_ld_h, t_head_free, t_dq)
                    for i in range(NCH):
                        for dc in range(2):
                            nc.gpsimd.tensor_tensor(out=qf[:, dc, i * 128:(i + 1) * 128], in0=qT[:, dc, i * 128:(i + 1) * 128], in1=dqf[:], op=ALU.mult)
                            i_ = nc.gpsimd.tensor_tensor(out=qb[:, dc, i * 128:(i + 1) * 128], in0=qT[:, dc, i * 128:(i + 1) * 128], in1=dqb[:],
                                                         op=ALU.mult)
                    t_q = tr.mark("pool", i_)
                    t_sd_ = [t_s0, t_s0]
                    bf_free = [None, None]
                    acc_tok = {}
                    psb_free = None
                    psf_free = None
                    sd_free = None
                    t_ev = None
                    p2 = None
                    for s_ in range(NCH):
                        ib = NCH - 1 - s_
                        i = s_
                        t_bfb = tr.op("act", lambda e: e.activation(out=Sbf[:, 1, :, :], in_=Sst[:, 1, :, :], func=AF.Identity),
                                      waits=[t_sd_[1], bf_free[1]])
                        tr.wait("pe", t_bfb, t_q, psb_free, o_free)
                        for ec in range(2):
                            for dc in range(2):
                                mm = nc.tensor.matmul(ps[ec][:, 0:128], lhsT=Sbf[:, 1, dc, ec * 128:(ec + 1) * 128],
                                                      rhs=qb[:, dc, ib * 128:(ib + 1) * 128], start=(dc == 0), stop=(dc == 1))
                        ptb = tr.mark("pe", mm)
                        bf_free[1] = ptb
                        tr.wait("dve", ptb, acc_tok.get(ib), o_free)
                        for ec in range(2):
                            if ib in acc_tok:
                                i_ = nc.vector.tensor_tensor(out=oacc[:, ec, ib * 128:(ib + 1) * 128], in0=oacc[:, ec, ib * 128:(ib + 1) * 128],
                                                             in1=ps[ec][:, 0:128], op=ALU.add)
                            else:
                                i_ = nc.vector.tensor_copy(out=oacc[:, ec, ib * 128:(ib + 1) * 128], in_=ps[ec][:, 0:128])
                        t_ev = tr.mark("dve", i_)
                        acc_tok[ib] = t_ev
                        psb_free = t_ev
                        if ib > 0:
                            t_sd_[1] = state_update(1, ib, h, False, [tk, t_sd_[1], ptb])
                        t_bff = tr.op("act", lambda e: e.activation(out=Sbf[:, 0, :, :], in_=Sst[:, 0, :, :], func=AF.Identity),
                                      waits=[t_sd_[0], bf_free[0]])
                        tr.wait("pe", t_ld_h, sd_free)
                        for dc in range(2):
                            mm = nc.tensor.matmul(ps[2][:, 0:128], lhsT=kT[:, dc, i * 128:(i + 1) * 128], rhs=qT[:, dc, i * 128:(i + 1) * 128],
                                                  start=(dc == 0), stop=(dc == 1))
                        p1 = tr.mark("pe", mm)
                        t_sd = tr.op("dve", lambda e: e.tensor_tensor(out=sdt[:], in0=ps[2][:, 0:128], in1=dint[:], op=ALU.mult),
                                     waits=[p1, sd_free, t_di])
                        tr.wait("pe", t_sd, t_bff, t_q, psf_free)
                        for ec in range(2):
                            nc.tensor.matmul(ps[3 + ec][:, 0:128], lhsT=vt[:, i, ec * 128:(ec + 1) * 128], rhs=sdt[:], start=True, stop=False)
                            for dc in range(2):
                                mm = nc.tensor.matmul(ps[3 + ec][:, 0:128], lhsT=Sbf[:, 0, dc, ec * 128:(ec + 1) * 128],
                                                      rhs=qf[:, dc, i * 128:(i + 1) * 128], start=False, stop=(dc == 1))
                        p2 = tr.mark("pe", mm)
                        sd_free = p2
                        bf_free[0] = p2
                        tr.wait("dve", p2, acc_tok.get(i), o_free)
                        for ec in range(2):
                            if i in acc_tok:
                                i_ = nc.vector.tensor_tensor(out=oacc[:, ec, i * 128:(i + 1) * 128], in0=oacc[:, ec, i * 128:(i + 1) * 128],
                                                             in1=ps[3 + ec][:, 0:128], op=ALU.add)
                            else:
                                i_ = nc.vector.tensor_copy(out=oacc[:, ec, i * 128:(i + 1) * 128], in_=ps[3 + ec][:, 0:128])
                        t_ev = tr.mark("dve", i_)
                        acc_tok[i] = t_ev
                        psf_free = t_ev
                        if i < NCH - 1:
                            t_sd_[0] = state_update(0, i, h, False, [tk, t_sd_[0], p2])
                    t_gl = tr.dma("sp", gT[:], rgT[h * 256:(h + 1) * 256, :].rearrange("(a p) n -> p a n", p=128), gsem, waits=[p2, t_ev])
                    t_y = None
                    pt = None
                    for q0 in range(0, NT, 512):
                        Tq = min(512, NT - q0)
                        tr.wait("act", t_ev, pt)
                        nc.scalar.activation(out=osq[:, :, 0:Tq], in_=oacc[:, :, q0:q0 + Tq], func=AF.Square)
                        i_ = nc.scalar.activation(out=obf[:, :, 0:Tq], in_=oacc[:, :, q0:q0 + Tq], func=AF.Identity)
                        t_sq = tr.mark("act", i_)
                        tr.wait("pe", t_sq, t_y)
                        for ec in range(2):
                            nc.tensor.matmul(ps[0][:, 0:Tq], lhsT=ones_bf, rhs=obf[:, ec, 0:Tq], start=(ec == 0), stop=(ec == 1))
                        for ec in range(2):
                            mm = nc.tensor.matmul(ps[1][:, 0:Tq], lhsT=ones_bf, rhs=osq[:, ec, 0:Tq], start=(ec == 0), stop=(ec == 1))
                        pt = tr.mark("pe", mm)
                        t_c = tr.chain("dve", [
                            lambda e: e.tensor_scalar(out=stat[:, 0, 0:Tq], in0=ps[0][:, 0:Tq], scalar1=1.0 / 256, scalar2=0.0, op0=ALU.mult, op1=ALU.add),
                            lambda e: e.tensor_tensor(out=stat[:, 1, 0:Tq], in0=stat[:, 0, 0:Tq], in1=stat[:, 0, 0:Tq], op=ALU.mult),
                            lambda e: e.scalar_tensor_tensor(out=stat[:, 1, 0:Tq], in0=ps[1][:, 0:Tq], scalar=1.0 / 256, in1=stat[:, 1, 0:Tq],
                                                             op0=ALU.mult, op1=ALU.subtract),
                            lambda e: e.tensor_scalar(out=stat[:, 1, 0:Tq], in0=stat[:, 1, 0:Tq], scalar1=EPS, scalar2=1.0, op0=ALU.add, op1=ALU.mult),
                        ], waits=[pt, t_y, y_free, t_gl])
                        t_c = tr.op("dve", lambda e: e.reciprocal(out=stat[:, 1, 0:Tq], in_=stat[:, 1, 0:Tq]), waits=[tr.op("act", lambda e: e.activation(out=stat[:, 1, 0:Tq], in_=stat[:, 1, 0:Tq], func=AF.Sqrt), waits=[t_c])])
                        for ec in range(2):
                            t_c = tr.chain("dve", [
                                lambda e, ec=ec: e.tensor_tensor(out=stat[:, 2, 0:Tq], in0=oacc[:, ec, q0:q0 + Tq], in1=stat[:, 0, 0:Tq], op=ALU.subtract),
                                lambda e, ec=ec: e.tensor_tensor(out=stat[:, 2, 0:Tq], in0=stat[:, 2, 0:Tq], in1=stat[:, 1, 0:Tq], op=ALU.mult),
                                lambda e, ec=ec: e.scalar_tensor_tensor(out=ymix[:, ec, q0:q0 + Tq], in0=stat[:, 2, 0:Tq],
                                                                        scalar=gnw[:, 2 * h + ec:2 * h + ec + 1], in1=gT[:, ec, q0:q0 + Tq],
                                                                        op0=ALU.mult, op1=ALU.mult),
                            ], waits=[t_c])
                        t_y = t_c
                    o_free = t_y
                    y_free = tr.dma("sp", mixT[h * 256:(h + 1) * 256, :].rearrange("(a p) n -> p a n", p=128), ymix[:], ysem, waits=[t_y])
                    t_head_free = t_y
            tr.barrier(mk)

    ret_phase(1)
    _ck(4)
    DHH = c.DHH
    NKC = c.NKC
    QT = 256
    with ExitStack() as es:
        lam_t = es.enter_context(nc.sbuf_tensor("lam_t", [128, 512], F32))
        lam_p = es.enter_context(nc.sbuf_tensor("lam_p", [128, 256], F32))
        lam_s = es.enter_context(nc.sbuf_tensor("lam_s", [128, 4], F32))
        subw = es.enter_context(nc.sbuf_tensor("subw_sb", [128, 256], F32))
        NB = 1
        KTs = [es.enter_context(nc.sbuf_tensor("KTs%d" % k, [128, 2, NKC * 128], BF16)) for k in range(NB)]
        Vs = [es.enter_context(nc.sbuf_tensor("Vs%d" % k, [128, NKC, 257], BF16)) for k in range(NB)]
        Qs = [es.enter_context(nc.sbuf_tensor("Qs%d" % k, [128, 2, NT], BF16)) for k in range(NB)]
        PT = [es.enter_context(nc.sbuf_tensor("PT%d" % k, [128, 2, QT], BF16)) for k in range(3)]
        o1 = es.enter_context(nc.sbuf_tensor("o1", [128, 2, 256], F32))
        comb = es.enter_context(nc.sbuf_tensor("comb", [128, 2, 256], F32))
        junk = es.enter_context(nc.sbuf_tensor("junk", [128, 2, 256], F32))
        rr = es.enter_context(nc.sbuf_tensor("rr", [128, 2, 4], F32))
        ybf = es.enter_context(nc.sbuf_tensor("ybf", [128, 2, 256], BF16))
        ydT = es.enter_context(nc.sbuf_tensor("ydT", [128, 2, QT], BF16))
        psO = [es.enter_context(nc.psum_tensor("psO%d" % k, [128, 512], F32)) for k in range(4)]
        psS = [es.enter_context(nc.psum_tensor("psS%d" % k, [128, 512], F32)) for k in range(3)]
        pstd = es.enter_context(nc.psum_tensor("pstd", [128, 1024], BF16))

        tr.dma("sp", lam_t[:], I["lamp"].partition_broadcast(128)[:, 0, :], ld)
        tr.dma("sp", subw[:], I["subw"].partition_broadcast(128)[:, 0, :], ld)
        t_l = tr.tot(ld)
        tr.wait("dve", t_l)
        for k in range(NB):
            nc.vector.memset(Vs[k][:, :, 256:257], 1.0)
        lt4 = lam_t[:].rearrange("p (a t b) -> p a t b", a=2, t=2)
        t_a = tr.chain("dve", [
            lambda e: e.tensor_tensor(out=lam_p[:].rearrange("p (a b) -> p a b", a=2), in0=lt4[:, :, 0, :], in1=lt4[:, :, 1, :], op=ALU.mult),
            lambda e: e.tensor_reduce(out=lam_s[:, 0:2], in_=lam_p[:].rearrange("p (a b) -> p a b", a=2), axis=AX.X, op=ALU.add),
        ], waits=[t_l])
        t_b = tr.op("act", lambda e: e.activation(out=lam_s[:, 2:4], in_=lam_s[:, 0:2], func=AF.Exp), waits=[t_a])
        t_lam = tr.chain("dve", [
            lambda e: e.tensor_tensor(out=lam_s[:, 0:1], in0=lam_s[:, 3:4], in1=lam_s[:, 2:3], op=ALU.subtract),
            lambda e: e.tensor_scalar(out=lam_s[:, 0:1], in0=lam_s[:, 0:1], scalar1=-0.2, scalar2=1.0, op0=ALU.add, op1=ALU.mult),
            lambda e: e.tensor_scalar(out=subw[:], in0=subw[:], scalar1=0.8, scalar2=0.0, op0=ALU.mult, op1=ALU.add),
        ], waits=[t_b])
        nlam = lam_s[:, 0:1]

        dl = [tr.dsem("dl0"), tr.dsem("dl1")]
        ydsem = tr.dsem("ydsem")
        def load_dhead(h, b, waits):
            tr.wait("sp", t_kv[h], *waits)
            r0 = h * 256
            for m in range(2):
                tr.dma("sp", KTs[b][:, m, 0:L], cdkT[r0 + m * 128:r0 + (m + 1) * 128, :], dl[b])
                tr.dma("sp", KTs[b][:, m, L:L + 4 * NT].rearrange("p (r n) -> p r n", r=4),
                       dk_all[h].rearrange("(r f) n -> f r n", r=4)[m * 128:(m + 1) * 128, :, :], dl[b])
                tr.dma("sp", Qs[b][:, m, :], dqT[r0 + m * 128:r0 + (m + 1) * 128, :], dl[b])
            tr.dma("sp", Vs[b][:, 0:c.LC, 0:256], cdv[:, r0:r0 + 256].rearrange("(a p) n -> p a n", p=128), dl[b])
            tr.dma("sp", Vs[b][:, c.LC:NKC, 0:256], dv_all[h].rearrange("(a p) n -> p a n", p=128), dl[b])
            return (dl[b].h, dl[b].v, "d_%d" % id(dl[b]))

        head_free = [None, None]
        t_ldh = load_dhead(0, 0, [])
        pt_free = [None, None, None]
        ps_free = [None, None, None]
        st8 = {"O_free": None, "y_st": None, "yd_free": None, "p3": None, "git": 0}
        nq = QT // 128
        for h in range(DHH):
            b = 0
            if h > 0:
                t_ldh = load_dhead(h, 0, [head_free[0]])
            KT, V, Q = KTs[b], Vs[b], Qs[b]
            its = [(q0, kc) for q0 in range(0, NT, QT) for kc in range(NKC)]
            nit = len(its)
            base = st8["git"]
            st8["git"] += nit
            t_es = {}
            pending = []

            def stageA(i):
                q0, kc = its[i]
                sb = (base + i) % 3
                tr.wait("pe", t_ldh, ps_free[sb], t_lam)
                for m in range(2):
                    mm = nc.tensor.matmul(psS[sb][:, m * QT:(m + 1) * QT], lhsT=KT[:, m, kc * 128:(kc + 1) * 128], rhs=Q[:, m, q0:q0 + QT],
                                          start=True, stop=True)
                p1 = tr.mark("pe", mm)
                t_e = tr.op("act", lambda e: e.activation(out=PT[sb][:].rearrange("p a b -> p (a b)"), in_=psS[sb][:, 0:2 * QT],
                                                          func=AF.Exp), waits=[p1, pt_free[sb]])
                ps_free[sb] = t_e
                t_es[i] = t_e

            def epilogue(q0, p2):
                toks = [[p2, st8["p3"], t_lam] for _ in range(nq)]
                steps = [
                    lambda e, qs: e.reciprocal(out=rr[:, qs, 0:1], in_=psO[qs][:, 256:257]),
                    lambda e, qs: e.reciprocal(out=rr[:, qs, 1:2], in_=psO[nq + qs][:, 256:257]),
                    lambda e, qs: e.tensor_tensor(out=rr[:, qs, 1:2], in0=rr[:, qs, 1:2], in1=nlam, op=ALU.mult),
                    lambda e, qs: e.tensor_scalar(out=o1[:, qs, :], in0=psO[qs][:, 0:256], scalar1=rr[:, qs, 0:1], scalar2=0.0,
                                                  op0=ALU.mult, op1=ALU.add),
                    lambda e, qs: e.scalar_tensor_tensor(out=comb[:, qs, :], in0=psO[nq + qs][:, 0:256], scalar=rr[:, qs, 1:2], in1=o1[:, qs, :],
                                                         op0=ALU.mult, op1=ALU.add),
                    lambda e, qs: e.tensor_tensor(out=junk[:, qs, :], in0=comb[:, qs, :], in1=comb[:, qs, :], op=ALU.mult),
                    lambda e, qs: e.tensor_reduce(out=rr[:, qs, 2:3], in_=junk[:, qs, :], axis=AX.X, op=ALU.add),
                    lambda e, qs: e.tensor_scalar(out=rr[:, qs, 2:3], in0=rr[:, qs, 2:3], scalar1=1.0 / 256, scalar2=EPS, op0=ALU.mult, op1=ALU.add),
                ]
                for k, fn in enumerate(steps):
                    for qs in range(nq):
                        toks[qs] = tr.op("dve", lambda e, fn=fn, qs=qs: fn(e, qs), waits=toks[qs] if isinstance(toks[qs], list) else [toks[qs]])
                    if k == 4:
                        st8["O_free"] = toks[nq - 1]
                for qs in range(nq):
                    t_ = tr.op("act", lambda e, qs=qs: e.activation(out=rr[:, qs, 3:4], in_=rr[:, qs, 2:3], func=AF.Ln), waits=[toks[qs]])
                    toks[qs] = tr.op("act", lambda e, qs=qs: e.activation(out=rr[:, qs, 2:3], in_=rr[:, qs, 3:4], func=AF.Exp, scale=-0.5), waits=[t_])
                for qs in range(nq):
                    toks[qs] = tr.op("dve", lambda e, qs=qs: e.scalar_tensor_tensor(out=ybf[:, qs, :], in0=comb[:, qs, :], scalar=rr[:, qs, 2:3],
                                                                                   in1=subw[:], op0=ALU.mult, op1=ALU.mult), waits=[toks[qs]])
                t_yb = list(toks)

                def transposes():
                    for qs in range(nq):
                        tr.wait("pe", t_yb[qs], st8["yd_free"])
                        for ec in range(2):
                            mm = nc.tensor.transpose(pstd[:, ec * 128:(ec + 1) * 128], ybf[:, qs, ec * 128:(ec + 1) * 128], ident_bf)
                        p3 = tr.mark("pe", mm)
                        st8["p3"] = p3
                        tr.wait("act", p3, st8["y_st"])
                        for ec in range(2):
                            i_ = nc.scalar.activation(out=ydT[:, ec, qs * 128:(qs + 1) * 128], in_=pstd[:, ec * 128:(ec + 1) * 128],
                                                      func=AF.Identity)
                        st8["yd_free"] = tr.mark("act", i_)
                    st8["y_st"] = tr.dma("sp", mixT[c.RW + h * 256:c.RW + (h + 1) * 256, q0:q0 + QT].rearrange("(a p) n -> p a n", p=128),
                                         ydT[:], ydsem, waits=[st8["yd_free"]])
                return transposes

            stageA(0)
            if nit > 1:
                stageA(1)
            t_last_pv = None
            for i in range(nit):
                if i + 2 < nit:
                    stageA(i + 2)
                q0, kc = its[i]
                sb = (base + i) % 3
                tr.wait("pe", t_es[i])
                if kc == 0:
                    tr.wait("pe", st8["O_free"])
                for m in range(2):
                    for qs in range(nq):
                        mm = nc.tensor.matmul(psO[m * nq + qs][:, 0:257], lhsT=PT[sb][:, m, qs * 128:(qs + 1) * 128], rhs=V[:, kc, :],
                                              start=(kc == 0), stop=(kc == NKC - 1))
                p2 = tr.mark("pe", mm)
                pt_free[sb] = p2
                t_last_pv = p2
                if kc == NKC - 1:
                    pending.append((i + 3, epilogue(q0, p2)))
                while pending and (pending[0][0] <= i or i == nit - 1):
                    pending.pop(0)[1]()
            head_free[b] = t_last_pv
        tr.barrier(mk)

    ret_phase(2)
    _ck(5)
    with ExitStack() as es:
        mT = es.enter_context(nc.sbuf_tensor("mTo", [128, c.MC, TT], BF16))
        rbufs = [es.enter_context(nc.sbuf_tensor("rbo%d" % k, [128, NDJ, TT], F32)) for k in range(2)]
        psd = [es.enter_context(nc.psum_tensor("pso%d" % k, [128, 512], F32)) for k in range(NDJ)]
        rb_state = make_rb_state(rbufs, "rbo")
        mfree = None
        msem = tr.dsem("msem")
        for (t0, T, stream) in tiles_lat:
            t_m = tr.dma("sp", mT[:, :, 0:T], mixT.rearrange("(a p) n -> p a n", p=128)[:, :, t0:t0 + T], msem, waits=[mfree])
            mfree = down_proj(mT, c.MC, T, x1T, x2T, MV(1, 2, 0), t0, psd, rb_state, [t_m])
        tr.barrier(mk)

    _ck(6)
    ffn_phase(1, x2T, x3T, tiles_lat, 2)
    _ck(7)

    with ExitStack() as es:
        TF = min(TT, 256)
        tiles_fin = [(t0_, min(TF, NT - t0_), 0) for t0_ in range(0, NT, TF)]
        xbufs = [es.enter_context(nc.sbuf_tensor("xbf%d" % k, [128, KC, TF], F32)) for k in range(2)]
        sqbs = [es.enter_context(nc.sbuf_tensor("sqf%d" % k, [128, KC, TF], BF16)) for k in range(2)]
        rstds = [es.enter_context(nc.sbuf_tensor("rstdf%d" % k, [128, TF], F32)) for k in range(2)]
        finw = es.enter_context(nc.sbuf_tensor("finw", [128, KC], F32))
        psss = [es.enter_context(nc.psum_tensor("psfin%d" % k, [128, 512], F32)) for k in range(2)]
        t_fw = tr.dma("sp", finw[:], I["finT"], ld)
        fsems = [tr.dsem("fsem%d" % k) for k in range(2)]
        xfree = [None, None]
        loads = {}
        def fin_load(n_):
            t0_, T_, _ = tiles_fin[n_]
            k_ = n_ % 2
            loads[n_] = tr.dma("sp", xbufs[k_][:, :, 0:T_], x3T.rearrange("(kc p) n -> p kc n", p=128)[:, :, t0_:t0_ + T_], xsems[k_],
                               waits=[xfree[k_]])

        fin_load(0)
        for n_, (t0, T, stream) in enumerate(tiles_fin):
            k = n_ % 2
            xv = xbufs[k][:, :, 0:T]
            sqb, rstd, pss = sqbs[k], rstds[k], psss[k]
            t_x = loads[n_]
            if n_ + 1 < len(tiles_fin):
                fin_load(n_ + 1)
            t_sq = tr.op("act", lambda e: e.activation(out=sqb[:, :, 0:T], in_=xv, func=AF.Square), waits=[t_x, xfree[k]])
            tr.wait("pe", t_sq)
            for kc in range(KC):
                mm = nc.tensor.matmul(pss[:, 0:T], lhsT=ones_bf, rhs=sqb[:, kc, 0:T], start=(kc == 0), stop=(kc == KC - 1))
            t_ss = tr.mark("pe", mm)
            t_r = tr.op("dve", lambda e: e.tensor_scalar(out=rstd[:, 0:T], in0=pss[:, 0:T], scalar1=1.0 / D, scalar2=EPS,
                                                         op0=ALU.mult, op1=ALU.add), waits=[t_ss, t_x, t_fw])
            t_r = tr.op("dve", lambda e: e.reciprocal(out=rstd[:, 0:T], in_=rstd[:, 0:T]),
                        waits=[tr.op("act", lambda e: e.activation(out=rstd[:, 0:T], in_=rstd[:, 0:T], func=AF.Sqrt), waits=[t_r])])
            tr.wait("dve", t_r)
            for kc in range(KC):
                i_ = nc.vector.scalar_tensor_tensor(out=xv[:, kc, :], in0=xv[:, kc, :], scalar=finw[:, kc:kc + 1], in1=rstd[:, 0:T],
                                                    op0=ALU.mult, op1=ALU.mult)
            t_y = tr.mark("dve", i_)
            xfree[k] = tr.dma("sp", yT.rearrange("(kc p) n -> p kc n", p=128)[:, :, t0:t0 + T], xv, fsems[k], waits=[t_y])
        tr.barrier(mk)
    assert ws.cur == len(ws.tiles), (ws.cur, len(ws.tiles))


def _rope_tables(c: Cfg, seg, head_dim):
    t = np.arange(seg * c.NT, (seg + 1) * c.NT)
    rows = (t // c.GW).astype(np.float32)
    cols = (t % c.GW).astype(np.float32)
    n_freq = head_dim // 4
    inv_freq = (np.float32(10000.0) ** (-np.arange(n_freq, dtype=np.float32) / np.float32(n_freq))).astype(np.float32)
    ang_r = rows[:, None] * inv_freq
    ang_c = cols[:, None] * inv_freq
    ang = np.concatenate([ang_r, ang_r, ang_c, ang_c], axis=-1).astype(np.float32)
    return np.cos(ang).astype(np.float32), np.sin(ang).astype(np.float32)


def _consts(c: Cfg):
    m = np.arange(128)[:, None]
    n = np.arange(128)[None, :]
    ident = np.eye(128, dtype=np.float32)
    P = np.zeros((128, 128), np.float32)
    for i in range(64):
        P[i, i + 64] = -1.0
        P[i + 64, i] = 1.0
    Pd = np.zeros((128, 128), np.float32)
    for base in (0, 64):
        for i in range(32):
            Pd[base + i, base + i + 32] = -1.0
            Pd[base + i + 32, base + i] = 1.0
    relf = np.where(m <= n, n - m, 0).astype(np.float32)
    maskf = (m <= n).astype(np.float32)
    relb = np.where(m > n, m - n, 0).astype(np.float32)
    maskb = (m > n).astype(np.float32)
    posqf = np.broadcast_to(n + 1, (128, 128)).astype(np.float32)
    posqb = np.broadcast_to(128 - n, (128, 128)).astype(np.float32)
    pkf = (127 - m).astype(np.float32)
    pkb = m.astype(np.float32)
    j = np.arange(c.LC)[None, :] * 128 + m
    ctxf = (c.L - 1 - j).astype(np.float32)
    ctxb = j.astype(np.float32)
    ii = np.arange(c.NCH)[None, :] * 128 + m
    posff = (c.NT - 1 - ii).astype(np.float32)
    posfb = ii.astype(np.float32)
    return np.ascontiguousarray(np.concatenate([ident, P.T, Pd.T, relf, maskf, relb, maskb, posqf, posqb, pkf, pkb, ctxf, ctxb, posff, posfb], axis=1))


def fm(v, nchunks):
    return np.ascontiguousarray(np.asarray(v, np.float32).reshape(nchunks, 128).T)


def make_in_maps(c: Cfg, inp):
    KC = c.KC
    f = lambda a: np.ascontiguousarray(np.asarray(a, dtype=np.float32))
    x, cc, ctx, c_ctx = f(inp["x"]), f(inp["c"]), f(inp["ctx"]), f(inp["c_ctx"])
    shared = {
        "nwT": fm(np.asarray(inp["norm_w"][0]).reshape(-1), 3 * KC),
        "finT": fm(inp["final_norm_w"], KC),
        "w_in": f(inp["w_in"][0]), "w_out": f(inp["w_out"][0]),
        "rdec": f(np.asarray(inp["ret_decay"][0]).reshape(1, -1)),
        "lamp": f(np.asarray(inp["diff_lambda"][0]).reshape(1, -1)),
        "subw": f(np.asarray(inp["diff_subln_w"][0]).reshape(1, -1)),
        "gnwT": fm(inp["ret_gn_w"][0], c.RW // 128),
        "consts": _consts(c),
    }
    for i in range(2):
        shared["wg%d" % i] = f(inp["ffn_gate"][0, i])
        shared["wu%d" % i] = f(inp["ffn_up"][0, i])
        shared["wd%d" % i] = f(inp["ffn_down"][0, i])
    maps = []
    NT = c.NT
    QC = 9 * KC // 4
    Q = QC * 128
    ada_w = np.asarray(inp["ada_w"][0], np.float32)
    ada_b = np.asarray(inp["ada_b"][0], np.float32)
    ada_q = [np.ascontiguousarray(ada_w[:, r * Q:(r + 1) * Q]) for r in range(4)]
    ada_bq = [fm(ada_b[r * Q:(r + 1) * Q], QC) for r in range(4)]
    for j in range(8):
        b, seg = j // 4, j % 4
        m = dict(shared)
        m["ada_wq"] = ada_q[seg]
        m["ada_bqT"] = ada_bq[seg]
        xs = x[b, seg * NT:(seg + 1) * NT, :]
        m["xin"] = np.ascontiguousarray(np.concatenate([xs, ctx[b]], axis=0).T)
        m["cT"] = np.ascontiguousarray(np.stack([fm(cc[b], KC), fm(c_ctx, KC)], axis=-1).reshape(128, KC * 2))
        cr, sr = _rope_tables(c, seg, 256)
        cd, sd = _rope_tables(c, seg, 128)
        m["cosr"] = np.ascontiguousarray(cr.T.reshape(2, 128, NT).transpose(1, 0, 2).reshape(128, 2 * NT))
        m["sinr"] = np.ascontiguousarray(sr.T.reshape(2, 128, NT).transpose(1, 0, 2).reshape(128, 2 * NT))
        m["cosd"] = np.ascontiguousarray(cd.T)
        m["sind"] = np.ascontiguousarray(sd.T)
        meta = np.zeros((1, 20), np.float32)
        for r in range(4):
            if r < seg:
                meta[0, r] = (seg - 1 - r) * NT
                meta[0, 4 + r] = 1.0
            if r > seg:
                meta[0, 9 + r] = (r - seg - 1) * NT
                meta[0, 13 + r] = 1.0
        meta[0, 8] = seg * NT
        meta[0, 17] = (3 - seg) * NT
        m["segmeta"] = meta
        maps.append(m)
    return maps


_NC_CACHE = {}


def run_cfg(c: Cfg, inp, stop=None, dbg=False, raw=False):
    key = (c.D, c.S, c.L, c.DFF, c.RH, c.DHH, c.TT, stop, dbg)
    if key not in _NC_CACHE:
        _NC_CACHE[key] = build(c, stop, dbg)
    nc = _NC_CACHE[key]
    maps = make_in_maps(c, inp)
    res = run_bass_kernel_spmd(nc, maps, core_ids=list(range(8)))
    if raw:
        return res.results
    B = 2
    out = np.empty((B, c.S, c.D), np.float32)
    for j in range(8):
        b, seg = j // 4, j % 4
        out[b, seg * c.NT:(seg + 1) * c.NT, :] = res.results[j]["yT"].T
    return out


def kernel(**inputs):
    return run_cfg(FULL, inputs)
```
